# Optimizing a Trainium2 kernel written in Bass

```python
import math
import jax, jax.numpy as jnp
from jax import lax
import numpy as np

D_MODEL = 1024
BATCH = 4
SEQ = 8192
DEPTH = 1

MIX_WIDTH = D_MODEL
ATTN_WIDTH = MIX_WIDTH // 2
SSM_WIDTH = MIX_WIDTH - ATTN_WIDTH
HEAD_DIM = 64
N_HEADS = ATTN_WIDTH // HEAD_DIM
DILATED_BRANCHES = ((128, 1), (512, 4), (2048, 16))
BLOCK = 128
SSM_GROUP = 16
N_SSM_GROUPS = SSM_WIDTH // SSM_GROUP
STATE_DIM = 64
D_FF = 2816
IN_WIDTH = 3 * ATTN_WIDTH + SSM_WIDTH
NORM_EPS = 1e-6
DT_MIN = 1e-3
DT_MAX = 1e-1

kernel_name = "hybrid_dilated_alibi_attn_s5_macaron_layer"


def rms_norm(x, g):
    xf = x.astype(jnp.float32)
    y = xf * lax.rsqrt(jnp.mean(xf * xf, axis=-1, keepdims=True) + NORM_EPS)
    return (y * g.astype(jnp.float32)).astype(x.dtype)


def swiglu(x, w_in, w_out):
    gate, up = jnp.split(x @ w_in, 2, axis=-1)
    return (jax.nn.silu(gate) * up) @ w_out


def alibi_slopes(n_heads):
    return 2.0 ** (-8.0 * jnp.arange(1, n_heads + 1, dtype=jnp.float32) / n_heads)


def dilated_window_branch(q, k, v, slopes, window, dilation):
    B, S, H, E = q.shape
    n_back = window // dilation
    L = -(-S // dilation)
    nb = -(-L // BLOCK)
    Lp = nb * BLOCK

    def to_blocks(t):
        t = jnp.pad(t, ((0, 0), (0, L * dilation - S), (0, 0), (0, 0)))
        t = t.reshape(B, L, dilation, H, E).transpose(0, 2, 1, 3, 4)
        t = jnp.pad(t, ((0, 0), (0, 0), (0, Lp - L), (0, 0), (0, 0)))
        return t.reshape(B, dilation, nb, BLOCK, H, E)

    def with_prev(t):
        prev = jnp.pad(t[:, :, :-1], ((0, 0), (0, 0), (1, 0), (0, 0), (0, 0), (0, 0)))
        return jnp.concatenate([prev, t], axis=3)

    def from_blocks(t):
        tail = t.shape[4:]
        t = t.reshape((B, dilation, Lp) + tail)[:, :, :L]
        t = jnp.moveaxis(t, 1, 2).reshape((B, L * dilation) + tail)
        return t[:, :S]

    qb, kb, vb = to_blocks(q), to_blocks(k), to_blocks(v)
    kk, vv = with_prev(kb), with_prev(vb)
    s = jnp.einsum('brnqhe,brnkhe->brnhqk', qb, kk) * (HEAD_DIM ** -0.5)

    qi = jnp.arange(BLOCK)[:, None]
    ci = jnp.arange(2 * BLOCK)[None, :]
    steps = BLOCK + qi - ci
    key_pos = (jnp.arange(nb)[:, None, None] - 1) * BLOCK + ci[None]
    valid = ((steps >= 0) & (steps <= n_back))[None] & (key_pos >= 0)
    dist = (steps * dilation).astype(jnp.float32)
    bias = -slopes[:, None, None] * dist[None]
    s = jnp.where(valid[None, None, :, None], s + bias, -jnp.inf)

    m = jnp.max(s, axis=-1, keepdims=True)
    p = jnp.exp(s - m)
    denom = jnp.sum(p, axis=-1, keepdims=True)
    o = jnp.einsum('brnhqk,brnkhe->brnqhe', p, vv)
    o = o * jnp.swapaxes(1.0 / denom[..., 0], -1, -2)[..., None]
    lse = jnp.swapaxes((m + jnp.log(denom))[..., 0], -1, -2)
    return from_blocks(o), from_blocks(lse)


def dilated_attention(q, k, v):
    B, S, _ = q.shape
    q, k, v = (t.astype(jnp.float32).reshape(B, S, N_HEADS, HEAD_DIM) for t in (q, k, v))
    slopes = alibi_slopes(N_HEADS)
    outs, lses = [], []
    for window, dilation in DILATED_BRANCHES:
        o, l = dilated_window_branch(q, k, v, slopes, window, dilation)
        outs.append(o)
        lses.append(l)
    w = jax.nn.softmax(jnp.stack(lses, axis=-1), axis=-1)
    o = jnp.einsum('bshn,nbshe->bshe', w, jnp.stack(outs, axis=0))
    return o.reshape(B, S, ATTN_WIDTH)


def s5_mixer(u, a_re, a_im, log_dt, b_re, b_im, c_re, c_im, d_skip, w_glu, b_glu):
    B, S, _ = u.shape
    f32 = jnp.float32
    uf = u.astype(f32).reshape(B, S, N_SSM_GROUPS, SSM_GROUP)
    dt = jnp.exp(log_dt.astype(f32))[:, None]
    a = lax.complex(a_re.astype(f32), a_im.astype(f32))
    a_bar = jnp.exp(dt * a)
    b = lax.complex(b_re.astype(f32), b_im.astype(f32))
    b_bar = ((a_bar - 1.0) / a)[..., None] * b
    bu = jnp.einsum('bsgc,gpc->bsgp', uf.astype(jnp.complex64), b_bar)
    a_seq = jnp.broadcast_to(a_bar, bu.shape)

    def combine(left, right):
        a_l, x_l = left
        a_r, x_r = right
        return a_r * a_l, a_r * x_l + x_r

    _, states = lax.associative_scan(combine, (a_seq, bu), axis=1)
    c = lax.complex(c_re.astype(f32), c_im.astype(f32))
    y = jnp.real(jnp.einsum('bsgp,gcp->bsgc', states, c))
    y = y + d_skip.astype(f32).reshape(N_SSM_GROUPS, SSM_GROUP) * uf
    y = jax.nn.gelu(y.reshape(B, S, SSM_WIDTH))
    return y * jax.nn.sigmoid(y @ w_glu.astype(f32) + b_glu.astype(f32))


def setup_inputs(seed: int = 0) -> dict:
    key = jax.random.key(seed)
    ks = jax.random.split(key, 24)
    f32 = jnp.float32
    L = DEPTH

    def nrm(k, shape, scale):
        return jax.random.normal(k, shape, f32) * scale

    def gain(k):
        return 1.0 + 0.05 * jax.random.normal(k, (L, D_MODEL), f32)

    n_idx = jnp.arange(STATE_DIM, dtype=f32)
    a_re = -0.5 + 0.01 * jax.random.normal(ks[9], (L, N_SSM_GROUPS, STATE_DIM), f32)
    a_im = math.pi * n_idx + 0.01 * jax.random.normal(ks[10], (L, N_SSM_GROUPS, STATE_DIM), f32)
    log_dt = jax.random.uniform(ks[11], (L, N_SSM_GROUPS), f32,
                                math.log(DT_MIN), math.log(DT_MAX))
    return {
        "x": jax.random.normal(ks[0], (BATCH, SEQ, D_MODEL), f32),
        "ffn1_pre_g": gain(ks[1]),
        "ffn1_w_in": nrm(ks[2], (L, D_MODEL, 2 * D_FF), D_MODEL ** -0.5),
        "ffn1_w_out": nrm(ks[3], (L, D_FF, D_MODEL), D_FF ** -0.5),
        "ffn1_post_g": gain(ks[4]),
        "mix_pre_g": gain(ks[5]),
        "w_mix_in": nrm(ks[6], (L, D_MODEL, IN_WIDTH), D_MODEL ** -0.5),
        "a_re": a_re,
        "a_im": a_im,
        "log_dt": log_dt,
        "b_re": nrm(ks[12], (L, N_SSM_GROUPS, STATE_DIM, SSM_GROUP), (2 * SSM_GROUP) ** -0.5),
        "b_im": nrm(ks[13], (L, N_SSM_GROUPS, STATE_DIM, SSM_GROUP), (2 * SSM_GROUP) ** -0.5),
        "c_re": nrm(ks[14], (L, N_SSM_GROUPS, SSM_GROUP, STATE_DIM), (2 * STATE_DIM) ** -0.5),
        "c_im": nrm(ks[15], (L, N_SSM_GROUPS, SSM_GROUP, STATE_DIM), (2 * STATE_DIM) ** -0.5),
        "d_skip": nrm(ks[16], (L, SSM_WIDTH), 1.0),
        "w_glu": nrm(ks[17], (L, SSM_WIDTH, SSM_WIDTH), SSM_WIDTH ** -0.5),
        "b_glu": nrm(ks[18], (L, SSM_WIDTH), 0.01),
        "w_mix_out": nrm(ks[19], (L, MIX_WIDTH, D_MODEL), MIX_WIDTH ** -0.5),
        "mix_post_g": gain(ks[20]),
        "ffn2_pre_g": gain(ks[21]),
        "ffn2_w_in": nrm(ks[22], (L, D_MODEL, 2 * D_FF), D_MODEL ** -0.5),
        "ffn2_w_out": nrm(ks[23], (L, D_FF, D_MODEL), D_FF ** -0.5),
        "ffn2_post_g": gain(ks[7]),
    }


def reference(x, ffn1_pre_g, ffn1_w_in, ffn1_w_out, ffn1_post_g, mix_pre_g, w_mix_in,
              a_re, a_im, log_dt, b_re, b_im, c_re, c_im, d_skip, w_glu, b_glu,
              w_mix_out, mix_post_g, ffn2_pre_g, ffn2_w_in, ffn2_w_out, ffn2_post_g):
    for l in range(DEPTH):
        h = rms_norm(x, ffn1_pre_g[l])
        x = x + 0.5 * rms_norm(swiglu(h, ffn1_w_in[l], ffn1_w_out[l]), ffn1_post_g[l])
        h = rms_norm(x, mix_pre_g[l])
        proj = h @ w_mix_in[l]
        q, k, v, u = jnp.split(proj, [ATTN_WIDTH, 2 * ATTN_WIDTH, 3 * ATTN_WIDTH], axis=-1)
        attn = dilated_attention(q, k, v).astype(x.dtype)
        ssm = s5_mixer(u, a_re[l], a_im[l], log_dt[l], b_re[l], b_im[l], c_re[l], c_im[l],
                       d_skip[l], w_glu[l], b_glu[l]).astype(x.dtype)
        mixed = jnp.concatenate([attn, ssm], axis=-1) @ w_mix_out[l]
        x = x + rms_norm(mixed, mix_post_g[l])
        h = rms_norm(x, ffn2_pre_g[l])
        x = x + 0.5 * rms_norm(swiglu(h, ffn2_w_in[l], ffn2_w_out[l]), ffn2_post_g[l])
    return x
```

```python
import numpy as np
import ml_dtypes
from contextlib import ExitStack
import concourse.bass as bass
import concourse.mybir as mybir
from concourse.bass_utils import run_bass_kernel_spmd

F32 = mybir.dt.float32
BF16 = mybir.dt.bfloat16
AF = mybir.ActivationFunctionType
ALU = mybir.AluOpType

D = 1024
DFF = 2816
NF = DFF // 128
SEQ = 8192
HALF = 4096
T = 512
NT_ALL = SEQ // T
NT_OWN = HALF // T
EPS = 1e-6
NCORES = 8


class Op:
    __slots__ = ("eng", "fn", "is_dma", "waits", "signaled", "idx", "count", "sem", "val", "is_nop")


class Prog:
    ENGS = ("pe", "act", "dve", "pool", "sp")

    def __init__(self, nc, stack):
        self.nc = nc
        self.stack = stack
        self.streams = {e: [] for e in self.ENGS}
        self.nops = {e: 0 for e in self.ENGS}
        self.nsig = {e: 0 for e in self.ENGS}
        self.esem = {e: stack.enter_context(nc.semaphore("sem_" + e)) for e in ("pe", "act", "dve", "pool")}
        self.dsem = {}
        self.dval = {}
        self.writers = {}
        self.readers = {}
        self.waited = {e: {} for e in self.ENGS}
        self.last = {}
        self.last_dma = {}
        self.defer = None
        self.deferred = []

    def _dma_sem(self, key):
        if key not in self.dsem:
            self.dsem[key] = self.stack.enter_context(self.nc.semaphore("dsem_%d" % len(self.dsem)))
            self.dval[key] = 0
        return self.dsem[key]

    def _dep(self, op, dep):
        if dep is op:
            return
        if dep.is_dma:
            tk = ("d", id(dep.sem))
            if self.waited[op.eng].get(tk, 0) >= dep.val:
                return
            self.waited[op.eng][tk] = dep.val
            op.waits.append(dep)
        else:
            if dep.eng == "pe" and op.eng == "pe" and not op.is_dma:
                return
            tk = ("e", dep.eng)
            if self.waited[op.eng].get(tk, -1) >= dep.idx:
                return
            self.waited[op.eng][tk] = dep.idx
            if dep.count is None:
                dep.signaled = True
            op.waits.append(dep)

    def op(self, eng, fn, r=(), w=(), dma=None):
        if self.defer is not None:
            self.defer.append((eng, fn, list(r), list(w), dma))
            return None
        o = Op()
        o.eng = eng
        o.fn = fn
        o.is_dma = dma is not None
        o.waits = []
        o.signaled = False
        o.idx = self.nops[eng]
        self.nops[eng] += 1
        o.count = None
        o.is_nop = False
        if o.is_dma:
            o.sem = self._dma_sem(dma)
            self.dval[dma] += 16
            o.val = self.dval[dma]
        for k in r:
            for d in self.writers.get(k, {}).values():
                self._dep(o, d)
        for k in w:
            for d in self.writers.get(k, {}).values():
                self._dep(o, d)
            for d in self.readers.get(k, {}).values():
                self._dep(o, d)
        tag = ("d", id(o.sem)) if o.is_dma else eng
        for k in r:
            self.readers.setdefault(k, {})[tag] = o
        for k in w:
            self.writers[k] = {tag: o}
            self.readers[k] = {}
        self.streams[eng].append(o)
        if o.is_dma:
            self.last_dma[id(o.sem)] = o
        else:
            self.last[eng] = o
        return o

    def replay(self, k):
        for _ in range(k):
            if not self.deferred:
                return
            eng, fn, r, w, dma = self.deferred.pop(0)
            self.op(eng, fn, r=r, w=w, dma=dma)

    def barrier(self):
        deps = [d for d in self.last.values() if not d.is_nop and d.eng != "sp"] + list(self.last_dma.values())
        for x in self.ENGS:
            o = Op()
            o.eng = x
            o.fn = lambda e: e.nop()
            o.is_dma = False
            o.is_nop = True
            o.waits = []
            o.signaled = False
            o.idx = self.nops[x]
            self.nops[x] += 1
            o.count = None
            for d in deps:
                self._dep(o, d)
            self.streams[x].append(o)

    def simulate(self):
        pos = {e: 0 for e in self.ENGS}
        done = set()
        progress = True
        while progress:
            progress = False
            for e in self.ENGS:
                st = self.streams[e]
                while pos[e] < len(st):
                    o = st[pos[e]]
                    if all((id(d) in done) or (d.count is not None and not d.is_dma and d not in self._cur) or
                           (d.is_dma and d not in self._cur) for d in o.waits):
                        done.add(id(o))
                        pos[e] += 1
                        progress = True
                    else:
                        break
        for e in self.ENGS:
            if pos[e] < len(self.streams[e]):
                o = self.streams[e][pos[e]]
                raise RuntimeError("deadlock: engine %s stuck at op %d/%d waiting on %s" % (
                    e, pos[e], len(self.streams[e]), [(d.eng, d.idx, d.is_dma) for d in o.waits if id(d) not in done]))

    def flush(self):
        nc = self.nc
        self._cur = set()
        for e in self.ENGS:
            self._cur.update(self.streams[e])
        self.simulate()
        for e in ("pe", "act", "dve", "pool"):
            c = self.nsig[e]
            pend = []
            comp = [o for o in self.streams[e] if not o.is_dma and not o.is_nop]
            if comp:
                comp[-1].signaled = True
            for o in self.streams[e]:
                if o.is_dma:
                    continue
                pend.append(o)
                if o.signaled:
                    c += 1
                    for p in pend:
                        p.count = c
                    pend = []
            self.nsig[e] = c
        streams = self.streams
        esem = self.esem

        def emit(eng_name, e):
            for o in streams[eng_name]:
                for d in o.waits:
                    if d.is_dma:
                        e.wait_ge(d.sem, d.val)
                    else:
                        assert d.count is not None
                        e.wait_ge(esem[d.eng], d.count)
                ins = o.fn(e)
                if o.is_nop:
                    continue
                if o.is_dma:
                    ins.then_inc(o.sem, 16)
                elif o.signaled:
                    ins.then_inc(esem[eng_name], 1)

        with nc.Block() as block:
            @block.tensor
            def _(e):
                emit("pe", e)

            @block.scalar
            def _(e):
                emit("act", e)

            @block.vector
            def _(e):
                emit("dve", e)

            @block.gpsimd
            def _(e):
                emit("pool", e)

            @block.sync
            def _(e):
                emit("sp", e)
        self.streams = {e: [] for e in self.ENGS}

    def final_wait(self, eng, ops):
        o = self.op(eng, lambda e: e.nop(), r=(), w=())
        o.is_nop = True
        for d in ops:
            self._dep(o, d)
        return o


def dview(t, c0, nchunks, t0, ntok):
    return t[c0 * 128:(c0 + nchunks) * 128, t0:t0 + ntok].rearrange("(c p) t -> p c t", p=128)


def build(debug=None):
    nc = bass.Bass("TRN2", target_bir_lowering=False)
    dt_ = nc.dram_tensor
    xT = dt_("xT", [D, SEQ], F32, kind="ExternalInput").ap()
    gains = dt_("gains", [128, 48], F32, kind="ExternalInput").ap()
    w1_in = dt_("ffn1_w_in", [D, 2 * DFF], F32, kind="ExternalInput").ap()
    w1_out = dt_("ffn1_w_out", [DFF, D], F32, kind="ExternalInput").ap()
    w2_in = dt_("ffn2_w_in", [D, 2 * DFF], F32, kind="ExternalInput").ap()
    w2_out = dt_("ffn2_w_out", [DFF, D], F32, kind="ExternalInput").ap()
    outT = dt_("outT", [D, HALF], F32, kind="ExternalOutput").ap()
    dbg_kind = "ExternalOutput" if debug else "Internal"
    w_mix_in = dt_("w_mix_in", [D, 2048], F32, kind="ExternalInput").ap()
    ident_d = dt_("ident", [128, 128], F32, kind="ExternalInput").ap()
    atab_d = dt_("atab", [128, 24 * 256], F32, kind="ExternalInput").ap()
    btab_d = dt_("btab", [128, 24 * 128], F32, kind="ExternalInput").ap()
    x1T = dt_("x1T", [D, HALF], F32, kind=dbg_kind).ap()
    qkvuT = dt_("qkvuT", [2048, SEQ], BF16, kind=dbg_kind).ap()
    catT = dt_("catT", [D, HALF], BF16, kind=dbg_kind).ap()
    x2T = dt_("x2T", [D, HALF], F32, kind=dbg_kind).ap()
    ssm_a_d = dt_("ssm_a", [128, 96], F32, kind="ExternalInput").ap()
    ssm_b_d = dt_("ssm_b", [128, 2 * 32 * 16], F32, kind="ExternalInput").ap()
    ssm_c_d = dt_("ssm_c", [128, 2 * 32 * 16], F32, kind="ExternalInput").ap()
    ssm_v_d = dt_("ssm_v", [128, 16], F32, kind="ExternalInput").ap()
    rmask_d = dt_("rmask", [128, 8], F32, kind="ExternalInput").ap()
    swap_d = dt_("swapm", [128, 128], F32, kind="ExternalInput").ap()
    w_glu = dt_("w_glu", [512, 512], F32, kind="ExternalInput").ap()
    sel_d = dt_("sel", [128, 64 * 128], F32, kind="ExternalInput").ap()
    selT_d = dt_("selT", [128, 64 * 128], F32, kind="ExternalInput").ap()
    cmask_d = dt_("cmask", [128, 128], F32, kind="ExternalInput").ap()
    dstk_d = dt_("dstk", [128, 32], F32, kind="ExternalInput").ap()
    w_mix_out = dt_("w_mix_out", [D, D], F32, kind="ExternalInput").ap()
    h2T = dt_("h2T", [D, SEQ], BF16, kind=("ExternalOutput" if debug else "Internal")).ap()

    with ExitStack() as gstack:
        P = Prog(nc, gstack)
        A = nc.alloc_sbuf_tensor
        ones = A("ones", [128, 128], BF16)
        gsb = A("gsb", [128, 48], F32)
        ghalf = A("ghalf", [128, 48], F32)
        P.op("pool", lambda e: e.memset(ones[:], 1.0), w=["ones"])
        P.op("sp", lambda e: e.dma_start(out=gsb[:], in_=gains), w=["gsb"], dma="c0")
        P.op("dve", lambda e: e.tensor_scalar(out=ghalf[:], in0=gsb[:], scalar1=0.5, scalar2=None, op0=ALU.mult),
             r=["gsb"], w=["ghalf"])
        G_F1PRE, G_F1POST, G_MIXPRE, G_MIXPOST, G_F2PRE, G_F2POST = range(6)

        def gcol(tile_, gi, c):
            return tile_[:, gi * 8 + c:gi * 8 + c + 1]

        def ffn_phase(name, src, tiles, w_in, w_out, g_pre, g_post, store_x, next_g, store_h):
            with ExitStack() as st:
                def S(nm, shape, dt):
                    return st.enter_context(nc.sbuf_tensor(name + nm, shape, dt))

                def PS(nm, shape):
                    return st.enter_context(nc.psum_tensor(name + nm, shape, F32))
                win = S("win", [128, 8, 2 * DFF], BF16)
                wout = S("wout", [128, NF, D], BF16)
                XA = S("xa", [128, 8, T], F32)
                hT = S("hT", [128, 8, T], BF16)
                act = S("act", [128, NF, T], BF16)
                sg = [S("sg%d" % i, [128, T], BF16) for i in range(2)]
                ysb = S("ysb", [128, 8, T], F32)
                sqj = [S("sq%d" % i, [128, T], BF16) for i in range(5)]
                rsA = S("rsA", [128, T], F32)
                rsB = S("rsB", [128, T], F32)
                h2c = [S("h2c%d" % i, [128, 1, T], BF16) for i in range(2)]
                mhalf = S("mhalf", [128, 1], F32)
                P.op("pool", lambda e: e.memset(mhalf[:], -0.5), w=["mhalf"])
                pG = [PS("pG%d" % i, [128, T]) for i in range(2)]
                pU = [PS("pU%d" % i, [128, T]) for i in range(2)]
                pY = [PS("pY%d" % i, [128, T]) for i in range(2)]
                pS0 = PS("pS0", [128, T])
                pS1 = PS("pS1", [128, T])

                fblocks = (2, 6, 7, 7)
                fstart = [sum(fblocks[:b]) for b in range(len(fblocks))]
                blk_of = [b for b, nb_ in enumerate(fblocks) for _ in range(nb_)]
                win_v = w_in.rearrange("(k p) f -> p k f", p=128)
                for b in range(len(fblocks)):
                    for half in range(2):
                        c0 = half * DFF + fstart[b] * 128
                        cw = fblocks[b] * 128
                        P.op("pool", (lambda e, c0=c0, cw=cw: e.dma_start(out=win[:, :, c0:c0 + cw], in_=win_v[:, :, c0:c0 + cw])),
                             w=[(name, "win", half, b)], dma=(name, "win", half, b))
                wout_v = w_out.rearrange("(f p) d -> p f d", p=128)
                for b in range(2):
                    P.op("pool", (lambda e, b=b: e.dma_start(out=wout[:, b * 11:(b + 1) * 11, :], in_=wout_v[:, b * 11:(b + 1) * 11, :])),
                         w=[(name, "wout", b)], dma=(name, "wout", b))
                nsq = [0]

                def stat_sq(src_ap, srckeys, ring="A", idx=None):
                    if ring == "A":
                        k = nsq[0] % 3
                        nsq[0] += 1
                        buf, key = sqj[k], ("sqj", k)
                    else:
                        buf, key = sqj[3 + idx % 2], ("sqj", 3 + idx % 2)
                    P.op("dve", (lambda e: e.tensor_tensor(out=buf[:], in0=src_ap, in1=src_ap, op=ALU.mult)),
                         r=srckeys, w=[key])
                    return (buf, key)

                def stat_mm(pS, pskey, bk, c):
                    buf, key = bk
                    P.op("pe", (lambda e: e.matmul(pS[:], lhsT=ones[:], rhs=buf[:], start=(c == 0), stop=(c == 7))),
                         r=[key, "ones"], w=[pskey])

                def rstd_step(pS, pskey, rs, rskey):
                    P.op("act", lambda e: e.activation(out=rs[:], in_=pS[:], func=AF.Sqrt, bias=EPS, scale=1.0 / D), r=[pskey], w=[rskey])
                    P.op("dve", lambda e: e.reciprocal(out=rs[:], in_=rs[:]), r=[rskey], w=[rskey])

                def stat_steps(src_fn, keys_fn, pS, pskey, lag):
                    st_ = []
                    ks = {}
                    for c in range(8 + lag):
                        def f_(c=c):
                            if c - lag >= 0:
                                stat_mm(pS, pskey, ks[c - lag], c - lag)
                            if c < 8:
                                ks[c] = stat_sq(src_fn(c), keys_fn(c))
                        st_.append(f_)
                    return st_

                def load_x(i):
                    s0 = tiles[i][0]
                    P.op("sp", (lambda e: e.dma_start(out=XA[:], in_=dview(src, 0, 8, s0, T))),
                         r=[("x2T", s0)] if name == "f2" else [], w=["xa"], dma=(name, "x"))

                def steps_N(i):
                    st_ = stat_steps(lambda c: XA[:, c, :], lambda c: ["xa"], pS0, "pS0", 2)
                    st_.append(lambda: rstd_step(pS0, "pS0", rsA, "rsA"))
                    for c in range(8):
                        st_.append(lambda c=c: P.op("dve", (lambda e: e.scalar_tensor_tensor(
                            out=hT[:, c, :], in0=XA[:, c, :], scalar=gcol(gsb, g_pre, c), in1=rsA[:],
                            op0=ALU.mult, op1=ALU.mult)), r=["xa", "rsA", "gsb"], w=[("hT", c)]))
                    return st_

                pend = {}

                def steps_R(i):
                    s0, d0, h0 = tiles[i]
                    st_ = []
                    nop_ = lambda: None
                    st_.append(lambda: pend.__setitem__(7, stat_sq(ysb[:, 7, :], [("ysb", 7)], "B", 7)))
                    st_.append(nop_)
                    st_.append(lambda: (stat_mm(pS1, "pS1", pend[6], 6), stat_mm(pS1, "pS1", pend[7], 7)))
                    st_.append(lambda: rstd_step(pS1, "pS1", rsB, "rsB"))
                    for c in range(8):
                        st_.append(lambda c=c: P.op("dve", (lambda e: e.scalar_tensor_tensor(
                            out=ysb[:, c, :], in0=ysb[:, c, :], scalar=gcol(ghalf, g_post, c), in1=rsB[:],
                            op0=ALU.mult, op1=ALU.mult)), r=[("ysb", c), "rsB", "ghalf"], w=[("ysb", c)]))
                    allk = [("ysb", c) for c in range(8)]
                    st_.append(lambda: P.op("pool", (lambda e: e.dma_start(out=ysb[:], in_=dview(src, 0, 8, s0, T), accum_op=ALU.add)),
                                            r=allk, w=allk, dma=(name, "xacc")))
                    if d0 is not None:
                        st_.append(lambda: P.op("pool", (lambda e: e.dma_start(out=dview(store_x, 0, 8, d0, T), in_=ysb[:])),
                                                r=allk, w=[(name, "dst", d0)], dma=(name, "st")))
                    if next_g is not None and h0 is not None:
                        st_ += [nop_] * 12
                        st_ += stat_steps(lambda c: ysb[:, c, :], lambda c: [("ysb", c)], pS0, "pS0", 2)
                        st_.append(lambda: rstd_step(pS0, "pS0", rsB, "rsB"))
                        for c in range(8):
                            def f_(c=c):
                                sl_ = c % 2
                                P.op("dve", (lambda e: e.scalar_tensor_tensor(
                                    out=h2c[sl_][:, 0, :], in0=ysb[:, c, :], scalar=gcol(gsb, next_g, c), in1=rsB[:],
                                    op0=ALU.mult, op1=ALU.mult)), r=[("ysb", c), "rsB", "gsb"], w=[("h2c", sl_)])
                                P.op("sp", (lambda e: e.dma_start(out=dview(store_h, c, 1, h0, T), in_=h2c[sl_][:])),
                                     r=[("h2c", sl_)], w=[("h2T", h0)], dma=(name, "sth", sl_))
                            st_.append(f_)
                    return st_

                def run_some(lst, k):
                    for _ in range(k):
                        if lst:
                            lst.pop(0)()

                n = len(tiles)
                load_x(0)
                run_some(steps_N(0), 99)
                for i in range(n):
                    if i + 1 < n:
                        load_x(i + 1)
                    side = steps_R(i - 1) if i >= 1 else []
                    per = 2 if side else 0
                    for f in range(NF):
                        pb = f % 2
                        blk = blk_of[f]
                        wk = [(name, "win", 0, blk), (name, "win", 1, blk)]
                        for c in range(8):
                            P.op("pe", (lambda e, c=c, f=f, pb=pb: e.matmul(
                                pG[pb][:], lhsT=win[:, c, f * 128:(f + 1) * 128], rhs=hT[:, c, :],
                                start=(c == 0), stop=(c == 7))), r=[("hT", c)] + wk, w=[("pG", pb)])
                        for c in range(8):
                            P.op("pe", (lambda e, c=c, f=f, pb=pb: e.matmul(
                                pU[pb][:], lhsT=win[:, c, DFF + f * 128:DFF + (f + 1) * 128], rhs=hT[:, c, :],
                                start=(c == 0), stop=(c == 7))), r=[("hT", c)] + wk, w=[("pU", pb)])
                        P.op("act", (lambda e, pb=pb: e.activation(out=sg[pb][:], in_=pG[pb][:], func=AF.Silu)),
                             r=[("pG", pb)], w=[("sg", pb)])
                        P.op("dve", (lambda e, pb=pb, f=f: e.tensor_tensor(
                            out=act[:, f, :], in0=sg[pb][:], in1=pU[pb][:], op=ALU.mult)),
                            r=[("sg", pb), ("pU", pb)], w=[("act", f)])
                        if f >= 1:
                            run_some(side, per)
                    run_some(side, 99)
                    side = steps_N(i + 1) if i + 1 < n else []
                    per = -(-len(side) // 7) if side else 0
                    for j in range(8):
                        pb = j % 2
                        for f in range(NF):
                            P.op("pe", (lambda e, j=j, f=f, pb=pb: e.matmul(
                                pY[pb][:], lhsT=wout[:, f, j * 128:(j + 1) * 128], rhs=act[:, f, :],
                                start=(f == 0), stop=(f == NF - 1))),
                                r=[("act", f), (name, "wout", f // 11)], w=[("pY", pb)])
                        P.op("act", (lambda e, j=j, pb=pb: e.activation(out=ysb[:, j, :], in_=pY[pb][:], func=AF.Copy)),
                             r=[("pY", pb)], w=[("ysb", j)])
                        if j >= 2:
                            stat_mm(pS1, "pS1", pend[j - 2], j - 2)
                        if j >= 1:
                            pend[j - 1] = stat_sq(ysb[:, j - 1, :], [("ysb", j - 1)], "B", j - 1)
                        if j >= 1:
                            run_some(side, per)
                    run_some(side, 99)
                run_some(steps_R(n - 1), 99)
                P.barrier()
                P.flush()

        def inproj_phase(ssmW=None):
            name = "ip"
            with ExitStack() as st:
                def S(nm, shape, dt):
                    return st.enter_context(nc.sbuf_tensor(name + nm, shape, dt))

                def PS(nm, shape):
                    return st.enter_context(nc.psum_tensor(name + nm, shape, F32))
                wmi = S("w", [128, 8, 2048], BF16)
                hin = [S("h%d" % i, [128, 8, T], BF16) for i in range(2)]
                stg = [S("stg%d" % i, [128, 16, T], BF16) for i in range(2)]
                pp = [PS("p%d" % i, [128, T]) for i in range(4)]
                psF = PS("psF", [128, T])
                if ssmW is not None:
                    P.defer = []
                    ssm_gen(ssmW, st, psF)
                    P.deferred, P.defer = P.defer, None
                    per_rep = -(-len(P.deferred) // 150)
                wv = w_mix_in.rearrange("(k p) f -> p k f", p=128)
                for b in (3, 1, 2, 0):
                    P.op("pool", (lambda e, b=b: e.dma_start(out=wmi[:, :, b * 512:(b + 1) * 512], in_=wv[:, :, b * 512:(b + 1) * 512])),
                         w=[("wmi", b)], dma=("ip", "w", b))
                n = 0
                for ti in range(NT_ALL):
                    slot = ti % 2
                    P.op("sp", (lambda e, slot=slot, ti=ti: e.dma_start(out=hin[slot][:], in_=dview(h2T, 0, 8, ti * T, T))),
                         r=[("h2T", ti * T)], w=[("hin", slot)], dma=("ip", "h", slot))
                    c_lo = 0 if ti >= NT_OWN else (4 if ti >= 4 else 12)
                    for cc in range(c_lo, 16):
                        pb = n % 4
                        for k in range(8):
                            P.op("pe", (lambda e, k=k, cc=cc, pb=pb, slot=slot: e.matmul(
                                pp[pb][:], lhsT=wmi[:, k, cc * 128:(cc + 1) * 128], rhs=hin[slot][:, k, :],
                                start=(k == 0), stop=(k == 7))), r=[("hin", slot), ("wmi", cc // 4)], w=[("ipp", pb)])
                        if cc >= 12:
                            o_ap = stg[slot][:, cc, :].rearrange("p (s j) -> p s j", s=8)
                            i_ap = pp[pb][:].rearrange("p (j s) -> p s j", s=8)
                        else:
                            o_ap = stg[slot][:, cc, :]
                            i_ap = pp[pb][:]
                        if n % 2 == 0:
                            P.op("act", (lambda e, o_ap=o_ap, i_ap=i_ap: e.activation(out=o_ap, in_=i_ap, func=AF.Copy)),
                                 r=[("ipp", pb)], w=[("stg", slot, cc)])
                        else:
                            P.op("dve", (lambda e, o_ap=o_ap, i_ap=i_ap: e.tensor_copy(out=o_ap, in_=i_ap)),
                                 r=[("ipp", pb)], w=[("stg", slot, cc)])
                        n += 1
                        if ssmW is not None:
                            P.replay(per_rep)
                    P.op("act", (lambda e, slot=slot, ti=ti, c_lo=c_lo: e.dma_start(
                        out=dview(qkvuT, c_lo, 16 - c_lo, ti * T, T), in_=stg[slot][:, c_lo:16, :])),
                        r=[("stg", slot, cc) for cc in range(c_lo, 16)], w=[("qkvu", ti)], dma=("ip", "st", slot))
                P.replay(1 << 30)
                P.barrier()
                P.flush()

        def attn_phase():
            name = "at"
            KW = SEQ - 2048
            with ExitStack() as st:
                def S(nm, shape, dt):
                    return st.enter_context(nc.sbuf_tensor(name + nm, shape, dt))
                ident = S("ident", [128, 128], BF16)
                atab = S("atab", [128, 24, 2, 128], F32)
                btab = S("btab", [128, 24, 128], F32)
                qT = S("qT", [128, HALF], BF16)
                kT = S("kT", [128, KW], BF16)
                vT = S("vT", [128, KW], BF16)
                vtok = S("vtok", [128, 48, 2, 128], BF16)
                acc = S("acc", [128, 2, HALF], F32)
                sb = [S("sb%d" % i, [128, 4, 128], F32) for i in range(3)]
                pT = [S("pT%d" % i, [128, 4, 128], BF16) for i in range(3)]
                rd = S("rd", [128, T], F32)
                ao = S("ao", [128, HALF], BF16)
                psS = [st.enter_context(nc.psum_tensor(name + "s%d" % i, [128, 4, 128], F32)) for i in range(3)]
                psO = [st.enter_context(nc.psum_tensor(name + "o%d" % i, [128, 512], F32)) for i in range(2)]
                psT = [st.enter_context(nc.psum_tensor(name + "t%d" % i, [128, 8, 128], BF16)) for i in range(2)]

                P.op("pool", lambda e: e.dma_start(out=ident[:], in_=ident_d), w=["ident"], dma=("at", "c"))
                P.op("sp", lambda e: e.dma_start(out=atab[:], in_=atab_d.rearrange("p (a b c) -> p a b c", a=24, b=2)), w=["atab"], dma=("at", "c1"))
                P.op("sp", lambda e: e.dma_start(out=btab[:], in_=btab_d.rearrange("p (a c) -> p a c", a=24)), w=["btab"], dma=("at", "c2"))
                P.op("pool", lambda e: e.memset(vtok[:, :, 0, 64:128], 1.0), w=["vones"])
                P.op("pool", lambda e: e.memset(vtok[:, :, 1, 0:64], 1.0), w=["vones"])
                allq = [("qkvu", ti) for ti in range(NT_ALL)]
                nq = 0
                for hp in range(4):
                    P.op("sp", (lambda e, hp=hp: e.dma_start(out=qT[:], in_=qkvuT[hp * 128:(hp + 1) * 128, HALF:SEQ])),
                         r=allq, w=["qT"], dma=("at", "q"))
                    P.op("sp", (lambda e, hp=hp: e.dma_start(out=kT[:], in_=qkvuT[512 + hp * 128:512 + (hp + 1) * 128, 2048:SEQ])),
                         r=allq, w=["kT"], dma=("at", "k"))
                    P.op("sp", (lambda e, hp=hp: e.dma_start(out=vT[:], in_=qkvuT[1024 + hp * 128:1024 + (hp + 1) * 128, 2048:SEQ])),
                         r=allq, w=["vT"], dma=("at", "v"))
                    for br, d in enumerate((1, 4, 16)):
                        nblk = 48 // d
                        n0 = 16 // d
                        for g4 in range(12 if (debug or {}).get("alvl", 9) >= 1 else 0):
                            tb = g4 % 2
                            for j in range(4):
                                blk = g4 * 4 + j
                                r_, n_ = blk // nblk, blk % nblk
                                s0 = r_ + d * 128 * n_
                                P.op("pe", (lambda e, tb=tb, j=j, s0=s0, d=d: e.transpose(
                                    psT[tb][:, j, :], vT[:, s0:s0 + 127 * d + 1:d], ident[:])),
                                    r=["vT", "ident"], w=[("psT", tb)])
                            P.op("act", (lambda e, tb=tb, g4=g4: e.activation(
                                out=vtok[:, g4 * 4:g4 * 4 + 4, 0, 0:64], in_=psT[tb][:, 0:4, 0:64], func=AF.Copy)),
                                r=[("psT", tb)], w=[("vtok", g4, 0)])
                            P.op("act", (lambda e, tb=tb, g4=g4: e.activation(
                                out=vtok[:, g4 * 4:g4 * 4 + 4, 1, 64:128], in_=psT[tb][:, 0:4, 64:128], func=AF.Copy)),
                                r=[("psT", tb)], w=[("vtok", g4, 1)])
                        pairs = [(h2, r_, n_) for h2 in range(2) for r_ in range(d) for n_ in range(n0, nblk, 2)]
                        LAG = 2

                        def stage_a(idx, h2, r_, n_, d=d, br=br, hp=hp, nblk=nblk, n0=n0):
                            rows = slice(h2 * 64, h2 * 64 + 64)
                            tix = (hp * 2 + h2) * 3 + br
                            sbi = idx % 3
                            for b in range(2):
                                kb = r_ + d * 128 * (n_ + b - 1)
                                qb = r_ + d * 128 * (n_ + b) - 2048
                                for hh in range(2):
                                    k0 = kb + hh * 128 * d
                                    P.op("pe", (lambda e, hh=hh, k0=k0, qb=qb, b=b: e.matmul(
                                        psS[sbi][:, 2 * b + hh, :], lhsT=kT[rows, k0:k0 + 127 * d + 1:d], rhs=qT[rows, qb:qb + 127 * d + 1:d],
                                        start=True, stop=True)), r=["kT", "qT"], w=[("psS", sbi)])
                            if n_ == n0:
                                P.op("dve", (lambda e: e.tensor_tensor(
                                    out=sb[sbi][:, 0, :], in0=psS[sbi][:, 0, :], in1=btab[:, tix, :], op=ALU.add)),
                                    r=[("psS", sbi), "btab"], w=[("sb", sbi)])
                                P.op("dve", (lambda e: e.tensor_tensor(
                                    out=sb[sbi][:, 1, :], in0=psS[sbi][:, 1, :], in1=atab[:, tix, 1, :], op=ALU.add)),
                                    r=[("psS", sbi), "atab"], w=[("sb", sbi)])
                                P.op("dve", (lambda e: e.tensor_tensor(
                                    out=sb[sbi][:, 2:4, :], in0=psS[sbi][:, 2:4, :], in1=atab[:, tix, :, :], op=ALU.add)),
                                    r=[("psS", sbi), "atab"], w=[("sb", sbi)])
                            else:
                                tb2 = atab[:, tix, :, :].rearrange("p a b -> p (a b)").unsqueeze(1).to_broadcast([128, 2, 256])
                                P.op("dve", (lambda e: e.tensor_tensor(
                                    out=sb[sbi][:].rearrange("p (x a) b -> p x (a b)", x=2),
                                    in0=psS[sbi][:].rearrange("p (x a) b -> p x (a b)", x=2), in1=tb2, op=ALU.add)),
                                    r=[("psS", sbi), "atab"], w=[("sb", sbi)])
                            P.op("act", (lambda e: e.activation(out=pT[sbi][:], in_=sb[sbi][:], func=AF.Exp, scale=0.125)),
                                 r=[("sb", sbi)], w=[("pT", sbi)])

                        def stage_b(idx, h2, r_, n_, d=d, br=br, hp=hp, nblk=nblk, n0=n0):
                            sbi = idx % 3
                            ob = idx % 2
                            for b in range(2):
                                blk = r_ * nblk + n_ + b
                                for hh in range(2):
                                    vb = blk - 1 + hh
                                    P.op("pe", (lambda e, hh=hh, vb=vb, b=b: e.matmul(
                                        psO[ob][:, b * 128:(b + 1) * 128], lhsT=vtok[:, vb, h2, :], rhs=pT[sbi][:, 2 * b + hh, :],
                                        start=(hh == 0), stop=(hh == 1))),
                                        r=[("pT", sbi), ("vtok", vb // 4, h2), "vones"], w=[("psO", ob)])
                            qb = r_ + d * 128 * n_ - 2048
                            asl = acc[:, h2, qb:qb + 255 * d + 1:d]
                            if br == 0:
                                P.op("act", (lambda e: e.activation(out=asl, in_=psO[ob][:, 0:256], func=AF.Copy)),
                                     r=[("psO", ob)], w=[("acc", h2)])
                            else:
                                P.op("dve", (lambda e: e.tensor_tensor(out=asl, in0=asl, in1=psO[ob][:, 0:256], op=ALU.add)),
                                     r=[("psO", ob), ("acc", h2)], w=[("acc", h2)])
                        for idx in range(len(pairs) + LAG):
                            if idx < len(pairs):
                                stage_a(idx, *pairs[idx])
                            if idx - LAG >= 0:
                                stage_b(idx - LAG, *pairs[idx - LAG])
                    for tt in range(NT_OWN):
                        ts_ = slice(tt * T, (tt + 1) * T)
                        P.op("dve", (lambda e, ts_=ts_: e.reciprocal(out=rd[0:64, :], in_=acc[64:128, 0, ts_])),
                             r=[("acc", 0), "ao"], w=["rd0"])
                        P.op("dve", (lambda e, ts_=ts_: e.tensor_tensor(out=ao[0:64, ts_], in0=acc[0:64, 0, ts_], in1=rd[0:64, :], op=ALU.mult)),
                             r=[("acc", 0), "rd0"], w=["ao"])
                        P.op("dve", (lambda e, ts_=ts_: e.reciprocal(out=rd[64:128, :], in_=acc[0:64, 1, ts_])),
                             r=[("acc", 1), "ao"], w=["rd1"])
                        P.op("dve", (lambda e, ts_=ts_: e.tensor_tensor(out=ao[64:128, ts_], in0=acc[64:128, 1, ts_], in1=rd[64:128, :], op=ALU.mult)),
                             r=[("acc", 1), "rd1"], w=["ao"])
                    P.op("act", (lambda e, hp=hp: e.dma_start(out=catT[hp * 128:(hp + 1) * 128, :], in_=ao[:])),
                         r=["ao"], w=[("cat", hp)], dma=("at", "st"))
                P.barrier()
                P.flush()

        NI = 10
        PI_ = float(np.pi)
        NJC = SEQ // 8
        GK = "ssgen"

        def ssm_persist(st):
            def S(nm, shape, dt):
                return st.enter_context(nc.sbuf_tensor("s2" + nm, shape, dt))
            sv = S("sv", [128, 16], F32)
            swp = S("swp", [128, 128], F32)
            idf = S("idf", [128, 128], F32)
            idb = S("idb", [128, 128], BF16)
            wglu = S("wglu", [128, 4, 512], BF16)
            Bdec = S("Bdec", [128, 32, 128], BF16)
            Cdec = S("Cdec", [128, 32, 128], BF16)
            Toep = S("Toep", [128, 32, 128], BF16)
            PR8 = S("PR8", [128, NI, 32], F32)
            PI8 = S("PI8", [128, NI, 32], F32)
            SPI8 = S("SPI8", [128, NI, 32], F32)

            def ld(dst, src_, key, eng="sp"):
                P.op(eng, (lambda e: e.dma_start(out=dst, in_=src_)), w=[key], dma=("s2", key))
            ld(sv[:], ssm_v_d, "sv")
            ld(swp[:], swap_d, "swp")
            ld(idf[:], ident_d, "idf")
            ld(idb[:], ident_d, "idb", "pool")
            ld(wglu[:], w_glu.rearrange("(k p) f -> p k f", p=128), "wglu", "pool")
            return (sv, swp, idf, idb, wglu, Bdec, Cdec, Toep, PR8, PI8, SPI8)

        def ssm_gen(W, st2, psF):
            name = "s2"
            sv, swp, idf, idb, wglu, Bdec, Cdec, Toep, PR8, PI8, SPI8 = W
            sgnA, sgnB = sv[:, 8:9], sv[:, 9:10]

            def ld(dst, src_, key, eng="sp"):
                P.op(eng, (lambda e: e.dma_start(out=dst, in_=src_)), w=[key], dma=("s2", key))

            def dve(fn, r=(GK,), w=(GK,)):
                P.op("dve", fn, r=list(r), w=list(w))

            def actf(fn, r=(GK,), w=(GK,)):
                P.op("act", fn, r=list(r), w=list(w))
            if True:
                if True:
                    def S2(nm, shape, dt):
                        return st2.enter_context(nc.sbuf_tensor(name + nm, shape, dt))
                    prm = S2("prm", [128, 96], F32)
                    bab = S2("bab", [128, 2, 32, 16], F32)
                    cab = S2("cab", [128, 2, 32, 16], F32)
                    cmask = S2("cmask", [128, 128], F32)
                    dstk = S2("dstk", [128, 32], F32)
                    tmpv = [S2("tv%d" % i, [128, 32], F32) for i in range(12)]
                    POWr = S2("POWr", [128, 16, 32], F32)
                    POWi = S2("POWi", [128, 16, 32], F32)
                    Qr = S2("Qr", [128, 32, 8], F32)
                    Qi = S2("Qi", [128, 32, 8], F32)
                    Q2r = S2("Q2r", [128, 32, 8], F32)
                    Q2i = S2("Q2i", [128, 32, 8], F32)
                    BdT = S2("BdT", [128, 32, 8, 16], F32)
                    big = S2("big", [128, 32, 9, 16], F32)
                    VV = S2("VV", [128, 32, 9, 16], F32)
                    rt = S2("rt", [128, 128], F32)
                    ld(prm[:], ssm_a_d, "prm")
                    ld(bab[:], ssm_b_d.rearrange("p (a g c) -> p a g c", a=2, g=32), "bab")
                    ld(cab[:], ssm_c_d.rearrange("p (a g c) -> p a g c", a=2, g=32), "cab")
                    ld(cmask[:], cmask_d, "cmask")
                    ld(dstk[:], dstk_d, "dstk")
                    are, aim, ldt = prm[:, 0:32], prm[:, 32:64], prm[:, 64:96]
                    dt_, lr, li, mag, angs, angc, m_, t1, t2, t3, wr, wi = [t[:] for t in tmpv]
                    actf(lambda e: e.activation(out=dt_, in_=ldt, func=AF.Exp), r=("prm", GK))
                    dve(lambda e: e.tensor_tensor(out=lr, in0=dt_, in1=are, op=ALU.mult), r=("prm", GK))
                    dve(lambda e: e.tensor_tensor(out=li, in0=dt_, in1=aim, op=ALU.mult))
                    actf(lambda e: e.activation(out=mag, in_=lr, func=AF.Exp))
                    dve(lambda e: e.tensor_copy(out=angs, in_=li))
                    dve(lambda e: e.tensor_scalar(out=angc, in0=li, scalar1=PI_ / 2, scalar2=None, op0=ALU.add))
                    for it in range(4):
                        for ang in (angs, angc):
                            dve(lambda e, ang=ang: e.tensor_scalar(out=m_, in0=ang, scalar1=PI_, scalar2=2 * PI_, op0=ALU.is_gt, op1=ALU.mult))
                            dve(lambda e, ang=ang: e.tensor_tensor(out=ang, in0=ang, in1=m_, op=ALU.subtract))
                    actf(lambda e: e.activation(out=angs, in_=angs, func=AF.Sin))
                    actf(lambda e: e.activation(out=angc, in_=angc, func=AF.Sin))
                    K0 = 7

                    def pw(k):
                        return POWr[:, K0 + k, :], POWi[:, K0 + k, :]
                    dve(lambda e: e.memset(POWr[:, K0, :], 1.0))
                    dve(lambda e: e.memset(POWi[:, K0, :], 0.0))
                    dve(lambda e: e.tensor_tensor(out=pw(1)[0], in0=mag, in1=angc, op=ALU.mult))
                    dve(lambda e: e.tensor_tensor(out=pw(1)[1], in0=mag, in1=angs, op=ALU.mult))

                    def cmul(zr, zi, xr, xi, yr, yi):
                        dve(lambda e: e.tensor_tensor(out=t1, in0=xr, in1=yr, op=ALU.mult))
                        dve(lambda e: e.tensor_tensor(out=t2, in0=xi, in1=yi, op=ALU.mult))
                        dve(lambda e: e.tensor_tensor(out=zr, in0=t1, in1=t2, op=ALU.subtract))
                        dve(lambda e: e.tensor_tensor(out=t1, in0=xr, in1=yi, op=ALU.mult))
                        dve(lambda e: e.tensor_tensor(out=t2, in0=xi, in1=yr, op=ALU.mult))
                        dve(lambda e: e.tensor_tensor(out=zi, in0=t1, in1=t2, op=ALU.add))
                    for k in range(2, 9):
                        cmul(*pw(k), *pw(k - 1), *pw(1))
                    dve(lambda e: e.tensor_tensor(out=t1, in0=pw(1)[0], in1=pw(1)[0], op=ALU.mult))
                    dve(lambda e: e.tensor_tensor(out=t2, in0=pw(1)[1], in1=pw(1)[1], op=ALU.mult))
                    dve(lambda e: e.tensor_tensor(out=t3, in0=t1, in1=t2, op=ALU.add))
                    dve(lambda e: e.reciprocal(out=t3, in_=t3))
                    dve(lambda e: e.tensor_tensor(out=pw(-1)[0], in0=pw(1)[0], in1=t3, op=ALU.mult))
                    dve(lambda e: e.scalar_tensor_tensor(out=pw(-1)[1], in0=pw(1)[1], scalar=-1.0, in1=t3, op0=ALU.mult, op1=ALU.mult))
                    for k in range(-2, -8, -1):
                        cmul(*pw(k), *pw(k + 1), *pw(-1))
                    dve(lambda e: e.tensor_scalar(out=m_, in0=pw(1)[0], scalar1=-1.0, scalar2=None, op0=ALU.add))
                    dve(lambda e: e.tensor_tensor(out=t1, in0=are, in1=are, op=ALU.mult))
                    dve(lambda e: e.tensor_tensor(out=t2, in0=aim, in1=aim, op=ALU.mult))
                    dve(lambda e: e.tensor_tensor(out=t3, in0=t1, in1=t2, op=ALU.add))
                    dve(lambda e: e.reciprocal(out=t3, in_=t3))
                    dve(lambda e: e.tensor_tensor(out=wr, in0=m_, in1=are, op=ALU.mult))
                    dve(lambda e: e.tensor_tensor(out=t1, in0=pw(1)[1], in1=aim, op=ALU.mult))
                    dve(lambda e: e.tensor_tensor(out=wr, in0=wr, in1=t1, op=ALU.add))
                    dve(lambda e: e.tensor_tensor(out=wr, in0=wr, in1=t3, op=ALU.mult))
                    dve(lambda e: e.tensor_tensor(out=wi, in0=pw(1)[1], in1=are, op=ALU.mult))
                    dve(lambda e: e.tensor_tensor(out=t1, in0=m_, in1=aim, op=ALU.mult))
                    dve(lambda e: e.tensor_tensor(out=wi, in0=wi, in1=t1, op=ALU.subtract))
                    dve(lambda e: e.tensor_tensor(out=wi, in0=wi, in1=t3, op=ALU.mult))
                    for s_ in range(8):
                        cmul(Qr[:, :, s_], Qi[:, :, s_], *pw(7 - s_), wr, wi)
                        cmul(Q2r[:, :, s_], Q2i[:, :, s_], *pw(-s_), wr, wi)
                    SH = [128, 32, 8, 16]
                    BAb = bab[:, 0, :, :].unsqueeze(2).to_broadcast(SH)
                    BBb = bab[:, 1, :, :].unsqueeze(2).to_broadcast(SH)
                    def bq(dst, qr_, qi_):
                        qrb = qr_[:].unsqueeze(3).to_broadcast(SH)
                        qib = qi_[:].unsqueeze(3).to_broadcast(SH)
                        dve(lambda e: e.tensor_tensor(out=big[:, :, 0:8, :], in0=BBb, in1=qib, op=ALU.mult), r=("bab", GK))
                        dve(lambda e: e.tensor_scalar(out=big[:, :, 0:8, :], in0=big[:, :, 0:8, :], scalar1=sgnB, scalar2=None, op0=ALU.mult), r=("sv", GK))
                        dve(lambda e: e.tensor_tensor(out=dst[:], in0=BAb, in1=qrb, op=ALU.mult), r=("bab", GK))
                        dve(lambda e: e.tensor_tensor(out=dst[:], in0=dst[:], in1=big[:, :, 0:8, :], op=ALU.add))
                    bq(BdT, Qr, Qi)
                    for g in range(32):
                        P.op("pe", (lambda e, g=g: e.transpose(psF[:, 0:128], BdT[:, g, :, :].rearrange("p s c -> p (s c)"), idf[:])),
                             r=[GK, "idf"], w=["psF"])
                        P.op("act", (lambda e, g=g: e.activation(out=Bdec[:, g, :], in_=psF[:, 0:128], func=AF.Copy)),
                             r=["psF"], w=["Bdec"])
                    UTp = BdT
                    bq(UTp, Q2r, Q2i)
                    SH9 = [128, 32, 9, 16]
                    CAb = cab[:, 0, :, :].unsqueeze(2).to_broadcast(SH9)
                    CBb = cab[:, 1, :, :].unsqueeze(2).to_broadcast(SH9)
                    prb = POWr[:, K0:K0 + 9, :].rearrange("p k g -> p g k").unsqueeze(3).to_broadcast(SH9)
                    pib = POWi[:, K0:K0 + 9, :].rearrange("p k g -> p g k").unsqueeze(3).to_broadcast(SH9)
                    dve(lambda e: e.tensor_tensor(out=VV[:], in0=CAb, in1=prb, op=ALU.mult), r=("cab", GK))
                    dve(lambda e: e.tensor_scalar(out=VV[:], in0=VV[:], scalar1=sgnA, scalar2=None, op0=ALU.mult), r=("sv", GK))
                    dve(lambda e: e.tensor_tensor(out=big[:], in0=CBb, in1=pib, op=ALU.mult), r=("cab", GK))
                    dve(lambda e: e.tensor_tensor(out=VV[:], in0=VV[:], in1=big[:], op=ALU.subtract))
                    dve(lambda e: e.tensor_copy(out=Cdec[:].rearrange("p g (t c) -> p g t c", t=8), in_=VV[:, :, 1:9, :]), w=(GK, "Cdec"))
                    for g in range(32):
                        P.op("pe", (lambda e, g=g: e.matmul(psF[:, 128:256], lhsT=UTp[:, g, :, :].rearrange("p s c -> p (s c)"),
                                                            rhs=VV[:, g, 0:8, :].rearrange("p t c -> p (t c)"), start=True, stop=True)),
                             r=[GK, "Bdec"], w=["psF"])
                        P.op("dve", (lambda e, g=g: e.tensor_tensor(out=rt[:], in0=psF[:, 128:256], in1=cmask[:], op=ALU.mult)),
                             r=["psF", "cmask"], w=["rt"])
                        P.op("dve", (lambda e, g=g: e.scalar_tensor_tensor(out=Toep[:, g, :], in0=idf[:], scalar=dstk[:, g:g + 1], in1=rt[:],
                                                                           op0=ALU.mult, op1=ALU.add)),
                             r=["rt", "idf", "dstk"], w=["Toep"])
                    dve(lambda e: e.tensor_copy(out=PR8[:, 0, :], in_=pw(8)[0]))
                    dve(lambda e: e.tensor_copy(out=PI8[:, 0, :], in_=pw(8)[1]))
                    for i in range(1, NI):
                        dve(lambda e, i=i: e.tensor_tensor(out=t1, in0=PR8[:, i - 1, :], in1=PR8[:, i - 1, :], op=ALU.mult))
                        dve(lambda e, i=i: e.tensor_tensor(out=t2, in0=PI8[:, i - 1, :], in1=PI8[:, i - 1, :], op=ALU.mult))
                        dve(lambda e, i=i: e.tensor_tensor(out=PR8[:, i, :], in0=t1, in1=t2, op=ALU.subtract))
                        dve(lambda e, i=i: e.scalar_tensor_tensor(out=PI8[:, i, :], in0=PR8[:, i - 1, :], scalar=2.0, in1=PI8[:, i - 1, :],
                                                                  op0=ALU.mult, op1=ALU.mult))
                    dve(lambda e: e.tensor_scalar(out=SPI8[:], in0=PI8[:], scalar1=sgnA, scalar2=None, op0=ALU.mult), r=("sv", GK))

        def ssm_phase(W):
            name = "s2"
            sv, swp, idf, idb, wglu, Bdec, Cdec, Toep, PR8, PI8, SPI8 = W
            with ExitStack() as st:
                def S(nm, shape, dt):
                    return st.enter_context(nc.sbuf_tensor(name + nm, shape, dt))

                def PS(nm, shape, dt=F32):
                    return st.enter_context(nc.psum_tensor(name + nm, shape, dt))
                sel = S("sel", [128, 64, 128], BF16)
                selT = S("selT", [128, 64, 128], BF16)
                psA = [PS("a%d" % i, [128, T]) for i in range(4)]
                psY = [PS("y%d" % i, [128, T]) for i in range(2)]
                P.op("pool", (lambda e: e.dma_start(out=sel[:], in_=sel_d.rearrange("p (a b) -> p a b", a=64))), w=["sel"], dma=("s2", "sel"))
                P.op("pool", (lambda e: e.dma_start(out=selT[:], in_=selT_d.rearrange("p (a b) -> p a b", a=64))), w=["selT"], dma=("s2", "selT"))
                Rq2 = [S("Rq%d" % i, [128, 8, NI, 128], BF16) for i in range(2)]
                rtmp = [S("rtmp%d" % i, [128, 128], F32) for i in range(2)]
                uT = S("uT", [128, SEQ], BF16)
                U1 = [S("U1%d" % i, [128, NJC], BF16) for i in range(4)]
                Hs = [[S("Hs%d_%d" % (a_, i), [128, NJC], BF16) for i in range(2)] for a_ in range(4)]
                g1 = [S("g1%d" % i, [128, T], F32) for i in range(2)]
                Yg = S("Yg", [128, 8, T], BF16)
                yg = S("yg", [128, 4, HALF], BF16)
                so = [S("so%d" % i, [128, 4, T], BF16) for i in range(2)]
                sgm = [S("sgm%d" % i, [128, T], F32) for i in range(2)]
                allq = [("qkvu", ti) for ti in range(NT_ALL)]
                nev = 0
                nrt = 0
                ngr = 0

                def evac(pb, dst, keys_w):
                    if pb % 2 == 0:
                        P.op("act", (lambda e: e.activation(out=dst, in_=psA[pb][:], func=AF.Copy)), r=[("psA", pb)], w=keys_w)
                    else:
                        P.op("dve", (lambda e: e.tensor_copy(out=dst, in_=psA[pb][:])), r=[("psA", pb)], w=keys_w)
                def gen_R(q):
                    nonlocal nrt
                    Rq_ = Rq2[q % 2]
                    for gm in range(8):
                        g = 8 * q + gm
                        for i in range(NI):
                            rb = nrt % 2
                            nrt += 1
                            P.op("dve", (lambda e, rb=rb, i=i, g=g: e.tensor_scalar(
                                out=rtmp[rb][:], in0=swp[:], scalar1=SPI8[:, i, g:g + 1], scalar2=None, op0=ALU.mult)),
                                r=[GK, "swp"], w=[("rtmp", rb)])
                            P.op("dve", (lambda e, rb=rb, i=i, g=g, gm=gm, Rq_=Rq_: e.scalar_tensor_tensor(
                                out=Rq_[:, gm, i, :], in0=idf[:], scalar=PR8[:, i, g:g + 1], in1=rtmp[rb][:],
                                op0=ALU.mult, op1=ALU.add)), r=[GK, "idf", ("rtmp", rb)], w=[("R", q % 2, gm)])
                gen_R(0)
                for q in range(4):
                    Rq = Rq2[q % 2]
                    P.op("sp", (lambda e, q=q: e.dma_start(out=uT[:], in_=qkvuT[1536 + q * 128:1536 + (q + 1) * 128, :])),
                         r=allq, w=["uT"], dma=("s2", "u"))
                    for pr_ in range(2):
                        if pr_ == 1 and q + 1 < 4:
                            gen_R(q + 1)
                        gms = tuple(range(4 * pr_, 4 * pr_ + 4))
                        for ab, gm in enumerate(gms):
                            for hf in range(2):
                                pb = nev % 4
                                nev += 1
                                for s_ in range(8):
                                    P.op("pe", (lambda e, pb=pb, gm=gm, s_=s_, hf=hf: e.matmul(
                                        psA[pb][:], lhsT=sel[:, gm * 8 + s_, :],
                                        rhs=uT[:].rearrange("p (t s j) -> p t s j", s=8, j=64)[:, hf * 8:(hf + 1) * 8, s_, :],
                                        start=(s_ == 0), stop=(s_ == 7))), r=["uT", "sel"], w=[("psA", pb)])
                                evac(pb, U1[ab][:, hf * T:(hf + 1) * T], [("U1", ab, hf)])
                        cur = 0
                        for ab, gm in enumerate(gms):
                            g = 8 * q + gm
                            for hf in range(2):
                                pb = nev % 4
                                nev += 1
                                P.op("pe", (lambda e, pb=pb, g=g, ab=ab, hf=hf: e.matmul(
                                    psA[pb][:], lhsT=Bdec[:, g, :], rhs=U1[ab][:, hf * T:(hf + 1) * T], start=True, stop=True)),
                                    r=[("U1", ab, hf), "Bdec"], w=[("psA", pb)])
                                evac(pb, Hs[ab][cur][:, hf * T:(hf + 1) * T], [("Hs", ab, cur, hf)])
                        for i in range(NI):
                            dd = 1 << i
                            nxt = 1 - cur
                            for tt in range(2):
                                for ab, gm in enumerate(gms):
                                    c0 = tt * T
                                    lo = max(0, dd - c0)
                                    has = lo < T
                                    pb = nev % 4
                                    nev += 1
                                    P.op("pe", (lambda e, pb=pb, c0=c0, cur=cur, has=has, ab=ab: e.matmul(
                                        psA[pb][:], lhsT=idb[:], rhs=Hs[ab][cur][:, c0:c0 + T], start=True, stop=not has)),
                                        r=[("Hs", ab, cur, tt), "idb"], w=[("psA", pb)])
                                    if has:
                                        s_lo = c0 + lo - dd
                                        s_hi = c0 + T - dd
                                        rk = [("Hs", ab, cur, s_lo // T), ("Hs", ab, cur, (s_hi - 1) // T), ("R", q % 2, gm)]
                                        P.op("pe", (lambda e, pb=pb, lo=lo, s_lo=s_lo, s_hi=s_hi, cur=cur, gm=gm, i=i, ab=ab, Rq=Rq: e.matmul(
                                            psA[pb][:, lo:T], lhsT=Rq[:, gm, i, :], rhs=Hs[ab][cur][:, s_lo:s_hi], start=False, stop=True)),
                                            r=rk, w=[("psA", pb)])
                                    evac(pb, Hs[ab][nxt][:, c0:c0 + T], [("Hs", ab, nxt, tt)])
                            cur = nxt
                        for ab, gm in enumerate(gms):
                            g = 8 * q + gm
                            yb = ab % 2
                            P.op("pe", (lambda e, yb=yb, g=g, ab=ab: e.matmul(
                                psY[yb][:], lhsT=Toep[:, g, :], rhs=U1[ab][:, T:2 * T], start=True, stop=False)),
                                r=[("U1", ab, 1), "Toep"], w=[("psY", yb)])
                            P.op("pe", (lambda e, yb=yb, g=g, cur=cur, ab=ab: e.matmul(
                                psY[yb][:], lhsT=Cdec[:, g, :], rhs=Hs[ab][cur][:, T - 1:2 * T - 1], start=False, stop=True)),
                                r=[("Hs", ab, cur, 0), ("Hs", ab, cur, 1), "Cdec"], w=[("psY", yb)])
                            gk = ("g1", yb)
                            gg = g1[yb]
                            P.op("act", (lambda e, yb=yb, gg=gg: e.activation(out=gg[:], in_=psY[yb][:], func=AF.Square)), r=[("psY", yb)], w=[gk])
                            P.op("dve", (lambda e, gg=gg: e.tensor_scalar(out=gg[:], in0=gg[:], scalar1=0.044715, scalar2=1.0, op0=ALU.mult, op1=ALU.add)),
                                 r=[gk], w=[gk])
                            P.op("dve", (lambda e, yb=yb, gg=gg: e.tensor_tensor(out=gg[:], in0=gg[:], in1=psY[yb][:], op=ALU.mult)),
                                 r=[gk, ("psY", yb)], w=[gk])
                            P.op("act", (lambda e, gg=gg: e.activation(out=gg[:], in_=gg[:], func=AF.Sigmoid, scale=1.5957691216057308)), r=[gk], w=[gk])
                            P.op("dve", (lambda e, yb=yb, gg=gg, gm=gm: e.tensor_tensor(out=Yg[:, gm, :], in0=gg[:], in1=psY[yb][:], op=ALU.mult)),
                                 r=[gk, ("psY", yb)], w=[("Yg", gm)])
                    for t_ in range(8):
                        pb = nev % 4
                        nev += 1
                        for gm in range(8):
                            P.op("pe", (lambda e, pb=pb, gm=gm, t_=t_: e.matmul(
                                psA[pb][:], lhsT=selT[:, gm * 8 + t_, :], rhs=Yg[:, gm, :], start=(gm == 0), stop=(gm == 7))),
                                r=[("Yg", gm), "selT"], w=[("psA", pb)])
                        evac(pb, yg[:, q, t_:t_ + 8 * 511 + 1:8], [("yg", q)])
                for t8 in range(NT_OWN):
                    ts_ = slice(t8 * T, (t8 + 1) * T)
                    sob = so[t8 % 2]
                    for jo in range(4):
                        yb = jo % 2
                        sg_ = sgm[jo % 2]
                        for k in range(4):
                            P.op("pe", (lambda e, yb=yb, k=k, jo=jo, ts_=ts_: e.matmul(
                                psY[yb][:], lhsT=wglu[:, k, jo * 128:(jo + 1) * 128], rhs=yg[:, k, ts_], start=(k == 0), stop=(k == 3))),
                                r=[("yg", k), "wglu"], w=[("psY", yb)])
                        P.op("act", (lambda e, yb=yb, jo=jo, sg_=sg_: e.activation(out=sg_[:], in_=psY[yb][:], func=AF.Sigmoid, bias=sv[:, 4 + jo:5 + jo])),
                             r=[("psY", yb), "sv"], w=[("sgm", jo % 2)])
                        P.op("dve", (lambda e, jo=jo, ts_=ts_, sg_=sg_, sob=sob: e.tensor_tensor(out=sob[:, jo, :], in0=yg[:, jo, ts_], in1=sg_[:], op=ALU.mult)),
                             r=[("sgm", jo % 2), ("yg", jo)], w=[("so", t8 % 2)])
                    P.op("sp", (lambda e, t8=t8, sob=sob: e.dma_start(out=dview(catT, 4, 4, t8 * T, T), in_=sob[:])),
                         r=[("so", t8 % 2)], w=[("cat", 4 + t8 % 4)], dma=("s2", "st", t8 % 2))
                P.barrier()
                P.flush()


        def outproj_phase():
            name = "op"
            with ExitStack() as st:
                def S(nm, shape, dt):
                    return st.enter_context(nc.sbuf_tensor(name + nm, shape, dt))

                def PS(nm, shape):
                    return st.enter_context(nc.psum_tensor(name + nm, shape, F32))
                wmo = S("w", [128, 8, D], BF16)
                cin = [S("c%d" % i, [128, 8, T], BF16) for i in range(2)]
                xin = [S("x%d" % i, [128, 8, T], F32) for i in range(2)]
                ysb2 = [S("ysb%d" % i, [128, 8, T], F32) for i in range(2)]
                sq2 = [S("sq%d" % i, [128, 8, T], BF16) for i in range(2)]
                rstd2 = [S("rstd%d" % i, [128, T], F32) for i in range(2)]
                pY = [PS("pY%d" % i, [128, T]) for i in range(4)]
                pS2 = [PS("pS%d" % i, [128, T]) for i in range(2)]
                wv = w_mix_out.rearrange("(k p) f -> p k f", p=128)
                for b in range(2):
                    P.op("pool", (lambda e, b=b: e.dma_start(out=wmo[:, b * 4:(b + 1) * 4, :], in_=wv[:, b * 4:(b + 1) * 4, :])),
                         w=[("wmo", b)], dma=("op", "w", b))
                allcat = [("cat", i) for i in range(8)]
                for ti in range(NT_OWN):
                    slot = ti % 2
                    t0 = ti * T
                    ysb, sq, rstd, pS = ysb2[slot], sq2[slot], rstd2[slot], pS2[slot]
                    kY, kQ, kR, kP = ("opysb", slot), ("opsq", slot), ("oprstd", slot), ("oppS", slot)
                    def op_loads(tj):
                        sl_, tq = tj % 2, tj * T
                        P.op("sp", (lambda e: e.dma_start(out=cin[sl_][:], in_=dview(catT, 0, 8, tq, T))),
                             r=allcat, w=[("cin", sl_)], dma=("op", "c", sl_))
                        P.op("sp", (lambda e: e.dma_start(out=xin[sl_][:], in_=dview(x1T, 0, 8, tq, T))),
                             r=[("f1", "dst", tq)], w=[("xin", sl_)], dma=("op", "x", sl_))
                    if ti == 0:
                        op_loads(0)
                    if ti + 1 < NT_OWN:
                        op_loads(ti + 1)
                    for j in range(8):
                        pb = (ti * 8 + j) % 4
                        for k in range(8):
                            P.op("pe", (lambda e, j=j, k=k, pb=pb, slot=slot: e.matmul(
                                pY[pb][:], lhsT=wmo[:, k, j * 128:(j + 1) * 128], rhs=cin[slot][:, k, :],
                                start=(k == 0), stop=(k == 7))), r=[("cin", slot), ("wmo", k // 4)], w=[("opY", pb)])
                        P.op("act", (lambda e, j=j, pb=pb, ysb=ysb: e.activation(out=ysb[:, j, :], in_=pY[pb][:], func=AF.Copy)),
                             r=[("opY", pb)], w=[kY + (j,)])
                        P.op("dve", (lambda e, j=j, ysb=ysb, sq=sq: e.tensor_tensor(out=sq[:, j, :], in0=ysb[:, j, :], in1=ysb[:, j, :], op=ALU.mult)),
                             r=[kY + (j,)], w=[kQ + (j,)])
                    for c in range(8):
                        P.op("pe", (lambda e, c=c, sq=sq, pS=pS: e.matmul(pS[:], lhsT=ones[:], rhs=sq[:, c, :], start=(c == 0), stop=(c == 7))),
                             r=[kQ + (c,), "ones"], w=[kP])
                    P.op("act", (lambda e, rstd=rstd, pS=pS: e.activation(out=rstd[:], in_=pS[:], func=AF.Sqrt, bias=EPS, scale=1.0 / D)),
                         r=[kP], w=[kR])
                    P.op("dve", (lambda e, rstd=rstd: e.reciprocal(out=rstd[:], in_=rstd[:])), r=[kR], w=[kR])
                    x = xin[slot]
                    for c in range(8):
                        P.op("dve", (lambda e, c=c, ysb=ysb, rstd=rstd: e.scalar_tensor_tensor(
                            out=ysb[:, c, :], in0=ysb[:, c, :], scalar=gcol(gsb, G_MIXPOST, c), in1=rstd[:],
                            op0=ALU.mult, op1=ALU.mult)), r=[kY + (c,), kR, "gsb"], w=[kY + (c,)])
                        P.op("dve", (lambda e, c=c, x=x, ysb=ysb: e.tensor_tensor(
                            out=x[:, c, :], in0=x[:, c, :], in1=ysb[:, c, :], op=ALU.add)),
                            r=[kY + (c,), ("xin", slot)], w=[("xin", slot)])
                    P.op("sp", (lambda e, x=x, t0=t0: e.dma_start(out=dview(x2T, 0, 8, t0, T), in_=x[:])),
                         r=[("xin", slot)], w=[("x2T", t0)], dma=("op", "st", slot))
                P.barrier()
                P.flush()

        ntl = NT_ALL if debug is None else debug.get("nt1", NT_ALL)
        tiles1 = []
        for i in range(NT_ALL - ntl, NT_ALL):
            tiles1.append((i * T, (i - NT_OWN) * T if i >= NT_OWN else None, i * T))
        ph = (debug or {}).get("phases", "all")
        last = []
        if ph == "all" or "ffn1" in ph:
            last = ffn_phase("f1", xT, tiles1, w1_in, w1_out, G_F1PRE, G_F1POST, x1T, G_MIXPRE, h2T)
        sstack = ExitStack()
        ssmW = ssm_persist(sstack) if (ph == "all" or "ssm" in ph) else None
        if ph == "all" or "inproj" in ph:
            inproj_phase(ssmW)
        if ph == "all" or "attn" in ph:
            attn_phase()
        if ph == "all" or "ssm" in ph:
            ssm_phase(ssmW)
        sstack.close()
        if ph == "all" or "outproj" in ph:
            outproj_phase()
        if ph == "all" or "ffn2" in ph:
            tiles2 = [(i * T, i * T, None) for i in range(NT_OWN)]
            last = ffn_phase("f2", x2T, tiles2, w2_in, w2_out, G_F2PRE, G_F2POST, outT, None, None)
        P.barrier()
        P.flush()
    return nc


def make_inputs(inputs):
    x = np.asarray(inputs["x"], dtype=np.float32)
    g = np.zeros((128, 48), np.float32)
    for i, k in enumerate(["ffn1_pre_g", "ffn1_post_g", "mix_pre_g", "mix_post_g", "ffn2_pre_g", "ffn2_post_g"]):
        g[:, i * 8:(i + 1) * 8] = np.asarray(inputs[k], np.float32)[0].reshape(8, 128).T
    common = {
        "gains": g,
        "ffn1_w_in": np.ascontiguousarray(inputs["ffn1_w_in"][0], dtype=np.float32),
        "ffn1_w_out": np.ascontiguousarray(inputs["ffn1_w_out"][0], dtype=np.float32),
        "ffn2_w_in": np.ascontiguousarray(inputs["ffn2_w_in"][0], dtype=np.float32),
        "ffn2_w_out": np.ascontiguousarray(inputs["ffn2_w_out"][0], dtype=np.float32),
    }
    slopes = 2.0 ** (-8.0 * np.arange(1, 9) / 8.0)
    cc = np.arange(128)[:, None]
    ii = np.arange(128)[None, :]
    atab = np.zeros((128, 24, 2, 128), np.float32)
    for h in range(8):
        for br, d in enumerate((1, 4, 16)):
            for hh in range(2):
                steps = 128 + ii - (hh * 128 + cc)
                valid = (steps >= 0) & (steps <= 128)
                atab[:, h * 3 + br, hh, :] = np.where(valid, -slopes[h] * d * steps * 8.0, -240000.0)
    btab_first = np.full((128, 24, 128), -240000.0, np.float32)
    btab_second = np.ascontiguousarray(atab[:, :, 0, :])
    common["w_mix_in"] = np.ascontiguousarray(inputs["w_mix_in"][0], dtype=np.float32)
    common["ident"] = np.eye(128, dtype=np.float32)
    f32 = np.float32
    a_re = np.asarray(inputs["a_re"], f32)[0]
    a_im = np.asarray(inputs["a_im"], f32)[0]
    ldt = np.asarray(inputs["log_dt"], f32)[0]
    ssm_a = np.zeros((128, 96), f32)
    ssm_a[:, 0:32] = np.tile(a_re.T, (2, 1))
    ssm_a[:, 32:64] = np.tile(a_im.T, (2, 1))
    ssm_a[:, 64:96] = np.tile(ldt[None, :], (128, 1))
    b_re = np.asarray(inputs["b_re"], f32)[0].transpose(1, 0, 2)
    b_im = np.asarray(inputs["b_im"], f32)[0].transpose(1, 0, 2)
    ssm_b = np.stack([np.concatenate([b_re, b_im], 0), np.concatenate([b_im, b_re], 0)], axis=1)
    c_re = np.asarray(inputs["c_re"], f32)[0].transpose(2, 0, 1)
    c_im = np.asarray(inputs["c_im"], f32)[0].transpose(2, 0, 1)
    ssm_c = np.stack([np.concatenate([c_re, c_im], 0), np.concatenate([c_im, c_re], 0)], axis=1)
    selm = np.zeros((128, 8, 8, 128), f32)
    for gm in range(8):
        for s_ in range(8):
            for c_ in range(16):
                selm[16 * gm + c_, gm, s_, 16 * s_ + c_] = 1.0
    selTm = np.ascontiguousarray(selm.transpose(3, 1, 2, 0))
    ss_, cc_ = np.arange(128) // 16, np.arange(128) % 16
    cmask = (ss_[None, :] >= ss_[:, None]).astype(f32)
    dsk = np.asarray(inputs["d_skip"], f32)[0]
    dstk = np.zeros((128, 32), f32)
    for g_ in range(32):
        dstk[:, g_] = dsk[16 * g_ + cc_]
    common.update({"sel": selm.reshape(128, -1), "selT": selTm.reshape(128, -1), "cmask": cmask, "dstk": dstk})
    ssm_v = np.zeros((128, 16), f32)
    ssm_v[:, 0:4] = np.asarray(inputs["d_skip"], f32)[0].reshape(4, 128).T
    ssm_v[:, 4:8] = np.asarray(inputs["b_glu"], f32)[0].reshape(4, 128).T
    ssm_v[:64, 8] = 1.0
    ssm_v[64:, 8] = -1.0
    ssm_v[:, 9] = -ssm_v[:, 8]
    rmask = np.zeros((128, 8), f32)
    for gm in range(8):
        rmask[16 * gm:16 * gm + 16, gm] = 1.0
    swapm = np.zeros((128, 128), f32)
    swapm[np.arange(128), (np.arange(128) + 64) % 128] = 1.0
    common.update({"ssm_a": ssm_a, "ssm_b": np.ascontiguousarray(ssm_b.reshape(128, -1)),
                   "ssm_c": np.ascontiguousarray(ssm_c.reshape(128, -1)), "ssm_v": ssm_v, "rmask": rmask, "swapm": swapm,
                   "w_glu": np.ascontiguousarray(inputs["w_glu"][0], dtype=f32)})
    common["w_mix_out"] = np.ascontiguousarray(inputs["w_mix_out"][0], dtype=np.float32)
    common["atab"] = atab.reshape(128, -1)
    in_maps = []
    for c in range(NCORES):
        b, hf = c // 2, c % 2
        xt = np.zeros((D, SEQ), np.float32)
        if hf == 0:
            xt[:, HALF:] = x[b, :HALF].T
        else:
            xt[:, :] = x[b].T
        m = dict(common)
        m["xT"] = xt
        m["btab"] = (btab_first if hf == 0 else btab_second).reshape(128, -1)
        in_maps.append(m)
    return in_maps


def kernel(**inputs):
    nc = build()
    in_maps = make_inputs(inputs)
    res = run_bass_kernel_spmd(nc, in_maps, core_ids=list(range(NCORES)))
    out = np.zeros((4, SEQ, D), np.float32)
    for c in range(NCORES):
        b, hf = c // 2, c % 2
        out[b, hf * HALF:(hf + 1) * HALF] = res.results[c]["outT"].T
    return out
```

```python
import numpy as np
import ml_dtypes
from contextlib import ExitStack
import concourse.bass as bass
import concourse.mybir as mybir
from concourse.bass_utils import run_bass_kernel_spmd

F32 = mybir.dt.float32
BF16 = mybir.dt.bfloat16
AF = mybir.ActivationFunctionType
ALU = mybir.AluOpType

D = 1024
DFF = 2816
NF = DFF // 128
SEQ = 8192
HALF = 4096
T = 512
NT_ALL = SEQ // T
NT_OWN = HALF // T
EPS = 1e-6
NCORES = 8


class Op:
    __slots__ = ("eng", "fn", "is_dma", "waits", "signaled", "idx", "count", "sem", "val", "is_nop")


class Prog:
    ENGS = ("pe", "act", "dve", "pool", "sp")

    def __init__(self, nc, stack):
        self.nc = nc
        self.stack = stack
        self.streams = {e: [] for e in self.ENGS}
        self.nops = {e: 0 for e in self.ENGS}
        self.nsig = {e: 0 for e in self.ENGS}
        self.esem = {e: stack.enter_context(nc.semaphore("sem_" + e)) for e in ("pe", "act", "dve", "pool")}
        self.dsem = {}
        self.dval = {}
        self.writers = {}
        self.readers = {}
        self.waited = {e: {} for e in self.ENGS}
        self.last = {}
        self.last_dma = {}
        self.defer = None
        self.deferred = []

    def _dma_sem(self, key):
        if key not in self.dsem:
            self.dsem[key] = self.stack.enter_context(self.nc.semaphore("dsem_%d" % len(self.dsem)))
            self.dval[key] = 0
        return self.dsem[key]

    def _dep(self, op, dep):
        if dep is op:
            return
        if dep.is_dma:
            tk = ("d", id(dep.sem))
            if self.waited[op.eng].get(tk, 0) >= dep.val:
                return
            self.waited[op.eng][tk] = dep.val
            op.waits.append(dep)
        else:
            if dep.eng == "pe" and op.eng == "pe" and not op.is_dma:
                return
            tk = ("e", dep.eng)
            if self.waited[op.eng].get(tk, -1) >= dep.idx:
                return
            self.waited[op.eng][tk] = dep.idx
            if dep.count is None:
                dep.signaled = True
            op.waits.append(dep)

    def op(self, eng, fn, r=(), w=(), dma=None):
        if self.defer is not None:
            self.defer.append((eng, fn, list(r), list(w), dma))
            return None
        o = Op()
        o.eng = eng
        o.fn = fn
        o.is_dma = dma is not None
        o.waits = []
        o.signaled = False
        o.idx = self.nops[eng]
        self.nops[eng] += 1
        o.count = None
        o.is_nop = False
        if o.is_dma:
            o.sem = self._dma_sem(dma)
            self.dval[dma] += 16
            o.val = self.dval[dma]
        for k in r:
            for d in self.writers.get(k, {}).values():
                self._dep(o, d)
        for k in w:
            for d in self.writers.get(k, {}).values():
                self._dep(o, d)
            for d in self.readers.get(k, {}).values():
                self._dep(o, d)
        tag = ("d", id(o.sem)) if o.is_dma else eng
        for k in r:
            self.readers.setdefault(k, {})[tag] = o
        for k in w:
            self.writers[k] = {tag: o}
            self.readers[k] = {}
        self.streams[eng].append(o)
        if o.is_dma:
            self.last_dma[id(o.sem)] = o
        else:
            self.last[eng] = o
        return o

    def replay(self, k):
        for _ in range(k):
            if not self.deferred:
                return
            eng, fn, r, w, dma = self.deferred.pop(0)
            self.op(eng, fn, r=r, w=w, dma=dma)

    def barrier(self):
        deps = [d for d in self.last.values() if not d.is_nop and d.eng != "sp"] + list(self.last_dma.values())
        for x in self.ENGS:
            o = Op()
            o.eng = x
            o.fn = lambda e: e.nop()
            o.is_dma = False
            o.is_nop = True
            o.waits = []
            o.signaled = False
            o.idx = self.nops[x]
            self.nops[x] += 1
            o.count = None
            for d in deps:
                self._dep(o, d)
            self.streams[x].append(o)

    def simulate(self):
        pos = {e: 0 for e in self.ENGS}
        done = set()
        progress = True
        while progress:
            progress = False
            for e in self.ENGS:
                st = self.streams[e]
                while pos[e] < len(st):
                    o = st[pos[e]]
                    if all((id(d) in done) or (d.count is not None and not d.is_dma and d not in self._cur) or
                           (d.is_dma and d not in self._cur) for d in o.waits):
                        done.add(id(o))
                        pos[e] += 1
                        progress = True
                    else:
                        break
        for e in self.ENGS:
            if pos[e] < len(self.streams[e]):
                o = self.streams[e][pos[e]]
                raise RuntimeError("deadlock: engine %s stuck at op %d/%d waiting on %s" % (
                    e, pos[e], len(self.streams[e]), [(d.eng, d.idx, d.is_dma) for d in o.waits if id(d) not in done]))

    def flush(self):
        nc = self.nc
        self._cur = set()
        for e in self.ENGS:
            self._cur.update(self.streams[e])
        self.simulate()
        for e in ("pe", "act", "dve", "pool"):
            c = self.nsig[e]
            pend = []
            comp = [o for o in self.streams[e] if not o.is_dma and not o.is_nop]
            if comp:
                comp[-1].signaled = True
            for o in self.streams[e]:
                if o.is_dma:
                    continue
                pend.append(o)
                if o.signaled:
                    c += 1
                    for p in pend:
                        p.count = c
                    pend = []
            self.nsig[e] = c
        streams = self.streams
        esem = self.esem

        def emit(eng_name, e):
            for o in streams[eng_name]:
                for d in o.waits:
                    if d.is_dma:
                        e.wait_ge(d.sem, d.val)
                    else:
                        assert d.count is not None
                        e.wait_ge(esem[d.eng], d.count)
                ins = o.fn(e)
                if o.is_nop:
                    continue
                if o.is_dma:
                    ins.then_inc(o.sem, 16)
                elif o.signaled:
                    ins.then_inc(esem[eng_name], 1)

        with nc.Block() as block:
            @block.tensor
            def _(e):
                emit("pe", e)

            @block.scalar
            def _(e):
                emit("act", e)

            @block.vector
            def _(e):
                emit("dve", e)

            @block.gpsimd
            def _(e):
                emit("pool", e)

            @block.sync
            def _(e):
                emit("sp", e)
        self.streams = {e: [] for e in self.ENGS}

    def final_wait(self, eng, ops):
        o = self.op(eng, lambda e: e.nop(), r=(), w=())
        o.is_nop = True
        for d in ops:
            self._dep(o, d)
        return o


def dview(t, c0, nchunks, t0, ntok):
    return t[c0 * 128:(c0 + nchunks) * 128, t0:t0 + ntok].rearrange("(c p) t -> p c t", p=128)


def build(debug=None):
    nc = bass.Bass("TRN2", target_bir_lowering=False)
    dt_ = nc.dram_tensor
    xT = dt_("xT", [D, SEQ], F32, kind="ExternalInput").ap()
    gains = dt_("gains", [128, 48], F32, kind="ExternalInput").ap()
    w1_in = dt_("ffn1_w_in", [D, 2 * DFF], F32, kind="ExternalInput").ap()
    w1_out = dt_("ffn1_w_out", [DFF, D], F32, kind="ExternalInput").ap()
    w2_in = dt_("ffn2_w_in", [D, 2 * DFF], F32, kind="ExternalInput").ap()
    w2_out = dt_("ffn2_w_out", [DFF, D], F32, kind="ExternalInput").ap()
    outT = dt_("outT", [D, HALF], F32, kind="ExternalOutput").ap()
    dbg_kind = "ExternalOutput" if debug else "Internal"
    w_mix_in = dt_("w_mix_in", [D, 2048], F32, kind="ExternalInput").ap()
    ident_d = dt_("ident", [128, 128], F32, kind="ExternalInput").ap()
    atab_d = dt_("atab", [128, 24 * 256], F32, kind="ExternalInput").ap()
    btab_d = dt_("btab", [128, 24 * 128], F32, kind="ExternalInput").ap()
    x1T = dt_("x1T", [D, HALF], F32, kind=dbg_kind).ap()
    qkvuT = dt_("qkvuT", [2048, SEQ], BF16, kind=dbg_kind).ap()
    catT = dt_("catT", [D, HALF], BF16, kind=dbg_kind).ap()
    x2T = dt_("x2T", [D, HALF], F32, kind=dbg_kind).ap()
    ssm_a_d = dt_("ssm_a", [128, 96], F32, kind="ExternalInput").ap()
    ssm_b_d = dt_("ssm_b", [128, 2 * 32 * 16], F32, kind="ExternalInput").ap()
    ssm_c_d = dt_("ssm_c", [128, 2 * 32 * 16], F32, kind="ExternalInput").ap()
    ssm_v_d = dt_("ssm_v", [128, 16], F32, kind="ExternalInput").ap()
    rmask_d = dt_("rmask", [128, 8], F32, kind="ExternalInput").ap()
    swap_d = dt_("swapm", [128, 128], F32, kind="ExternalInput").ap()
    w_glu = dt_("w_glu", [512, 512], F32, kind="ExternalInput").ap()
    sel_d = dt_("sel", [128, 64 * 128], F32, kind="ExternalInput").ap()
    selT_d = dt_("selT", [128, 64 * 128], F32, kind="ExternalInput").ap()
    cmask_d = dt_("cmask", [128, 128], F32, kind="ExternalInput").ap()
    dstk_d = dt_("dstk", [128, 32], F32, kind="ExternalInput").ap()
    w_mix_out = dt_("w_mix_out", [D, D], F32, kind="ExternalInput").ap()
    h2T = dt_("h2T", [D, SEQ], BF16, kind=("ExternalOutput" if debug else "Internal")).ap()

    with ExitStack() as gstack:
        P = Prog(nc, gstack)
        A = nc.alloc_sbuf_tensor
        ones = A("ones", [128, 128], BF16)
        gsb = A("gsb", [128, 48], F32)
        ghalf = A("ghalf", [128, 48], F32)
        P.op("pool", lambda e: e.memset(ones[:], 1.0), w=["ones"])
        P.op("sp", lambda e: e.dma_start(out=gsb[:], in_=gains), w=["gsb"], dma="c0")
        P.op("dve", lambda e: e.tensor_scalar(out=ghalf[:], in0=gsb[:], scalar1=0.5, scalar2=None, op0=ALU.mult),
             r=["gsb"], w=["ghalf"])
        G_F1PRE, G_F1POST, G_MIXPRE, G_MIXPOST, G_F2PRE, G_F2POST = range(6)

        def gcol(tile_, gi, c):
            return tile_[:, gi * 8 + c:gi * 8 + c + 1]

        def ffn_phase(name, src, tiles, w_in, w_out, g_pre, g_post, store_x, next_g, store_h):
            with ExitStack() as st:
                def S(nm, shape, dt):
                    return st.enter_context(nc.sbuf_tensor(name + nm, shape, dt))

                def PS(nm, shape):
                    return st.enter_context(nc.psum_tensor(name + nm, shape, F32))
                win = S("win", [128, 8, 2 * DFF], BF16)
                wout = S("wout", [128, NF, D], BF16)
                XA = S("xa", [128, 8, T], F32)
                hT = S("hT", [128, 8, T], BF16)
                act = S("act", [128, NF, T], BF16)
                sg = [S("sg%d" % i, [128, T], BF16) for i in range(2)]
                ysb = S("ysb", [128, 8, T], F32)
                sqj = [S("sq%d" % i, [128, T], BF16) for i in range(5)]
                rsA = S("rsA", [128, T], F32)
                rsB = S("rsB", [128, T], F32)
                h2c = [S("h2c%d" % i, [128, 1, T], BF16) for i in range(2)]
                mhalf = S("mhalf", [128, 1], F32)
                P.op("pool", lambda e: e.memset(mhalf[:], -0.5), w=["mhalf"])
                pG = [PS("pG%d" % i, [128, T]) for i in range(2)]
                pU = [PS("pU%d" % i, [128, T]) for i in range(2)]
                pY = [PS("pY%d" % i, [128, T]) for i in range(2)]
                pS0 = PS("pS0", [128, T])
                pS1 = PS("pS1", [128, T])

                fblocks = (2, 6, 7, 7)
                fstart = [sum(fblocks[:b]) for b in range(len(fblocks))]
                blk_of = [b for b, nb_ in enumerate(fblocks) for _ in range(nb_)]
                win_v = w_in.rearrange("(k p) f -> p k f", p=128)
                for b in range(len(fblocks)):
                    for half in range(2):
                        c0 = half * DFF + fstart[b] * 128
                        cw = fblocks[b] * 128
                        P.op("pool", (lambda e, c0=c0, cw=cw: e.dma_start(out=win[:, :, c0:c0 + cw], in_=win_v[:, :, c0:c0 + cw])),
                             w=[(name, "win", half, b)], dma=(name, "win", half, b))
                wout_v = w_out.rearrange("(f p) d -> p f d", p=128)
                for b in range(2):
                    P.op("pool", (lambda e, b=b: e.dma_start(out=wout[:, b * 11:(b + 1) * 11, :], in_=wout_v[:, b * 11:(b + 1) * 11, :])),
                         w=[(name, "wout", b)], dma=(name, "wout", b))
                nsq = [0]

                def stat_sq(src_ap, srckeys, ring="A", idx=None):
                    if ring == "A":
                        k = nsq[0] % 3
                        nsq[0] += 1
                        buf, key = sqj[k], ("sqj", k)
                    else:
                        buf, key = sqj[3 + idx % 2], ("sqj", 3 + idx % 2)
                    P.op("dve", (lambda e: e.tensor_tensor(out=buf[:], in0=src_ap, in1=src_ap, op=ALU.mult)),
                         r=srckeys, w=[key])
                    return (buf, key)

                def stat_mm(pS, pskey, bk, c):
                    buf, key = bk
                    P.op("pe", (lambda e: e.matmul(pS[:], lhsT=ones[:], rhs=buf[:], start=(c == 0), stop=(c == 7))),
                         r=[key, "ones"], w=[pskey])

                def rstd_step(pS, pskey, rs, rskey):
                    P.op("act", lambda e: e.activation(out=rs[:], in_=pS[:], func=AF.Sqrt, bias=EPS, scale=1.0 / D), r=[pskey], w=[rskey])
                    P.op("dve", lambda e: e.reciprocal(out=rs[:], in_=rs[:]), r=[rskey], w=[rskey])

                def stat_steps(src_fn, keys_fn, pS, pskey, lag):
                    st_ = []
                    ks = {}
                    for c in range(8 + lag):
                        def f_(c=c):
                            if c - lag >= 0:
                                stat_mm(pS, pskey, ks[c - lag], c - lag)
                            if c < 8:
                                ks[c] = stat_sq(src_fn(c), keys_fn(c))
                        st_.append(f_)
                    return st_

                def load_x(i):
                    s0 = tiles[i][0]
                    P.op("sp", (lambda e: e.dma_start(out=XA[:], in_=dview(src, 0, 8, s0, T))),
                         r=[("x2T", s0)] if name == "f2" else [], w=["xa"], dma=(name, "x"))

                def steps_N(i):
                    st_ = stat_steps(lambda c: XA[:, c, :], lambda c: ["xa"], pS0, "pS0", 2)
                    st_.append(lambda: rstd_step(pS0, "pS0", rsA, "rsA"))
                    for c in range(8):
                        st_.append(lambda c=c: P.op("dve", (lambda e: e.scalar_tensor_tensor(
                            out=hT[:, c, :], in0=XA[:, c, :], scalar=gcol(gsb, g_pre, c), in1=rsA[:],
                            op0=ALU.mult, op1=ALU.mult)), r=["xa", "rsA", "gsb"], w=[("hT", c)]))
                    return st_

                pend = {}

                def steps_R(i):
                    s0, d0, h0 = tiles[i]
                    st_ = []
                    nop_ = lambda: None
                    st_.append(lambda: pend.__setitem__(7, stat_sq(ysb[:, 7, :], [("ysb", 7)], "B", 7)))
                    st_.append(nop_)
                    st_.append(lambda: (stat_mm(pS1, "pS1", pend[6], 6), stat_mm(pS1, "pS1", pend[7], 7)))
                    st_.append(lambda: rstd_step(pS1, "pS1", rsB, "rsB"))
                    for c in range(8):
                        st_.append(lambda c=c: P.op("dve", (lambda e: e.scalar_tensor_tensor(
                            out=ysb[:, c, :], in0=ysb[:, c, :], scalar=gcol(ghalf, g_post, c), in1=rsB[:],
                            op0=ALU.mult, op1=ALU.mult)), r=[("ysb", c), "rsB", "ghalf"], w=[("ysb", c)]))
                    allk = [("ysb", c) for c in range(8)]
                    st_.append(lambda: P.op("pool", (lambda e: e.dma_start(out=ysb[:], in_=dview(src, 0, 8, s0, T), accum_op=ALU.add)),
                                            r=allk, w=allk, dma=(name, "xacc")))
                    if d0 is not None:
                        st_.append(lambda: P.op("pool", (lambda e: e.dma_start(out=dview(store_x, 0, 8, d0, T), in_=ysb[:])),
                                                r=allk, w=[(name, "dst", d0)], dma=(name, "st")))
                    if next_g is not None and h0 is not None:
                        st_ += [nop_] * 12
                        st_ += stat_steps(lambda c: ysb[:, c, :], lambda c: [("ysb", c)], pS0, "pS0", 2)
                        st_.append(lambda: rstd_step(pS0, "pS0", rsB, "rsB"))
                        for c in range(8):
                            def f_(c=c):
                                sl_ = c % 2
                                P.op("dve", (lambda e: e.scalar_tensor_tensor(
                                    out=h2c[sl_][:, 0, :], in0=ysb[:, c, :], scalar=gcol(gsb, next_g, c), in1=rsB[:],
                                    op0=ALU.mult, op1=ALU.mult)), r=[("ysb", c), "rsB", "gsb"], w=[("h2c", sl_)])
                                P.op("sp", (lambda e: e.dma_start(out=dview(store_h, c, 1, h0, T), in_=h2c[sl_][:])),
                                     r=[("h2c", sl_)], w=[("h2T", h0)], dma=(name, "sth", sl_))
                            st_.append(f_)
                    return st_

                def run_some(lst, k):
                    for _ in range(k):
                        if lst:
                            lst.pop(0)()

                n = len(tiles)
                load_x(0)
                run_some(steps_N(0), 99)
                for i in range(n):
                    if i + 1 < n:
                        load_x(i + 1)
                    side = steps_R(i - 1) if i >= 1 else []
                    per = 2 if side else 0
                    for f in range(NF):
                        pb = f % 2
                        blk = blk_of[f]
                        wk = [(name, "win", 0, blk), (name, "win", 1, blk)]
                        for c in range(8):
                            P.op("pe", (lambda e, c=c, f=f, pb=pb: e.matmul(
                                pG[pb][:], lhsT=win[:, c, f * 128:(f + 1) * 128], rhs=hT[:, c, :],
                                start=(c == 0), stop=(c == 7))), r=[("hT", c)] + wk, w=[("pG", pb)])
                        for c in range(8):
                            P.op("pe", (lambda e, c=c, f=f, pb=pb: e.matmul(
                                pU[pb][:], lhsT=win[:, c, DFF + f * 128:DFF + (f + 1) * 128], rhs=hT[:, c, :],
                                start=(c == 0), stop=(c == 7))), r=[("hT", c)] + wk, w=[("pU", pb)])
                        P.op("act", (lambda e, pb=pb: e.activation(out=sg[pb][:], in_=pG[pb][:], func=AF.Silu)),
                             r=[("pG", pb)], w=[("sg", pb)])
                        P.op("dve", (lambda e, pb=pb, f=f: e.tensor_tensor(
                            out=act[:, f, :], in0=sg[pb][:], in1=pU[pb][:], op=ALU.mult)),
                            r=[("sg", pb), ("pU", pb)], w=[("act", f)])
                        if f >= 1:
                            run_some(side, per)
                    run_some(side, 99)
                    side = steps_N(i + 1) if i + 1 < n else []
                    per = -(-len(side) // 7) if side else 0
                    for j in range(8):
                        pb = j % 2
                        for f in range(NF):
                            P.op("pe", (lambda e, j=j, f=f, pb=pb: e.matmul(
                                pY[pb][:], lhsT=wout[:, f, j * 128:(j + 1) * 128], rhs=act[:, f, :],
                                start=(f == 0), stop=(f == NF - 1))),
                                r=[("act", f), (name, "wout", f // 11)], w=[("pY", pb)])
                        P.op("act", (lambda e, j=j, pb=pb: e.activation(out=ysb[:, j, :], in_=pY[pb][:], func=AF.Copy)),
                             r=[("pY", pb)], w=[("ysb", j)])
                        if j >= 2:
                            stat_mm(pS1, "pS1", pend[j - 2], j - 2)
                        if j >= 1:
                            pend[j - 1] = stat_sq(ysb[:, j - 1, :], [("ysb", j - 1)], "B", j - 1)
                        if j >= 1:
                            run_some(side, per)
                    run_some(side, 99)
                run_some(steps_R(n - 1), 99)
                P.barrier()
                P.flush()

        def inproj_phase(ssmW=None):
            name = "ip"
            with ExitStack() as st:
                def S(nm, shape, dt):
                    return st.enter_context(nc.sbuf_tensor(name + nm, shape, dt))

                def PS(nm, shape):
                    return st.enter_context(nc.psum_tensor(name + nm, shape, F32))
                wmi = S("w", [128, 8, 2048], BF16)
                hin = [S("h%d" % i, [128, 8, T], BF16) for i in range(2)]
                stg = [S("stg%d" % i, [128, 16, T], BF16) for i in range(2)]
                pp = [PS("p%d" % i, [128, T]) for i in range(4)]
                psF = PS("psF", [128, T])
                if ssmW is not None:
                    P.defer = []
                    ssm_gen(ssmW, st, psF)
                    P.deferred, P.defer = P.defer, None
                    per_rep = -(-len(P.deferred) // 150)
                wv = w_mix_in.rearrange("(k p) f -> p k f", p=128)
                for b in (3, 1, 2, 0):
                    P.op("pool", (lambda e, b=b: e.dma_start(out=wmi[:, :, b * 512:(b + 1) * 512], in_=wv[:, :, b * 512:(b + 1) * 512])),
                         w=[("wmi", b)], dma=("ip", "w", b))
                n = 0
                for ti in range(NT_ALL):
                    slot = ti % 2
                    P.op("sp", (lambda e, slot=slot, ti=ti: e.dma_start(out=hin[slot][:], in_=dview(h2T, 0, 8, ti * T, T))),
                         r=[("h2T", ti * T)], w=[("hin", slot)], dma=("ip", "h", slot))
                    c_lo = 0 if ti >= NT_OWN else (4 if ti >= 4 else 12)
                    for cc in range(c_lo, 16):
                        pb = n % 4
                        for k in range(8):
                            P.op("pe", (lambda e, k=k, cc=cc, pb=pb, slot=slot: e.matmul(
                                pp[pb][:], lhsT=wmi[:, k, cc * 128:(cc + 1) * 128], rhs=hin[slot][:, k, :],
                                start=(k == 0), stop=(k == 7))), r=[("hin", slot), ("wmi", cc // 4)], w=[("ipp", pb)])
                        if cc >= 12:
                            o_ap = stg[slot][:, cc, :].rearrange("p (s j) -> p s j", s=8)
                            i_ap = pp[pb][:].rearrange("p (j s) -> p s j", s=8)
                        else:
                            o_ap = stg[slot][:, cc, :]
                            i_ap = pp[pb][:]
                        if n % 2 == 0:
                            P.op("act", (lambda e, o_ap=o_ap, i_ap=i_ap: e.activation(out=o_ap, in_=i_ap, func=AF.Copy)),
                                 r=[("ipp", pb)], w=[("stg", slot, cc)])
                        else:
                            P.op("dve", (lambda e, o_ap=o_ap, i_ap=i_ap: e.tensor_copy(out=o_ap, in_=i_ap)),
                                 r=[("ipp", pb)], w=[("stg", slot, cc)])
                        n += 1
                        if ssmW is not None:
                            P.replay(per_rep)
                    P.op("act", (lambda e, slot=slot, ti=ti, c_lo=c_lo: e.dma_start(
                        out=dview(qkvuT, c_lo, 16 - c_lo, ti * T, T), in_=stg[slot][:, c_lo:16, :])),
                        r=[("stg", slot, cc) for cc in range(c_lo, 16)], w=[("qkvu", ti)], dma=("ip", "st", slot))
                P.replay(1 << 30)
                P.barrier()
                P.flush()

        def attn_phase():
            name = "at"
            KW = SEQ - 2048
            with ExitStack() as st:
                def S(nm, shape, dt):
                    return st.enter_context(nc.sbuf_tensor(name + nm, shape, dt))
                ident = S("ident", [128, 128], BF16)
                atab = S("atab", [128, 24, 2, 128], F32)
                btab = S("btab", [128, 24, 128], F32)
                qT = S("qT", [128, HALF], BF16)
                kT = S("kT", [128, KW], BF16)
                vT = S("vT", [128, KW], BF16)
                vtoks = [S("vtok%d" % i, [128, 48, 2, 128], BF16) for i in range(2)]
                acc = S("acc", [128, 2, HALF], F32)
                sb = [S("sb%d" % i, [128, 4, 128], F32) for i in range(3)]
                pT = [S("pT%d" % i, [128, 4, 128], BF16) for i in range(3)]
                rd = S("rd", [128, T], F32)
                ao = S("ao", [128, HALF], BF16)
                psS = [st.enter_context(nc.psum_tensor(name + "s%d" % i, [128, 4, 128], F32)) for i in range(3)]
                psO = [st.enter_context(nc.psum_tensor(name + "o%d" % i, [128, 512], F32)) for i in range(2)]
                psT = [st.enter_context(nc.psum_tensor(name + "t%d" % i, [128, 8, 128], BF16)) for i in range(2)]

                P.op("pool", lambda e: e.dma_start(out=ident[:], in_=ident_d), w=["ident"], dma=("at", "c"))
                P.op("sp", lambda e: e.dma_start(out=atab[:], in_=atab_d.rearrange("p (a b c) -> p a b c", a=24, b=2)), w=["atab"], dma=("at", "c1"))
                P.op("sp", lambda e: e.dma_start(out=btab[:], in_=btab_d.rearrange("p (a c) -> p a c", a=24)), w=["btab"], dma=("at", "c2"))
                for vt_ in vtoks:
                    P.op("pool", (lambda e, vt_=vt_: e.memset(vt_[:, :, 0, 64:128], 1.0)), w=["vones"])
                    P.op("pool", (lambda e, vt_=vt_: e.memset(vt_[:, :, 1, 0:64], 1.0)), w=["vones"])
                allq = [("qkvu", ti) for ti in range(NT_ALL)]
                nq = 0
                for hp in range(4):
                    P.op("sp", (lambda e, hp=hp: e.dma_start(out=qT[:], in_=qkvuT[hp * 128:(hp + 1) * 128, HALF:SEQ])),
                         r=allq, w=["qT"], dma=("at", "q"))
                    P.op("sp", (lambda e, hp=hp: e.dma_start(out=kT[:], in_=qkvuT[512 + hp * 128:512 + (hp + 1) * 128, 2048:SEQ])),
                         r=allq, w=["kT"], dma=("at", "k"))
                    P.op("sp", (lambda e, hp=hp: e.dma_start(out=vT[:], in_=qkvuT[1024 + hp * 128:1024 + (hp + 1) * 128, 2048:SEQ])),
                         r=allq, w=["vT"], dma=("at", "v"))
                    def vbuild_steps(br_, hp=hp):
                        d_ = (1, 4, 16)[br_]
                        nblk_ = 48 // d_
                        vb_ = (hp * 3 + br_) % 2
                        vt_ = vtoks[vb_]
                        steps_ = []
                        for g4 in range(12):
                            def f_(g4=g4):
                                tb = g4 % 2
                                for j in range(4):
                                    blk = g4 * 4 + j
                                    r_, n_ = blk // nblk_, blk % nblk_
                                    s0 = r_ + d_ * 128 * n_
                                    P.op("pe", (lambda e, j=j, s0=s0: e.transpose(
                                        psT[tb][:, j, :], vT[:, s0:s0 + 127 * d_ + 1:d_], ident[:])),
                                        r=["vT", "ident"], w=[("psT", tb)])
                                P.op("act", (lambda e: e.activation(
                                    out=vt_[:, g4 * 4:g4 * 4 + 4, 0, 0:64], in_=psT[tb][:, 0:4, 0:64], func=AF.Copy)),
                                    r=[("psT", tb)], w=[("vtok", vb_, g4, 0)])
                                P.op("act", (lambda e: e.activation(
                                    out=vt_[:, g4 * 4:g4 * 4 + 4, 1, 64:128], in_=psT[tb][:, 0:4, 64:128], func=AF.Copy)),
                                    r=[("psT", tb)], w=[("vtok", vb_, g4, 1)])
                            steps_.append(f_)
                        return steps_
                    for f_ in vbuild_steps(0):
                        f_()
                    for br, d in enumerate((1, 4, 16)):
                        nblk = 48 // d
                        n0 = 16 // d
                        vbi = (hp * 3 + br) % 2
                        vtok = vtoks[vbi]
                        vnext = vbuild_steps(br + 1) if br < 2 else []
                        pairs = [(h2, r_, n_) for h2 in range(2) for r_ in range(d) for n_ in range(n0, nblk, 2)]
                        LAG = 2

                        def stage_a(idx, h2, r_, n_, d=d, br=br, hp=hp, nblk=nblk, n0=n0):
                            rows = slice(h2 * 64, h2 * 64 + 64)
                            tix = (hp * 2 + h2) * 3 + br
                            sbi = idx % 3
                            for b in range(2):
                                kb = r_ + d * 128 * (n_ + b - 1)
                                qb = r_ + d * 128 * (n_ + b) - 2048
                                for hh in range(2):
                                    k0 = kb + hh * 128 * d
                                    P.op("pe", (lambda e, hh=hh, k0=k0, qb=qb, b=b: e.matmul(
                                        psS[sbi][:, 2 * b + hh, :], lhsT=kT[rows, k0:k0 + 127 * d + 1:d], rhs=qT[rows, qb:qb + 127 * d + 1:d],
                                        start=True, stop=True)), r=["kT", "qT"], w=[("psS", sbi)])
                            if n_ == n0:
                                P.op("dve", (lambda e: e.tensor_tensor(
                                    out=sb[sbi][:, 0, :], in0=psS[sbi][:, 0, :], in1=btab[:, tix, :], op=ALU.add)),
                                    r=[("psS", sbi), "btab"], w=[("sb", sbi)])
                                P.op("dve", (lambda e: e.tensor_tensor(
                                    out=sb[sbi][:, 1, :], in0=psS[sbi][:, 1, :], in1=atab[:, tix, 1, :], op=ALU.add)),
                                    r=[("psS", sbi), "atab"], w=[("sb", sbi)])
                                P.op("dve", (lambda e: e.tensor_tensor(
                                    out=sb[sbi][:, 2:4, :], in0=psS[sbi][:, 2:4, :], in1=atab[:, tix, :, :], op=ALU.add)),
                                    r=[("psS", sbi), "atab"], w=[("sb", sbi)])
                            else:
                                tb2 = atab[:, tix, :, :].rearrange("p a b -> p (a b)").unsqueeze(1).to_broadcast([128, 2, 256])
                                P.op("dve", (lambda e: e.tensor_tensor(
                                    out=sb[sbi][:].rearrange("p (x a) b -> p x (a b)", x=2),
                                    in0=psS[sbi][:].rearrange("p (x a) b -> p x (a b)", x=2), in1=tb2, op=ALU.add)),
                                    r=[("psS", sbi), "atab"], w=[("sb", sbi)])
                            P.op("act", (lambda e: e.activation(out=pT[sbi][:], in_=sb[sbi][:], func=AF.Exp, scale=0.125)),
                                 r=[("sb", sbi)], w=[("pT", sbi)])

                        def stage_b(idx, h2, r_, n_, d=d, br=br, hp=hp, nblk=nblk, n0=n0, vtok=vtok, vbi=vbi):
                            sbi = idx % 3
                            ob = idx % 2
                            for b in range(2):
                                blk = r_ * nblk + n_ + b
                                for hh in range(2):
                                    vb = blk - 1 + hh
                                    P.op("pe", (lambda e, hh=hh, vb=vb, b=b: e.matmul(
                                        psO[ob][:, b * 128:(b + 1) * 128], lhsT=vtok[:, vb, h2, :], rhs=pT[sbi][:, 2 * b + hh, :],
                                        start=(hh == 0), stop=(hh == 1))),
                                        r=[("pT", sbi), ("vtok", vbi, vb // 4, h2), "vones"], w=[("psO", ob)])
                            qb = r_ + d * 128 * n_ - 2048
                            asl = acc[:, h2, qb:qb + 255 * d + 1:d]
                            if br == 0:
                                P.op("act", (lambda e: e.activation(out=asl, in_=psO[ob][:, 0:256], func=AF.Copy)),
                                     r=[("psO", ob)], w=[("acc", h2)])
                            else:
                                P.op("dve", (lambda e: e.tensor_tensor(out=asl, in0=asl, in1=psO[ob][:, 0:256], op=ALU.add)),
                                     r=[("psO", ob), ("acc", h2)], w=[("acc", h2)])
                        for idx in range(len(pairs) + LAG):
                            if idx < len(pairs):
                                stage_a(idx, *pairs[idx])
                            if idx - LAG >= 0:
                                stage_b(idx - LAG, *pairs[idx - LAG])
                            if idx % 2 == 1 and vnext:
                                vnext.pop(0)()
                        while vnext:
                            vnext.pop(0)()
                    for tt in range(NT_OWN):
                        ts_ = slice(tt * T, (tt + 1) * T)
                        P.op("dve", (lambda e, ts_=ts_: e.reciprocal(out=rd[0:64, :], in_=acc[64:128, 0, ts_])),
                             r=[("acc", 0), "ao"], w=["rd0"])
                        P.op("dve", (lambda e, ts_=ts_: e.tensor_tensor(out=ao[0:64, ts_], in0=acc[0:64, 0, ts_], in1=rd[0:64, :], op=ALU.mult)),
                             r=[("acc", 0), "rd0"], w=["ao"])
                        P.op("dve", (lambda e, ts_=ts_: e.reciprocal(out=rd[64:128, :], in_=acc[0:64, 1, ts_])),
                             r=[("acc", 1), "ao"], w=["rd1"])
                        P.op("dve", (lambda e, ts_=ts_: e.tensor_tensor(out=ao[64:128, ts_], in0=acc[64:128, 1, ts_], in1=rd[64:128, :], op=ALU.mult)),
                             r=[("acc", 1), "rd1"], w=["ao"])
                    P.op("act", (lambda e, hp=hp: e.dma_start(out=catT[hp * 128:(hp + 1) * 128, :], in_=ao[:])),
                         r=["ao"], w=[("cat", hp)], dma=("at", "st"))
                P.barrier()
                P.flush()

        NI = 10
        PI_ = float(np.pi)
        NJC = SEQ // 8
        GK = "ssgen"

        def ssm_persist(st):
            def S(nm, shape, dt):
                return st.enter_context(nc.sbuf_tensor("s2" + nm, shape, dt))
            sv = S("sv", [128, 16], F32)
            swp = S("swp", [128, 128], F32)
            idf = S("idf", [128, 128], F32)
            idb = S("idb", [128, 128], BF16)
            wglu = S("wglu", [128, 4, 512], BF16)
            Bdec = S("Bdec", [128, 32, 128], BF16)
            Cdec = S("Cdec", [128, 32, 128], BF16)
            Toep = S("Toep", [128, 32, 128], BF16)
            PR8 = S("PR8", [128, NI, 32], F32)
            PI8 = S("PI8", [128, NI, 32], F32)
            SPI8 = S("SPI8", [128, NI, 32], F32)

            def ld(dst, src_, key, eng="sp"):
                P.op(eng, (lambda e: e.dma_start(out=dst, in_=src_)), w=[key], dma=("s2", key))
            ld(sv[:], ssm_v_d, "sv")
            ld(swp[:], swap_d, "swp")
            ld(idf[:], ident_d, "idf")
            ld(idb[:], ident_d, "idb", "pool")
            ld(wglu[:], w_glu.rearrange("(k p) f -> p k f", p=128), "wglu", "pool")
            return (sv, swp, idf, idb, wglu, Bdec, Cdec, Toep, PR8, PI8, SPI8)

        def ssm_gen(W, st2, psF):
            name = "s2"
            sv, swp, idf, idb, wglu, Bdec, Cdec, Toep, PR8, PI8, SPI8 = W
            sgnA, sgnB = sv[:, 8:9], sv[:, 9:10]

            def ld(dst, src_, key, eng="sp"):
                P.op(eng, (lambda e: e.dma_start(out=dst, in_=src_)), w=[key], dma=("s2", key))

            def dve(fn, r=(GK,), w=(GK,)):
                P.op("dve", fn, r=list(r), w=list(w))

            def actf(fn, r=(GK,), w=(GK,)):
                P.op("act", fn, r=list(r), w=list(w))
            if True:
                if True:
                    def S2(nm, shape, dt):
                        return st2.enter_context(nc.sbuf_tensor(name + nm, shape, dt))
                    prm = S2("prm", [128, 96], F32)
                    bab = S2("bab", [128, 2, 32, 16], F32)
                    cab = S2("cab", [128, 2, 32, 16], F32)
                    cmask = S2("cmask", [128, 128], F32)
                    dstk = S2("dstk", [128, 32], F32)
                    tmpv = [S2("tv%d" % i, [128, 32], F32) for i in range(12)]
                    POWr = S2("POWr", [128, 16, 32], F32)
                    POWi = S2("POWi", [128, 16, 32], F32)
                    Qr = S2("Qr", [128, 32, 8], F32)
                    Qi = S2("Qi", [128, 32, 8], F32)
                    Q2r = S2("Q2r", [128, 32, 8], F32)
                    Q2i = S2("Q2i", [128, 32, 8], F32)
                    BdT = S2("BdT", [128, 32, 8, 16], F32)
                    big = S2("big", [128, 32, 9, 16], F32)
                    VV = S2("VV", [128, 32, 9, 16], F32)
                    rt = S2("rt", [128, 128], F32)
                    ld(prm[:], ssm_a_d, "prm")
                    ld(bab[:], ssm_b_d.rearrange("p (a g c) -> p a g c", a=2, g=32), "bab")
                    ld(cab[:], ssm_c_d.rearrange("p (a g c) -> p a g c", a=2, g=32), "cab")
                    ld(cmask[:], cmask_d, "cmask")
                    ld(dstk[:], dstk_d, "dstk")
                    are, aim, ldt = prm[:, 0:32], prm[:, 32:64], prm[:, 64:96]
                    dt_, lr, li, mag, angs, angc, m_, t1, t2, t3, wr, wi = [t[:] for t in tmpv]
                    actf(lambda e: e.activation(out=dt_, in_=ldt, func=AF.Exp), r=("prm", GK))
                    dve(lambda e: e.tensor_tensor(out=lr, in0=dt_, in1=are, op=ALU.mult), r=("prm", GK))
                    dve(lambda e: e.tensor_tensor(out=li, in0=dt_, in1=aim, op=ALU.mult))
                    actf(lambda e: e.activation(out=mag, in_=lr, func=AF.Exp))
                    dve(lambda e: e.tensor_copy(out=angs, in_=li))
                    dve(lambda e: e.tensor_scalar(out=angc, in0=li, scalar1=PI_ / 2, scalar2=None, op0=ALU.add))
                    for it in range(4):
                        for ang in (angs, angc):
                            dve(lambda e, ang=ang: e.tensor_scalar(out=m_, in0=ang, scalar1=PI_, scalar2=2 * PI_, op0=ALU.is_gt, op1=ALU.mult))
                            dve(lambda e, ang=ang: e.tensor_tensor(out=ang, in0=ang, in1=m_, op=ALU.subtract))
                    actf(lambda e: e.activation(out=angs, in_=angs, func=AF.Sin))
                    actf(lambda e: e.activation(out=angc, in_=angc, func=AF.Sin))
                    K0 = 7

                    def pw(k):
                        return POWr[:, K0 + k, :], POWi[:, K0 + k, :]
                    dve(lambda e: e.memset(POWr[:, K0, :], 1.0))
                    dve(lambda e: e.memset(POWi[:, K0, :], 0.0))
                    dve(lambda e: e.tensor_tensor(out=pw(1)[0], in0=mag, in1=angc, op=ALU.mult))
                    dve(lambda e: e.tensor_tensor(out=pw(1)[1], in0=mag, in1=angs, op=ALU.mult))

                    def cmul(zr, zi, xr, xi, yr, yi):
                        dve(lambda e: e.tensor_tensor(out=t1, in0=xr, in1=yr, op=ALU.mult))
                        dve(lambda e: e.tensor_tensor(out=t2, in0=xi, in1=yi, op=ALU.mult))
                        dve(lambda e: e.tensor_tensor(out=zr, in0=t1, in1=t2, op=ALU.subtract))
                        dve(lambda e: e.tensor_tensor(out=t1, in0=xr, in1=yi, op=ALU.mult))
                        dve(lambda e: e.tensor_tensor(out=t2, in0=xi, in1=yr, op=ALU.mult))
                        dve(lambda e: e.tensor_tensor(out=zi, in0=t1, in1=t2, op=ALU.add))
                    for k in range(2, 9):
                        cmul(*pw(k), *pw(k - 1), *pw(1))
                    dve(lambda e: e.tensor_tensor(out=t1, in0=pw(1)[0], in1=pw(1)[0], op=ALU.mult))
                    dve(lambda e: e.tensor_tensor(out=t2, in0=pw(1)[1], in1=pw(1)[1], op=ALU.mult))
                    dve(lambda e: e.tensor_tensor(out=t3, in0=t1, in1=t2, op=ALU.add))
                    dve(lambda e: e.reciprocal(out=t3, in_=t3))
                    dve(lambda e: e.tensor_tensor(out=pw(-1)[0], in0=pw(1)[0], in1=t3, op=ALU.mult))
                    dve(lambda e: e.scalar_tensor_tensor(out=pw(-1)[1], in0=pw(1)[1], scalar=-1.0, in1=t3, op0=ALU.mult, op1=ALU.mult))
                    for k in range(-2, -8, -1):
                        cmul(*pw(k), *pw(k + 1), *pw(-1))
                    dve(lambda e: e.tensor_scalar(out=m_, in0=pw(1)[0], scalar1=-1.0, scalar2=None, op0=ALU.add))
                    dve(lambda e: e.tensor_tensor(out=t1, in0=are, in1=are, op=ALU.mult))
                    dve(lambda e: e.tensor_tensor(out=t2, in0=aim, in1=aim, op=ALU.mult))
                    dve(lambda e: e.tensor_tensor(out=t3, in0=t1, in1=t2, op=ALU.add))
                    dve(lambda e: e.reciprocal(out=t3, in_=t3))
                    dve(lambda e: e.tensor_tensor(out=wr, in0=m_, in1=are, op=ALU.mult))
                    dve(lambda e: e.tensor_tensor(out=t1, in0=pw(1)[1], in1=aim, op=ALU.mult))
                    dve(lambda e: e.tensor_tensor(out=wr, in0=wr, in1=t1, op=ALU.add))
                    dve(lambda e: e.tensor_tensor(out=wr, in0=wr, in1=t3, op=ALU.mult))
                    dve(lambda e: e.tensor_tensor(out=wi, in0=pw(1)[1], in1=are, op=ALU.mult))
                    dve(lambda e: e.tensor_tensor(out=t1, in0=m_, in1=aim, op=ALU.mult))
                    dve(lambda e: e.tensor_tensor(out=wi, in0=wi, in1=t1, op=ALU.subtract))
                    dve(lambda e: e.tensor_tensor(out=wi, in0=wi, in1=t3, op=ALU.mult))
                    for s_ in range(8):
                        cmul(Qr[:, :, s_], Qi[:, :, s_], *pw(7 - s_), wr, wi)
                        cmul(Q2r[:, :, s_], Q2i[:, :, s_], *pw(-s_), wr, wi)
                    SH = [128, 32, 8, 16]
                    BAb = bab[:, 0, :, :].unsqueeze(2).to_broadcast(SH)
                    BBb = bab[:, 1, :, :].unsqueeze(2).to_broadcast(SH)
                    def bq(dst, qr_, qi_):
                        qrb = qr_[:].unsqueeze(3).to_broadcast(SH)
                        qib = qi_[:].unsqueeze(3).to_broadcast(SH)
                        dve(lambda e: e.tensor_tensor(out=big[:, :, 0:8, :], in0=BBb, in1=qib, op=ALU.mult), r=("bab", GK))
                        dve(lambda e: e.tensor_scalar(out=big[:, :, 0:8, :], in0=big[:, :, 0:8, :], scalar1=sgnB, scalar2=None, op0=ALU.mult), r=("sv", GK))
                        dve(lambda e: e.tensor_tensor(out=dst[:], in0=BAb, in1=qrb, op=ALU.mult), r=("bab", GK))
                        dve(lambda e: e.tensor_tensor(out=dst[:], in0=dst[:], in1=big[:, :, 0:8, :], op=ALU.add))
                    bq(BdT, Qr, Qi)
                    for g in range(32):
                        P.op("pe", (lambda e, g=g: e.transpose(psF[:, 0:128], BdT[:, g, :, :].rearrange("p s c -> p (s c)"), idf[:])),
                             r=[GK, "idf"], w=["psF"])
                        P.op("act", (lambda e, g=g: e.activation(out=Bdec[:, g, :], in_=psF[:, 0:128], func=AF.Copy)),
                             r=["psF"], w=["Bdec"])
                    UTp = BdT
                    bq(UTp, Q2r, Q2i)
                    SH9 = [128, 32, 9, 16]
                    CAb = cab[:, 0, :, :].unsqueeze(2).to_broadcast(SH9)
                    CBb = cab[:, 1, :, :].unsqueeze(2).to_broadcast(SH9)
                    prb = POWr[:, K0:K0 + 9, :].rearrange("p k g -> p g k").unsqueeze(3).to_broadcast(SH9)
                    pib = POWi[:, K0:K0 + 9, :].rearrange("p k g -> p g k").unsqueeze(3).to_broadcast(SH9)
                    dve(lambda e: e.tensor_tensor(out=VV[:], in0=CAb, in1=prb, op=ALU.mult), r=("cab", GK))
                    dve(lambda e: e.tensor_scalar(out=VV[:], in0=VV[:], scalar1=sgnA, scalar2=None, op0=ALU.mult), r=("sv", GK))
                    dve(lambda e: e.tensor_tensor(out=big[:], in0=CBb, in1=pib, op=ALU.mult), r=("cab", GK))
                    dve(lambda e: e.tensor_tensor(out=VV[:], in0=VV[:], in1=big[:], op=ALU.subtract))
                    dve(lambda e: e.tensor_copy(out=Cdec[:].rearrange("p g (t c) -> p g t c", t=8), in_=VV[:, :, 1:9, :]), w=(GK, "Cdec"))
                    for g in range(32):
                        P.op("pe", (lambda e, g=g: e.matmul(psF[:, 128:256], lhsT=UTp[:, g, :, :].rearrange("p s c -> p (s c)"),
                                                            rhs=VV[:, g, 0:8, :].rearrange("p t c -> p (t c)"), start=True, stop=True)),
                             r=[GK, "Bdec"], w=["psF"])
                        P.op("dve", (lambda e, g=g: e.tensor_tensor(out=rt[:], in0=psF[:, 128:256], in1=cmask[:], op=ALU.mult)),
                             r=["psF", "cmask"], w=["rt"])
                        P.op("dve", (lambda e, g=g: e.scalar_tensor_tensor(out=Toep[:, g, :], in0=idf[:], scalar=dstk[:, g:g + 1], in1=rt[:],
                                                                           op0=ALU.mult, op1=ALU.add)),
                             r=["rt", "idf", "dstk"], w=["Toep"])
                    dve(lambda e: e.tensor_copy(out=PR8[:, 0, :], in_=pw(8)[0]))
                    dve(lambda e: e.tensor_copy(out=PI8[:, 0, :], in_=pw(8)[1]))
                    for i in range(1, NI):
                        dve(lambda e, i=i: e.tensor_tensor(out=t1, in0=PR8[:, i - 1, :], in1=PR8[:, i - 1, :], op=ALU.mult))
                        dve(lambda e, i=i: e.tensor_tensor(out=t2, in0=PI8[:, i - 1, :], in1=PI8[:, i - 1, :], op=ALU.mult))
                        dve(lambda e, i=i: e.tensor_tensor(out=PR8[:, i, :], in0=t1, in1=t2, op=ALU.subtract))
                        dve(lambda e, i=i: e.scalar_tensor_tensor(out=PI8[:, i, :], in0=PR8[:, i - 1, :], scalar=2.0, in1=PI8[:, i - 1, :],
                                                                  op0=ALU.mult, op1=ALU.mult))
                    dve(lambda e: e.tensor_scalar(out=SPI8[:], in0=PI8[:], scalar1=sgnA, scalar2=None, op0=ALU.mult), r=("sv", GK))

        def ssm_phase(W):
            name = "s2"
            sv, swp, idf, idb, wglu, Bdec, Cdec, Toep, PR8, PI8, SPI8 = W
            with ExitStack() as st:
                def S(nm, shape, dt):
                    return st.enter_context(nc.sbuf_tensor(name + nm, shape, dt))

                def PS(nm, shape, dt=F32):
                    return st.enter_context(nc.psum_tensor(name + nm, shape, dt))
                sel = S("sel", [128, 64, 128], BF16)
                selT = S("selT", [128, 64, 128], BF16)
                psA = [PS("a%d" % i, [128, T]) for i in range(4)]
                psY = [PS("y%d" % i, [128, T]) for i in range(2)]
                P.op("pool", (lambda e: e.dma_start(out=sel[:], in_=sel_d.rearrange("p (a b) -> p a b", a=64))), w=["sel"], dma=("s2", "sel"))
                P.op("pool", (lambda e: e.dma_start(out=selT[:], in_=selT_d.rearrange("p (a b) -> p a b", a=64))), w=["selT"], dma=("s2", "selT"))
                Rq2 = [S("Rq%d" % i, [128, 8, NI, 128], BF16) for i in range(2)]
                rtmp = [S("rtmp%d" % i, [128, 128], F32) for i in range(2)]
                uT = S("uT", [128, SEQ], BF16)
                U1 = [S("U1%d" % i, [128, NJC], BF16) for i in range(4)]
                Hs = [[S("Hs%d_%d" % (a_, i), [128, NJC], BF16) for i in range(2)] for a_ in range(4)]
                g1 = [S("g1%d" % i, [128, T], F32) for i in range(2)]
                Yg = S("Yg", [128, 8, T], BF16)
                yg = S("yg", [128, 4, HALF], BF16)
                so = [S("so%d" % i, [128, 4, T], BF16) for i in range(2)]
                sgm = [S("sgm%d" % i, [128, T], F32) for i in range(2)]
                allq = [("qkvu", ti) for ti in range(NT_ALL)]
                nev = 0
                nrt = 0
                ngr = 0

                def evac(pb, dst, keys_w):
                    if pb % 2 == 0:
                        P.op("act", (lambda e: e.activation(out=dst, in_=psA[pb][:], func=AF.Copy)), r=[("psA", pb)], w=keys_w)
                    else:
                        P.op("dve", (lambda e: e.tensor_copy(out=dst, in_=psA[pb][:])), r=[("psA", pb)], w=keys_w)
                def gen_R(q):
                    nonlocal nrt
                    Rq_ = Rq2[q % 2]
                    for gm in range(8):
                        g = 8 * q + gm
                        for i in range(NI):
                            rb = nrt % 2
                            nrt += 1
                            P.op("dve", (lambda e, rb=rb, i=i, g=g: e.tensor_scalar(
                                out=rtmp[rb][:], in0=swp[:], scalar1=SPI8[:, i, g:g + 1], scalar2=None, op0=ALU.mult)),
                                r=[GK, "swp"], w=[("rtmp", rb)])
                            P.op("dve", (lambda e, rb=rb, i=i, g=g, gm=gm, Rq_=Rq_: e.scalar_tensor_tensor(
                                out=Rq_[:, gm, i, :], in0=idf[:], scalar=PR8[:, i, g:g + 1], in1=rtmp[rb][:],
                                op0=ALU.mult, op1=ALU.add)), r=[GK, "idf", ("rtmp", rb)], w=[("R", q % 2, gm)])
                gen_R(0)
                for q in range(4):
                    Rq = Rq2[q % 2]
                    P.op("sp", (lambda e, q=q: e.dma_start(out=uT[:], in_=qkvuT[1536 + q * 128:1536 + (q + 1) * 128, :])),
                         r=allq, w=["uT"], dma=("s2", "u"))
                    for pr_ in range(2):
                        if pr_ == 1 and q + 1 < 4:
                            gen_R(q + 1)
                        gms = tuple(range(4 * pr_, 4 * pr_ + 4))
                        for ab, gm in enumerate(gms):
                            for hf in range(2):
                                pb = nev % 4
                                nev += 1
                                for s_ in range(8):
                                    P.op("pe", (lambda e, pb=pb, gm=gm, s_=s_, hf=hf: e.matmul(
                                        psA[pb][:], lhsT=sel[:, gm * 8 + s_, :],
                                        rhs=uT[:].rearrange("p (t s j) -> p t s j", s=8, j=64)[:, hf * 8:(hf + 1) * 8, s_, :],
                                        start=(s_ == 0), stop=(s_ == 7))), r=["uT", "sel"], w=[("psA", pb)])
                                evac(pb, U1[ab][:, hf * T:(hf + 1) * T], [("U1", ab, hf)])
                        cur = 0
                        for ab, gm in enumerate(gms):
                            g = 8 * q + gm
                            for hf in range(2):
                                pb = nev % 4
                                nev += 1
                                P.op("pe", (lambda e, pb=pb, g=g, ab=ab, hf=hf: e.matmul(
                                    psA[pb][:], lhsT=Bdec[:, g, :], rhs=U1[ab][:, hf * T:(hf + 1) * T], start=True, stop=True)),
                                    r=[("U1", ab, hf), "Bdec"], w=[("psA", pb)])
                                evac(pb, Hs[ab][cur][:, hf * T:(hf + 1) * T], [("Hs", ab, cur, hf)])
                        for i in range(NI):
                            dd = 1 << i
                            nxt = 1 - cur
                            for tt in range(2):
                                for ab, gm in enumerate(gms):
                                    c0 = tt * T
                                    lo = max(0, dd - c0)
                                    has = lo < T
                                    pb = nev % 4
                                    nev += 1
                                    P.op("pe", (lambda e, pb=pb, c0=c0, cur=cur, has=has, ab=ab: e.matmul(
                                        psA[pb][:], lhsT=idb[:], rhs=Hs[ab][cur][:, c0:c0 + T], start=True, stop=not has)),
                                        r=[("Hs", ab, cur, tt), "idb"], w=[("psA", pb)])
                                    if has:
                                        s_lo = c0 + lo - dd
                                        s_hi = c0 + T - dd
                                        rk = [("Hs", ab, cur, s_lo // T), ("Hs", ab, cur, (s_hi - 1) // T), ("R", q % 2, gm)]
                                        P.op("pe", (lambda e, pb=pb, lo=lo, s_lo=s_lo, s_hi=s_hi, cur=cur, gm=gm, i=i, ab=ab, Rq=Rq: e.matmul(
                                            psA[pb][:, lo:T], lhsT=Rq[:, gm, i, :], rhs=Hs[ab][cur][:, s_lo:s_hi], start=False, stop=True)),
                                            r=rk, w=[("psA", pb)])
                                    evac(pb, Hs[ab][nxt][:, c0:c0 + T], [("Hs", ab, nxt, tt)])
                            cur = nxt
                        for ab, gm in enumerate(gms):
                            g = 8 * q + gm
                            yb = ab % 2
                            P.op("pe", (lambda e, yb=yb, g=g, ab=ab: e.matmul(
                                psY[yb][:], lhsT=Toep[:, g, :], rhs=U1[ab][:, T:2 * T], start=True, stop=False)),
                                r=[("U1", ab, 1), "Toep"], w=[("psY", yb)])
                            P.op("pe", (lambda e, yb=yb, g=g, cur=cur, ab=ab: e.matmul(
                                psY[yb][:], lhsT=Cdec[:, g, :], rhs=Hs[ab][cur][:, T - 1:2 * T - 1], start=False, stop=True)),
                                r=[("Hs", ab, cur, 0), ("Hs", ab, cur, 1), "Cdec"], w=[("psY", yb)])
                            gk = ("g1", yb)
                            gg = g1[yb]
                            P.op("act", (lambda e, yb=yb, gg=gg: e.activation(out=gg[:], in_=psY[yb][:], func=AF.Square)), r=[("psY", yb)], w=[gk])
                            P.op("dve", (lambda e, gg=gg: e.tensor_scalar(out=gg[:], in0=gg[:], scalar1=0.044715, scalar2=1.0, op0=ALU.mult, op1=ALU.add)),
                                 r=[gk], w=[gk])
                            P.op("dve", (lambda e, yb=yb, gg=gg: e.tensor_tensor(out=gg[:], in0=gg[:], in1=psY[yb][:], op=ALU.mult)),
                                 r=[gk, ("psY", yb)], w=[gk])
                            P.op("act", (lambda e, gg=gg: e.activation(out=gg[:], in_=gg[:], func=AF.Sigmoid, scale=1.5957691216057308)), r=[gk], w=[gk])
                            P.op("dve", (lambda e, yb=yb, gg=gg, gm=gm: e.tensor_tensor(out=Yg[:, gm, :], in0=gg[:], in1=psY[yb][:], op=ALU.mult)),
                                 r=[gk, ("psY", yb)], w=[("Yg", gm)])
                    for t_ in range(8):
                        pb = nev % 4
                        nev += 1
                        for gm in range(8):
                            P.op("pe", (lambda e, pb=pb, gm=gm, t_=t_: e.matmul(
                                psA[pb][:], lhsT=selT[:, gm * 8 + t_, :], rhs=Yg[:, gm, :], start=(gm == 0), stop=(gm == 7))),
                                r=[("Yg", gm), "selT"], w=[("psA", pb)])
                        evac(pb, yg[:, q, t_:t_ + 8 * 511 + 1:8], [("yg", q)])
                for t8 in range(NT_OWN):
                    ts_ = slice(t8 * T, (t8 + 1) * T)
                    sob = so[t8 % 2]
                    for jo in range(4):
                        yb = jo % 2
                        sg_ = sgm[jo % 2]
                        for k in range(4):
                            P.op("pe", (lambda e, yb=yb, k=k, jo=jo, ts_=ts_: e.matmul(
                                psY[yb][:], lhsT=wglu[:, k, jo * 128:(jo + 1) * 128], rhs=yg[:, k, ts_], start=(k == 0), stop=(k == 3))),
                                r=[("yg", k), "wglu"], w=[("psY", yb)])
                        P.op("act", (lambda e, yb=yb, jo=jo, sg_=sg_: e.activation(out=sg_[:], in_=psY[yb][:], func=AF.Sigmoid, bias=sv[:, 4 + jo:5 + jo])),
                             r=[("psY", yb), "sv"], w=[("sgm", jo % 2)])
                        P.op("dve", (lambda e, jo=jo, ts_=ts_, sg_=sg_, sob=sob: e.tensor_tensor(out=sob[:, jo, :], in0=yg[:, jo, ts_], in1=sg_[:], op=ALU.mult)),
                             r=[("sgm", jo % 2), ("yg", jo)], w=[("so", t8 % 2)])
                    P.op("sp", (lambda e, t8=t8, sob=sob: e.dma_start(out=dview(catT, 4, 4, t8 * T, T), in_=sob[:])),
                         r=[("so", t8 % 2)], w=[("cat", 4 + t8 % 4)], dma=("s2", "st", t8 % 2))
                P.barrier()
                P.flush()


        def outproj_phase():
            name = "op"
            with ExitStack() as st:
                def S(nm, shape, dt):
                    return st.enter_context(nc.sbuf_tensor(name + nm, shape, dt))

                def PS(nm, shape):
                    return st.enter_context(nc.psum_tensor(name + nm, shape, F32))
                wmo = S("w", [128, 8, D], BF16)
                cin = [S("c%d" % i, [128, 8, T], BF16) for i in range(2)]
                xin = [S("x%d" % i, [128, 8, T], F32) for i in range(2)]
                ysb2 = [S("ysb%d" % i, [128, 8, T], F32) for i in range(2)]
                sq2 = [S("sq%d" % i, [128, 8, T], BF16) for i in range(2)]
                rstd2 = [S("rstd%d" % i, [128, T], F32) for i in range(2)]
                pY = [PS("pY%d" % i, [128, T]) for i in range(4)]
                pS2 = [PS("pS%d" % i, [128, T]) for i in range(2)]
                wv = w_mix_out.rearrange("(k p) f -> p k f", p=128)
                for b in range(2):
                    P.op("pool", (lambda e, b=b: e.dma_start(out=wmo[:, b * 4:(b + 1) * 4, :], in_=wv[:, b * 4:(b + 1) * 4, :])),
                         w=[("wmo", b)], dma=("op", "w", b))
                allcat = [("cat", i) for i in range(8)]
                for ti in range(NT_OWN):
                    slot = ti % 2
                    t0 = ti * T
                    ysb, sq, rstd, pS = ysb2[slot], sq2[slot], rstd2[slot], pS2[slot]
                    kY, kQ, kR, kP = ("opysb", slot), ("opsq", slot), ("oprstd", slot), ("oppS", slot)
                    def op_loads(tj):
                        sl_, tq = tj % 2, tj * T
                        P.op("sp", (lambda e: e.dma_start(out=cin[sl_][:], in_=dview(catT, 0, 8, tq, T))),
                             r=allcat, w=[("cin", sl_)], dma=("op", "c", sl_))
                        P.op("sp", (lambda e: e.dma_start(out=xin[sl_][:], in_=dview(x1T, 0, 8, tq, T))),
                             r=[("f1", "dst", tq)], w=[("xin", sl_)], dma=("op", "x", sl_))
                    if ti == 0:
                        op_loads(0)
                    if ti + 1 < NT_OWN:
                        op_loads(ti + 1)
                    for j in range(8):
                        pb = (ti * 8 + j) % 4
                        for k in range(8):
                            P.op("pe", (lambda e, j=j, k=k, pb=pb, slot=slot: e.matmul(
                                pY[pb][:], lhsT=wmo[:, k, j * 128:(j + 1) * 128], rhs=cin[slot][:, k, :],
                                start=(k == 0), stop=(k == 7))), r=[("cin", slot), ("wmo", k // 4)], w=[("opY", pb)])
                        P.op("act", (lambda e, j=j, pb=pb, ysb=ysb: e.activation(out=ysb[:, j, :], in_=pY[pb][:], func=AF.Copy)),
                             r=[("opY", pb)], w=[kY + (j,)])
                        P.op("dve", (lambda e, j=j, ysb=ysb, sq=sq: e.tensor_tensor(out=sq[:, j, :], in0=ysb[:, j, :], in1=ysb[:, j, :], op=ALU.mult)),
                             r=[kY + (j,)], w=[kQ + (j,)])
                    for c in range(8):
                        P.op("pe", (lambda e, c=c, sq=sq, pS=pS: e.matmul(pS[:], lhsT=ones[:], rhs=sq[:, c, :], start=(c == 0), stop=(c == 7))),
                             r=[kQ + (c,), "ones"], w=[kP])
                    P.op("act", (lambda e, rstd=rstd, pS=pS: e.activation(out=rstd[:], in_=pS[:], func=AF.Sqrt, bias=EPS, scale=1.0 / D)),
                         r=[kP], w=[kR])
                    P.op("dve", (lambda e, rstd=rstd: e.reciprocal(out=rstd[:], in_=rstd[:])), r=[kR], w=[kR])
                    x = xin[slot]
                    for c in range(8):
                        P.op("dve", (lambda e, c=c, ysb=ysb, rstd=rstd: e.scalar_tensor_tensor(
                            out=ysb[:, c, :], in0=ysb[:, c, :], scalar=gcol(gsb, G_MIXPOST, c), in1=rstd[:],
                            op0=ALU.mult, op1=ALU.mult)), r=[kY + (c,), kR, "gsb"], w=[kY + (c,)])
                        P.op("dve", (lambda e, c=c, x=x, ysb=ysb: e.tensor_tensor(
                            out=x[:, c, :], in0=x[:, c, :], in1=ysb[:, c, :], op=ALU.add)),
                            r=[kY + (c,), ("xin", slot)], w=[("xin", slot)])
                    P.op("sp", (lambda e, x=x, t0=t0: e.dma_start(out=dview(x2T, 0, 8, t0, T), in_=x[:])),
                         r=[("xin", slot)], w=[("x2T", t0)], dma=("op", "st", slot))
                P.barrier()
                P.flush()

        ntl = NT_ALL if debug is None else debug.get("nt1", NT_ALL)
        tiles1 = []
        for i in range(NT_ALL - ntl, NT_ALL):
            tiles1.append((i * T, (i - NT_OWN) * T if i >= NT_OWN else None, i * T))
        ph = (debug or {}).get("phases", "all")
        last = []
        if ph == "all" or "ffn1" in ph:
            last = ffn_phase("f1", xT, tiles1, w1_in, w1_out, G_F1PRE, G_F1POST, x1T, G_MIXPRE, h2T)
        sstack = ExitStack()
        ssmW = ssm_persist(sstack) if (ph == "all" or "ssm" in ph) else None
        if ph == "all" or "inproj" in ph:
            inproj_phase(ssmW)
        if ph == "all" or "attn" in ph:
            attn_phase()
        if ph == "all" or "ssm" in ph:
            ssm_phase(ssmW)
        sstack.close()
        if ph == "all" or "outproj" in ph:
            outproj_phase()
        if ph == "all" or "ffn2" in ph:
            tiles2 = [(i * T, i * T, None) for i in range(NT_OWN)]
            last = ffn_phase("f2", x2T, tiles2, w2_in, w2_out, G_F2PRE, G_F2POST, outT, None, None)
        P.barrier()
        P.flush()
    return nc


def make_inputs(inputs):
    x = np.asarray(inputs["x"], dtype=np.float32)
    g = np.zeros((128, 48), np.float32)
    for i, k in enumerate(["ffn1_pre_g", "ffn1_post_g", "mix_pre_g", "mix_post_g", "ffn2_pre_g", "ffn2_post_g"]):
        g[:, i * 8:(i + 1) * 8] = np.asarray(inputs[k], np.float32)[0].reshape(8, 128).T
    common = {
        "gains": g,
        "ffn1_w_in": np.ascontiguousarray(inputs["ffn1_w_in"][0], dtype=np.float32),
        "ffn1_w_out": np.ascontiguousarray(inputs["ffn1_w_out"][0], dtype=np.float32),
        "ffn2_w_in": np.ascontiguousarray(inputs["ffn2_w_in"][0], dtype=np.float32),
        "ffn2_w_out": np.ascontiguousarray(inputs["ffn2_w_out"][0], dtype=np.float32),
    }
    slopes = 2.0 ** (-8.0 * np.arange(1, 9) / 8.0)
    cc = np.arange(128)[:, None]
    ii = np.arange(128)[None, :]
    atab = np.zeros((128, 24, 2, 128), np.float32)
    for h in range(8):
        for br, d in enumerate((1, 4, 16)):
            for hh in range(2):
                steps = 128 + ii - (hh * 128 + cc)
                valid = (steps >= 0) & (steps <= 128)
                atab[:, h * 3 + br, hh, :] = np.where(valid, -slopes[h] * d * steps * 8.0, -240000.0)
    btab_first = np.full((128, 24, 128), -240000.0, np.float32)
    btab_second = np.ascontiguousarray(atab[:, :, 0, :])
    common["w_mix_in"] = np.ascontiguousarray(inputs["w_mix_in"][0], dtype=np.float32)
    common["ident"] = np.eye(128, dtype=np.float32)
    f32 = np.float32
    a_re = np.asarray(inputs["a_re"], f32)[0]
    a_im = np.asarray(inputs["a_im"], f32)[0]
    ldt = np.asarray(inputs["log_dt"], f32)[0]
    ssm_a = np.zeros((128, 96), f32)
    ssm_a[:, 0:32] = np.tile(a_re.T, (2, 1))
    ssm_a[:, 32:64] = np.tile(a_im.T, (2, 1))
    ssm_a[:, 64:96] = np.tile(ldt[None, :], (128, 1))
    b_re = np.asarray(inputs["b_re"], f32)[0].transpose(1, 0, 2)
    b_im = np.asarray(inputs["b_im"], f32)[0].transpose(1, 0, 2)
    ssm_b = np.stack([np.concatenate([b_re, b_im], 0), np.concatenate([b_im, b_re], 0)], axis=1)
    c_re = np.asarray(inputs["c_re"], f32)[0].transpose(2, 0, 1)
    c_im = np.asarray(inputs["c_im"], f32)[0].transpose(2, 0, 1)
    ssm_c = np.stack([np.concatenate([c_re, c_im], 0), np.concatenate([c_im, c_re], 0)], axis=1)
    selm = np.zeros((128, 8, 8, 128), f32)
    for gm in range(8):
        for s_ in range(8):
            for c_ in range(16):
                selm[16 * gm + c_, gm, s_, 16 * s_ + c_] = 1.0
    selTm = np.ascontiguousarray(selm.transpose(3, 1, 2, 0))
    ss_, cc_ = np.arange(128) // 16, np.arange(128) % 16
    cmask = (ss_[None, :] >= ss_[:, None]).astype(f32)
    dsk = np.asarray(inputs["d_skip"], f32)[0]
    dstk = np.zeros((128, 32), f32)
    for g_ in range(32):
        dstk[:, g_] = dsk[16 * g_ + cc_]
    common.update({"sel": selm.reshape(128, -1), "selT": selTm.reshape(128, -1), "cmask": cmask, "dstk": dstk})
    ssm_v = np.zeros((128, 16), f32)
    ssm_v[:, 0:4] = np.asarray(inputs["d_skip"], f32)[0].reshape(4, 128).T
    ssm_v[:, 4:8] = np.asarray(inputs["b_glu"], f32)[0].reshape(4, 128).T
    ssm_v[:64, 8] = 1.0
    ssm_v[64:, 8] = -1.0
    ssm_v[:, 9] = -ssm_v[:, 8]
    rmask = np.zeros((128, 8), f32)
    for gm in range(8):
        rmask[16 * gm:16 * gm + 16, gm] = 1.0
    swapm = np.zeros((128, 128), f32)
    swapm[np.arange(128), (np.arange(128) + 64) % 128] = 1.0
    common.update({"ssm_a": ssm_a, "ssm_b": np.ascontiguousarray(ssm_b.reshape(128, -1)),
                   "ssm_c": np.ascontiguousarray(ssm_c.reshape(128, -1)), "ssm_v": ssm_v, "rmask": rmask, "swapm": swapm,
                   "w_glu": np.ascontiguousarray(inputs["w_glu"][0], dtype=f32)})
    common["w_mix_out"] = np.ascontiguousarray(inputs["w_mix_out"][0], dtype=np.float32)
    common["atab"] = atab.reshape(128, -1)
    in_maps = []
    for c in range(NCORES):
        b, hf = c // 2, c % 2
        xt = np.zeros((D, SEQ), np.float32)
        if hf == 0:
            xt[:, HALF:] = x[b, :HALF].T
        else:
            xt[:, :] = x[b].T
        m = dict(common)
        m["xT"] = xt
        m["btab"] = (btab_first if hf == 0 else btab_second).reshape(128, -1)
        in_maps.append(m)
    return in_maps


def kernel(**inputs):
    nc = build()
    in_maps = make_inputs(inputs)
    res = run_bass_kernel_spmd(nc, in_maps, core_ids=list(range(NCORES)))
    out = np.zeros((4, SEQ, D), np.float32)
    for c in range(NCORES):
        b, hf = c // 2, c % 2
        out[b, hf * HALF:(hf + 1) * HALF] = res.results[c]["outT"].T
    return out
```

```python
import numpy as np
import ml_dtypes
from contextlib import ExitStack
import concourse.bass as bass
import concourse.mybir as mybir
from concourse.bass_utils import run_bass_kernel_spmd

F32 = mybir.dt.float32
BF16 = mybir.dt.bfloat16
AF = mybir.ActivationFunctionType
ALU = mybir.AluOpType

D = 1024
DFF = 2816
NF = DFF // 128
SEQ = 8192
HALF = 4096
T = 512
NT_ALL = SEQ // T
NT_OWN = HALF // T
EPS = 1e-6
NCORES = 8


class Op:
    __slots__ = ("eng", "fn", "is_dma", "waits", "signaled", "idx", "count", "sem", "val", "is_nop")


class Prog:
    ENGS = ("pe", "act", "dve", "pool", "sp")

    def __init__(self, nc, stack):
        self.nc = nc
        self.stack = stack
        self.streams = {e: [] for e in self.ENGS}
        self.nops = {e: 0 for e in self.ENGS}
        self.nsig = {e: 0 for e in self.ENGS}
        self.esem = {e: stack.enter_context(nc.semaphore("sem_" + e)) for e in ("pe", "act", "dve", "pool")}
        self.dsem = {}
        self.dval = {}
        self.writers = {}
        self.readers = {}
        self.waited = {e: {} for e in self.ENGS}
        self.last = {}
        self.last_dma = {}
        self.defer = None
        self.deferred = []

    def _dma_sem(self, key):
        if key not in self.dsem:
            self.dsem[key] = self.stack.enter_context(self.nc.semaphore("dsem_%d" % len(self.dsem)))
            self.dval[key] = 0
        return self.dsem[key]

    def _dep(self, op, dep):
        if dep is op:
            return
        if dep.is_dma:
            tk = ("d", id(dep.sem))
            if self.waited[op.eng].get(tk, 0) >= dep.val:
                return
            self.waited[op.eng][tk] = dep.val
            op.waits.append(dep)
        else:
            if dep.eng == "pe" and op.eng == "pe" and not op.is_dma:
                return
            tk = ("e", dep.eng)
            if self.waited[op.eng].get(tk, -1) >= dep.idx:
                return
            self.waited[op.eng][tk] = dep.idx
            if dep.count is None:
                dep.signaled = True
            op.waits.append(dep)

    def op(self, eng, fn, r=(), w=(), dma=None):
        if self.defer is not None:
            self.defer.append((eng, fn, list(r), list(w), dma))
            return None
        o = Op()
        o.eng = eng
        o.fn = fn
        o.is_dma = dma is not None
        o.waits = []
        o.signaled = False
        o.idx = self.nops[eng]
        self.nops[eng] += 1
        o.count = None
        o.is_nop = False
        if o.is_dma:
            o.sem = self._dma_sem(dma)
            self.dval[dma] += 16
            o.val = self.dval[dma]
        for k in r:
            for d in self.writers.get(k, {}).values():
                self._dep(o, d)
        for k in w:
            for d in self.writers.get(k, {}).values():
                self._dep(o, d)
            for d in self.readers.get(k, {}).values():
                self._dep(o, d)
        tag = ("d", id(o.sem)) if o.is_dma else eng
        for k in r:
            self.readers.setdefault(k, {})[tag] = o
        for k in w:
            self.writers[k] = {tag: o}
            self.readers[k] = {}
        self.streams[eng].append(o)
        if o.is_dma:
            self.last_dma[id(o.sem)] = o
        else:
            self.last[eng] = o
        return o

    def replay(self, k):
        for _ in range(k):
            if not self.deferred:
                return
            eng, fn, r, w, dma = self.deferred.pop(0)
            self.op(eng, fn, r=r, w=w, dma=dma)

    def barrier(self):
        deps = [d for d in self.last.values() if not d.is_nop and d.eng != "sp"] + list(self.last_dma.values())
        for x in self.ENGS:
            o = Op()
            o.eng = x
            o.fn = lambda e: e.nop()
            o.is_dma = False
            o.is_nop = True
            o.waits = []
            o.signaled = False
            o.idx = self.nops[x]
            self.nops[x] += 1
            o.count = None
            for d in deps:
                self._dep(o, d)
            self.streams[x].append(o)

    def simulate(self):
        pos = {e: 0 for e in self.ENGS}
        done = set()
        progress = True
        while progress:
            progress = False
            for e in self.ENGS:
                st = self.streams[e]
                while pos[e] < len(st):
                    o = st[pos[e]]
                    if all((id(d) in done) or (d.count is not None and not d.is_dma and d not in self._cur) or
                           (d.is_dma and d not in self._cur) for d in o.waits):
                        done.add(id(o))
                        pos[e] += 1
                        progress = True
                    else:
                        break
        for e in self.ENGS:
            if pos[e] < len(self.streams[e]):
                o = self.streams[e][pos[e]]
                raise RuntimeError("deadlock: engine %s stuck at op %d/%d waiting on %s" % (
                    e, pos[e], len(self.streams[e]), [(d.eng, d.idx, d.is_dma) for d in o.waits if id(d) not in done]))

    def flush(self):
        nc = self.nc
        self._cur = set()
        for e in self.ENGS:
            self._cur.update(self.streams[e])
        self.simulate()
        for e in ("pe", "act", "dve", "pool"):
            c = self.nsig[e]
            pend = []
            comp = [o for o in self.streams[e] if not o.is_dma and not o.is_nop]
            if comp:
                comp[-1].signaled = True
            for o in self.streams[e]:
                if o.is_dma:
                    continue
                pend.append(o)
                if o.signaled:
                    c += 1
                    for p in pend:
                        p.count = c
                    pend = []
            self.nsig[e] = c
        streams = self.streams
        esem = self.esem

        def emit(eng_name, e):
            for o in streams[eng_name]:
                for d in o.waits:
                    if d.is_dma:
                        e.wait_ge(d.sem, d.val)
                    else:
                        assert d.count is not None
                        e.wait_ge(esem[d.eng], d.count)
                ins = o.fn(e)
                if o.is_nop:
                    continue
                if o.is_dma:
                    ins.then_inc(o.sem, 16)
                elif o.signaled:
                    ins.then_inc(esem[eng_name], 1)

        with nc.Block() as block:
            @block.tensor
            def _(e):
                emit("pe", e)

            @block.scalar
            def _(e):
                emit("act", e)

            @block.vector
            def _(e):
                emit("dve", e)

            @block.gpsimd
            def _(e):
                emit("pool", e)

            @block.sync
            def _(e):
                emit("sp", e)
        self.streams = {e: [] for e in self.ENGS}

    def final_wait(self, eng, ops):
        o = self.op(eng, lambda e: e.nop(), r=(), w=())
        o.is_nop = True
        for d in ops:
            self._dep(o, d)
        return o


def dview(t, c0, nchunks, t0, ntok):
    return t[c0 * 128:(c0 + nchunks) * 128, t0:t0 + ntok].rearrange("(c p) t -> p c t", p=128)


def build(debug=None):
    nc = bass.Bass("TRN2", target_bir_lowering=False)
    dt_ = nc.dram_tensor
    xT = dt_("xT", [D, SEQ], F32, kind="ExternalInput").ap()
    gains = dt_("gains", [128, 48], F32, kind="ExternalInput").ap()
    w1_in = dt_("ffn1_w_in", [D, 2 * DFF], F32, kind="ExternalInput").ap()
    w1_out = dt_("ffn1_w_out", [DFF, D], F32, kind="ExternalInput").ap()
    w2_in = dt_("ffn2_w_in", [D, 2 * DFF], F32, kind="ExternalInput").ap()
    w2_out = dt_("ffn2_w_out", [DFF, D], F32, kind="ExternalInput").ap()
    outT = dt_("outT", [D, HALF], F32, kind="ExternalOutput").ap()
    dbg_kind = "ExternalOutput" if debug else "Internal"
    w_mix_in = dt_("w_mix_in", [D, 2048], F32, kind="ExternalInput").ap()
    ident_d = dt_("ident", [128, 128], F32, kind="ExternalInput").ap()
    atab_d = dt_("atab", [128, 24 * 256], F32, kind="ExternalInput").ap()
    btab_d = dt_("btab", [128, 24 * 128], F32, kind="ExternalInput").ap()
    x1T = dt_("x1T", [D, HALF], F32, kind=dbg_kind).ap()
    qkvuT = dt_("qkvuT", [2048, SEQ], BF16, kind=dbg_kind).ap()
    catT = dt_("catT", [D, HALF], BF16, kind=dbg_kind).ap()
    x2T = dt_("x2T", [D, HALF], F32, kind=dbg_kind).ap()
    ssm_a_d = dt_("ssm_a", [128, 96], F32, kind="ExternalInput").ap()
    ssm_b_d = dt_("ssm_b", [128, 2 * 32 * 16], F32, kind="ExternalInput").ap()
    ssm_c_d = dt_("ssm_c", [128, 2 * 32 * 16], F32, kind="ExternalInput").ap()
    ssm_v_d = dt_("ssm_v", [128, 16], F32, kind="ExternalInput").ap()
    rmask_d = dt_("rmask", [128, 8], F32, kind="ExternalInput").ap()
    swap_d = dt_("swapm", [128, 128], F32, kind="ExternalInput").ap()
    w_glu = dt_("w_glu", [512, 512], F32, kind="ExternalInput").ap()
    sel_d = dt_("sel", [128, 64 * 128], F32, kind="ExternalInput").ap()
    selT_d = dt_("selT", [128, 64 * 128], F32, kind="ExternalInput").ap()
    cmask_d = dt_("cmask", [128, 128], F32, kind="ExternalInput").ap()
    dstk_d = dt_("dstk", [128, 32], F32, kind="ExternalInput").ap()
    w_mix_out = dt_("w_mix_out", [D, D], F32, kind="ExternalInput").ap()
    h2T = dt_("h2T", [D, SEQ], BF16, kind=("ExternalOutput" if debug else "Internal")).ap()

    with ExitStack() as gstack:
        P = Prog(nc, gstack)
        A = nc.alloc_sbuf_tensor
        ones = A("ones", [128, 128], BF16)
        gsb = A("gsb", [128, 48], F32)
        ghalf = A("ghalf", [128, 48], F32)
        P.op("pool", lambda e: e.memset(ones[:], 1.0), w=["ones"])
        P.op("sp", lambda e: e.dma_start(out=gsb[:], in_=gains), w=["gsb"], dma="c0")
        P.op("dve", lambda e: e.tensor_scalar(out=ghalf[:], in0=gsb[:], scalar1=0.5, scalar2=None, op0=ALU.mult),
             r=["gsb"], w=["ghalf"])
        G_F1PRE, G_F1POST, G_MIXPRE, G_MIXPOST, G_F2PRE, G_F2POST = range(6)

        def gcol(tile_, gi, c):
            return tile_[:, gi * 8 + c:gi * 8 + c + 1]

        def ffn_phase(name, src, tiles, w_in, w_out, g_pre, g_post, store_x, next_g, store_h):
            with ExitStack() as st:
                def S(nm, shape, dt):
                    return st.enter_context(nc.sbuf_tensor(name + nm, shape, dt))

                def PS(nm, shape):
                    return st.enter_context(nc.psum_tensor(name + nm, shape, F32))
                win = S("win", [128, 8, 2 * DFF], BF16)
                wout = S("wout", [128, NF, D], BF16)
                XA = S("xa", [128, 8, T], F32)
                hT = S("hT", [128, 8, T], BF16)
                act = S("act", [128, NF, T], BF16)
                sg = [S("sg%d" % i, [128, T], BF16) for i in range(2)]
                ysb = S("ysb", [128, 8, T], F32)
                sqj = [S("sq%d" % i, [128, T], BF16) for i in range(5)]
                rsA = S("rsA", [128, T], F32)
                rsB = S("rsB", [128, T], F32)
                h2c = [S("h2c%d" % i, [128, 1, T], BF16) for i in range(2)]
                mhalf = S("mhalf", [128, 1], F32)
                P.op("pool", lambda e: e.memset(mhalf[:], -0.5), w=["mhalf"])
                pG = [PS("pG%d" % i, [128, T]) for i in range(2)]
                pU = [PS("pU%d" % i, [128, T]) for i in range(2)]
                pY = [PS("pY%d" % i, [128, T]) for i in range(2)]
                pS0 = PS("pS0", [128, T])
                pS1 = PS("pS1", [128, T])

                fblocks = (2, 6, 7, 7)
                fstart = [sum(fblocks[:b]) for b in range(len(fblocks))]
                blk_of = [b for b, nb_ in enumerate(fblocks) for _ in range(nb_)]
                win_v = w_in.rearrange("(k p) f -> p k f", p=128)
                for b in range(len(fblocks)):
                    for half in range(2):
                        c0 = half * DFF + fstart[b] * 128
                        cw = fblocks[b] * 128
                        P.op("pool", (lambda e, c0=c0, cw=cw: e.dma_start(out=win[:, :, c0:c0 + cw], in_=win_v[:, :, c0:c0 + cw])),
                             w=[(name, "win", half, b)], dma=(name, "win", half, b))
                wout_v = w_out.rearrange("(f p) d -> p f d", p=128)
                for b in range(2):
                    P.op("pool", (lambda e, b=b: e.dma_start(out=wout[:, b * 11:(b + 1) * 11, :], in_=wout_v[:, b * 11:(b + 1) * 11, :])),
                         w=[(name, "wout", b)], dma=(name, "wout", b))
                nsq = [0]

                def stat_sq(src_ap, srckeys, ring="A", idx=None):
                    if ring == "A":
                        k = nsq[0] % 3
                        nsq[0] += 1
                        buf, key = sqj[k], ("sqj", k)
                    else:
                        buf, key = sqj[3 + idx % 2], ("sqj", 3 + idx % 2)
                    P.op("dve", (lambda e: e.tensor_tensor(out=buf[:], in0=src_ap, in1=src_ap, op=ALU.mult)),
                         r=srckeys, w=[key])
                    return (buf, key)

                def stat_mm(pS, pskey, bk, c):
                    buf, key = bk
                    P.op("pe", (lambda e: e.matmul(pS[:], lhsT=ones[:], rhs=buf[:], start=(c == 0), stop=(c == 7))),
                         r=[key, "ones"], w=[pskey])

                def rstd_step(pS, pskey, rs, rskey):
                    P.op("act", lambda e: e.activation(out=rs[:], in_=pS[:], func=AF.Sqrt, bias=EPS, scale=1.0 / D), r=[pskey], w=[rskey])
                    P.op("dve", lambda e: e.reciprocal(out=rs[:], in_=rs[:]), r=[rskey], w=[rskey])

                def stat_steps(src_fn, keys_fn, pS, pskey, lag):
                    st_ = []
                    ks = {}
                    for c in range(8 + lag):
                        def f_(c=c):
                            if c - lag >= 0:
                                stat_mm(pS, pskey, ks[c - lag], c - lag)
                            if c < 8:
                                ks[c] = stat_sq(src_fn(c), keys_fn(c))
                        st_.append(f_)
                    return st_

                def load_x(i):
                    s0 = tiles[i][0]
                    P.op("sp", (lambda e: e.dma_start(out=XA[:], in_=dview(src, 0, 8, s0, T))),
                         r=[("x2T", s0)] if name == "f2" else [], w=["xa"], dma=(name, "x"))

                def steps_N(i):
                    st_ = stat_steps(lambda c: XA[:, c, :], lambda c: ["xa"], pS0, "pS0", 2)
                    st_.append(lambda: rstd_step(pS0, "pS0", rsA, "rsA"))
                    for c in range(8):
                        st_.append(lambda c=c: P.op("dve", (lambda e: e.scalar_tensor_tensor(
                            out=hT[:, c, :], in0=XA[:, c, :], scalar=gcol(gsb, g_pre, c), in1=rsA[:],
                            op0=ALU.mult, op1=ALU.mult)), r=["xa", "rsA", "gsb"], w=[("hT", c)]))
                    return st_

                pend = {}

                def steps_R(i):
                    s0, d0, h0 = tiles[i]
                    st_ = []
                    nop_ = lambda: None
                    st_.append(lambda: pend.__setitem__(7, stat_sq(ysb[:, 7, :], [("ysb", 7)], "B", 7)))
                    st_.append(nop_)
                    st_.append(lambda: (stat_mm(pS1, "pS1", pend[6], 6), stat_mm(pS1, "pS1", pend[7], 7)))
                    st_.append(lambda: rstd_step(pS1, "pS1", rsB, "rsB"))
                    for c in range(8):
                        st_.append(lambda c=c: P.op("dve", (lambda e: e.scalar_tensor_tensor(
                            out=ysb[:, c, :], in0=ysb[:, c, :], scalar=gcol(ghalf, g_post, c), in1=rsB[:],
                            op0=ALU.mult, op1=ALU.mult)), r=[("ysb", c), "rsB", "ghalf"], w=[("ysb", c)]))
                    allk = [("ysb", c) for c in range(8)]
                    st_.append(lambda: P.op("pool", (lambda e: e.dma_start(out=ysb[:], in_=dview(src, 0, 8, s0, T), accum_op=ALU.add)),
                                            r=allk, w=allk, dma=(name, "xacc")))
                    if d0 is not None:
                        st_.append(lambda: P.op("pool", (lambda e: e.dma_start(out=dview(store_x, 0, 8, d0, T), in_=ysb[:])),
                                                r=allk, w=[(name, "dst", d0)], dma=(name, "st")))
                    if next_g is not None and h0 is not None:
                        st_ += [nop_] * 12
                        st_ += stat_steps(lambda c: ysb[:, c, :], lambda c: [("ysb", c)], pS0, "pS0", 2)
                        st_.append(lambda: rstd_step(pS0, "pS0", rsB, "rsB"))
                        for c in range(8):
                            def f_(c=c):
                                sl_ = c % 2
                                P.op("dve", (lambda e: e.scalar_tensor_tensor(
                                    out=h2c[sl_][:, 0, :], in0=ysb[:, c, :], scalar=gcol(gsb, next_g, c), in1=rsB[:],
                                    op0=ALU.mult, op1=ALU.mult)), r=[("ysb", c), "rsB", "gsb"], w=[("h2c", sl_)])
                                P.op("sp", (lambda e: e.dma_start(out=dview(store_h, c, 1, h0, T), in_=h2c[sl_][:])),
                                     r=[("h2c", sl_)], w=[("h2T", h0)], dma=(name, "sth", sl_))
                            st_.append(f_)
                    return st_

                def run_some(lst, k):
                    for _ in range(k):
                        if lst:
                            lst.pop(0)()

                n = len(tiles)
                load_x(0)
                run_some(steps_N(0), 99)
                for i in range(n):
                    if i + 1 < n:
                        load_x(i + 1)
                    side = steps_R(i - 1) if i >= 1 else []
                    per = 2 if side else 0
                    for f in range(NF):
                        pb = f % 2
                        blk = blk_of[f]
                        wk = [(name, "win", 0, blk), (name, "win", 1, blk)]
                        for c in range(8):
                            P.op("pe", (lambda e, c=c, f=f, pb=pb: e.matmul(
                                pG[pb][:], lhsT=win[:, c, f * 128:(f + 1) * 128], rhs=hT[:, c, :],
                                start=(c == 0), stop=(c == 7))), r=[("hT", c)] + wk, w=[("pG", pb)])
                        for c in range(8):
                            P.op("pe", (lambda e, c=c, f=f, pb=pb: e.matmul(
                                pU[pb][:], lhsT=win[:, c, DFF + f * 128:DFF + (f + 1) * 128], rhs=hT[:, c, :],
                                start=(c == 0), stop=(c == 7))), r=[("hT", c)] + wk, w=[("pU", pb)])
                        P.op("act", (lambda e, pb=pb: e.activation(out=sg[pb][:], in_=pG[pb][:], func=AF.Silu)),
                             r=[("pG", pb)], w=[("sg", pb)])
                        P.op("dve", (lambda e, pb=pb, f=f: e.tensor_tensor(
                            out=act[:, f, :], in0=sg[pb][:], in1=pU[pb][:], op=ALU.mult)),
                            r=[("sg", pb), ("pU", pb)], w=[("act", f)])
                        if f >= 1:
                            run_some(side, per)
                    run_some(side, 99)
                    side = steps_N(i + 1) if i + 1 < n else []
                    per = -(-len(side) // 7) if side else 0
                    for j in range(8):
                        pb = j % 2
                        for f in range(NF):
                            P.op("pe", (lambda e, j=j, f=f, pb=pb: e.matmul(
                                pY[pb][:], lhsT=wout[:, f, j * 128:(j + 1) * 128], rhs=act[:, f, :],
                                start=(f == 0), stop=(f == NF - 1))),
                                r=[("act", f), (name, "wout", f // 11)], w=[("pY", pb)])
                        P.op("act", (lambda e, j=j, pb=pb: e.activation(out=ysb[:, j, :], in_=pY[pb][:], func=AF.Copy)),
                             r=[("pY", pb)], w=[("ysb", j)])
                        if j >= 2:
                            stat_mm(pS1, "pS1", pend[j - 2], j - 2)
                        if j >= 1:
                            pend[j - 1] = stat_sq(ysb[:, j - 1, :], [("ysb", j - 1)], "B", j - 1)
                        if j >= 1:
                            run_some(side, per)
                    run_some(side, 99)
                run_some(steps_R(n - 1), 99)
                P.barrier()
                P.flush()

        def inproj_phase(ssmW=None):
            name = "ip"
            with ExitStack() as st:
                def S(nm, shape, dt):
                    return st.enter_context(nc.sbuf_tensor(name + nm, shape, dt))

                def PS(nm, shape):
                    return st.enter_context(nc.psum_tensor(name + nm, shape, F32))
                wmi = S("w", [128, 8, 2048], BF16)
                hin = [S("h%d" % i, [128, 8, T], BF16) for i in range(2)]
                stg = [S("stg%d" % i, [128, 16, T], BF16) for i in range(2)]
                pp = [PS("p%d" % i, [128, T]) for i in range(4)]
                psF = PS("psF", [128, T])
                if ssmW is not None:
                    P.defer = []
                    ssm_gen(ssmW, st, psF)
                    P.deferred, P.defer = P.defer, None
                    per_rep = -(-len(P.deferred) // 150)
                wv = w_mix_in.rearrange("(k p) f -> p k f", p=128)
                for b in (3, 1, 2, 0):
                    P.op("pool", (lambda e, b=b: e.dma_start(out=wmi[:, :, b * 512:(b + 1) * 512], in_=wv[:, :, b * 512:(b + 1) * 512])),
                         w=[("wmi", b)], dma=("ip", "w", b))
                n = 0
                for ti in range(NT_ALL):
                    slot = ti % 2
                    P.op("sp", (lambda e, slot=slot, ti=ti: e.dma_start(out=hin[slot][:], in_=dview(h2T, 0, 8, ti * T, T))),
                         r=[("h2T", ti * T)], w=[("hin", slot)], dma=("ip", "h", slot))
                    c_lo = 0 if ti >= NT_OWN else (4 if ti >= 4 else 12)
                    for cc in range(c_lo, 16):
                        pb = n % 4
                        for k in range(8):
                            P.op("pe", (lambda e, k=k, cc=cc, pb=pb, slot=slot: e.matmul(
                                pp[pb][:], lhsT=wmi[:, k, cc * 128:(cc + 1) * 128], rhs=hin[slot][:, k, :],
                                start=(k == 0), stop=(k == 7))), r=[("hin", slot), ("wmi", cc // 4)], w=[("ipp", pb)])
                        if cc >= 12:
                            o_ap = stg[slot][:, cc, :].rearrange("p (s j) -> p s j", s=8)
                            i_ap = pp[pb][:].rearrange("p (j s) -> p s j", s=8)
                        else:
                            o_ap = stg[slot][:, cc, :]
                            i_ap = pp[pb][:]
                        if n % 2 == 0:
                            P.op("act", (lambda e, o_ap=o_ap, i_ap=i_ap: e.activation(out=o_ap, in_=i_ap, func=AF.Copy)),
                                 r=[("ipp", pb)], w=[("stg", slot, cc)])
                        else:
                            P.op("dve", (lambda e, o_ap=o_ap, i_ap=i_ap: e.tensor_copy(out=o_ap, in_=i_ap)),
                                 r=[("ipp", pb)], w=[("stg", slot, cc)])
                        n += 1
                        if ssmW is not None:
                            P.replay(per_rep)
                    P.op("act", (lambda e, slot=slot, ti=ti, c_lo=c_lo: e.dma_start(
                        out=dview(qkvuT, c_lo, 16 - c_lo, ti * T, T), in_=stg[slot][:, c_lo:16, :])),
                        r=[("stg", slot, cc) for cc in range(c_lo, 16)], w=[("qkvu", ti)], dma=("ip", "st", slot))
                P.replay(1 << 30)
                P.barrier()
                P.flush()

        def attn_phase():
            name = "at"
            KW = SEQ - 2048
            with ExitStack() as st:
                def S(nm, shape, dt):
                    return st.enter_context(nc.sbuf_tensor(name + nm, shape, dt))
                ident = S("ident", [128, 128], BF16)
                atab = S("atab", [128, 24, 2, 128], F32)
                btab = S("btab", [128, 24, 128], F32)
                qT = S("qT", [128, HALF], BF16)
                kT = S("kT", [128, KW], BF16)
                vT = S("vT", [128, KW], BF16)
                vtoks = [S("vtok%d" % i, [128, 48, 2, 128], BF16) for i in range(2)]
                acc = S("acc", [128, 2, HALF], F32)
                sb = [S("sb%d" % i, [128, 4, 128], F32) for i in range(3)]
                pT = [S("pT%d" % i, [128, 4, 128], BF16) for i in range(3)]
                rd = S("rd", [128, T], F32)
                ao = S("ao", [128, HALF], BF16)
                psS = [st.enter_context(nc.psum_tensor(name + "s%d" % i, [128, 4, 128], F32)) for i in range(3)]
                psO = [st.enter_context(nc.psum_tensor(name + "o%d" % i, [128, 512], F32)) for i in range(2)]
                psT = [st.enter_context(nc.psum_tensor(name + "t%d" % i, [128, 8, 128], BF16)) for i in range(2)]

                P.op("pool", lambda e: e.dma_start(out=ident[:], in_=ident_d), w=["ident"], dma=("at", "c"))
                P.op("sp", lambda e: e.dma_start(out=atab[:], in_=atab_d.rearrange("p (a b c) -> p a b c", a=24, b=2)), w=["atab"], dma=("at", "c1"))
                P.op("sp", lambda e: e.dma_start(out=btab[:], in_=btab_d.rearrange("p (a c) -> p a c", a=24)), w=["btab"], dma=("at", "c2"))
                for vt_ in vtoks:
                    P.op("pool", (lambda e, vt_=vt_: e.memset(vt_[:, :, 0, 64:128], 1.0)), w=["vones"])
                    P.op("pool", (lambda e, vt_=vt_: e.memset(vt_[:, :, 1, 0:64], 1.0)), w=["vones"])
                allq = [("qkvu", ti) for ti in range(NT_ALL)]
                nq = 0
                for hp in range(4):
                    P.op("sp", (lambda e, hp=hp: e.dma_start(out=qT[:], in_=qkvuT[hp * 128:(hp + 1) * 128, HALF:SEQ])),
                         r=allq, w=["qT"], dma=("at", "q"))
                    P.op("sp", (lambda e, hp=hp: e.dma_start(out=kT[:], in_=qkvuT[512 + hp * 128:512 + (hp + 1) * 128, 2048:SEQ])),
                         r=allq, w=["kT"], dma=("at", "k"))
                    def load_vT(hq):
                        P.op("sp", (lambda e: e.dma_start(out=vT[:], in_=qkvuT[1024 + hq * 128:1024 + (hq + 1) * 128, 2048:SEQ])),
                             r=allq, w=["vT"], dma=("at", "v"))
                    if hp == 0:
                        load_vT(0)
                    def vbuild_steps(br_, hp=hp):
                        d_ = (1, 4, 16)[br_]
                        nblk_ = 48 // d_
                        vb_ = (hp * 3 + br_) % 2
                        vt_ = vtoks[vb_]
                        steps_ = []
                        for g4 in range(12):
                            def f_(g4=g4):
                                tb = g4 % 2
                                for j in range(4):
                                    blk = g4 * 4 + j
                                    r_, n_ = blk // nblk_, blk % nblk_
                                    s0 = r_ + d_ * 128 * n_
                                    P.op("pe", (lambda e, j=j, s0=s0: e.transpose(
                                        psT[tb][:, j, :], vT[:, s0:s0 + 127 * d_ + 1:d_], ident[:])),
                                        r=["vT", "ident"], w=[("psT", tb)])
                                P.op("act", (lambda e: e.activation(
                                    out=vt_[:, g4 * 4:g4 * 4 + 4, 0, 0:64], in_=psT[tb][:, 0:4, 0:64], func=AF.Copy)),
                                    r=[("psT", tb)], w=[("vtok", vb_, g4, 0)])
                                P.op("act", (lambda e: e.activation(
                                    out=vt_[:, g4 * 4:g4 * 4 + 4, 1, 64:128], in_=psT[tb][:, 0:4, 64:128], func=AF.Copy)),
                                    r=[("psT", tb)], w=[("vtok", vb_, g4, 1)])
                            steps_.append(f_)
                        return steps_
                    if hp == 0:
                        for f_ in vbuild_steps(0):
                            f_()
                    for br, d in enumerate((1, 4, 16)):
                        nblk = 48 // d
                        n0 = 16 // d
                        vbi = (hp * 3 + br) % 2
                        vtok = vtoks[vbi]
                        if br < 2:
                            vnext = vbuild_steps(br + 1)
                        elif hp + 1 < 4:
                            load_vT(hp + 1)
                            vnext = vbuild_steps(0, hp + 1)
                        else:
                            vnext = []
                        pairs = [(h2, r_, n_) for h2 in range(2) for r_ in range(d) for n_ in range(n0, nblk, 2)]
                        LAG = 2

                        def stage_a(idx, h2, r_, n_, d=d, br=br, hp=hp, nblk=nblk, n0=n0):
                            rows = slice(h2 * 64, h2 * 64 + 64)
                            tix = (hp * 2 + h2) * 3 + br
                            sbi = idx % 3
                            for b in range(2):
                                kb = r_ + d * 128 * (n_ + b - 1)
                                qb = r_ + d * 128 * (n_ + b) - 2048
                                for hh in range(2):
                                    k0 = kb + hh * 128 * d
                                    P.op("pe", (lambda e, hh=hh, k0=k0, qb=qb, b=b: e.matmul(
                                        psS[sbi][:, 2 * b + hh, :], lhsT=kT[rows, k0:k0 + 127 * d + 1:d], rhs=qT[rows, qb:qb + 127 * d + 1:d],
                                        start=True, stop=True)), r=["kT", "qT"], w=[("psS", sbi)])
                            if n_ == n0:
                                P.op("dve", (lambda e: e.tensor_tensor(
                                    out=sb[sbi][:, 0, :], in0=psS[sbi][:, 0, :], in1=btab[:, tix, :], op=ALU.add)),
                                    r=[("psS", sbi), "btab"], w=[("sb", sbi)])
                                P.op("dve", (lambda e: e.tensor_tensor(
                                    out=sb[sbi][:, 1, :], in0=psS[sbi][:, 1, :], in1=atab[:, tix, 1, :], op=ALU.add)),
                                    r=[("psS", sbi), "atab"], w=[("sb", sbi)])
                                P.op("dve", (lambda e: e.tensor_tensor(
                                    out=sb[sbi][:, 2:4, :], in0=psS[sbi][:, 2:4, :], in1=atab[:, tix, :, :], op=ALU.add)),
                                    r=[("psS", sbi), "atab"], w=[("sb", sbi)])
                            else:
                                tb2 = atab[:, tix, :, :].rearrange("p a b -> p (a b)").unsqueeze(1).to_broadcast([128, 2, 256])
                                P.op("dve", (lambda e: e.tensor_tensor(
                                    out=sb[sbi][:].rearrange("p (x a) b -> p x (a b)", x=2),
                                    in0=psS[sbi][:].rearrange("p (x a) b -> p x (a b)", x=2), in1=tb2, op=ALU.add)),
                                    r=[("psS", sbi), "atab"], w=[("sb", sbi)])
                            P.op("act", (lambda e: e.activation(out=pT[sbi][:], in_=sb[sbi][:], func=AF.Exp, scale=0.125)),
                                 r=[("sb", sbi)], w=[("pT", sbi)])

                        def stage_b(idx, h2, r_, n_, d=d, br=br, hp=hp, nblk=nblk, n0=n0, vtok=vtok, vbi=vbi):
                            sbi = idx % 3
                            ob = idx % 2
                            for b in range(2):
                                blk = r_ * nblk + n_ + b
                                for hh in range(2):
                                    vb = blk - 1 + hh
                                    P.op("pe", (lambda e, hh=hh, vb=vb, b=b: e.matmul(
                                        psO[ob][:, b * 128:(b + 1) * 128], lhsT=vtok[:, vb, h2, :], rhs=pT[sbi][:, 2 * b + hh, :],
                                        start=(hh == 0), stop=(hh == 1))),
                                        r=[("pT", sbi), ("vtok", vbi, vb // 4, h2), "vones"], w=[("psO", ob)])
                            qb = r_ + d * 128 * n_ - 2048
                            asl = acc[:, h2, qb:qb + 255 * d + 1:d]
                            if br == 0:
                                P.op("act", (lambda e: e.activation(out=asl, in_=psO[ob][:, 0:256], func=AF.Copy)),
                                     r=[("psO", ob)], w=[("acc", h2)])
                            else:
                                P.op("dve", (lambda e: e.tensor_tensor(out=asl, in0=asl, in1=psO[ob][:, 0:256], op=ALU.add)),
                                     r=[("psO", ob), ("acc", h2)], w=[("acc", h2)])
                        for idx in range(len(pairs) + LAG):
                            if idx < len(pairs):
                                stage_a(idx, *pairs[idx])
                            if idx - LAG >= 0:
                                stage_b(idx - LAG, *pairs[idx - LAG])
                            if idx % 2 == 1 and vnext:
                                vnext.pop(0)()
                        while vnext:
                            vnext.pop(0)()
                    for tt in range(NT_OWN):
                        ts_ = slice(tt * T, (tt + 1) * T)
                        P.op("dve", (lambda e, ts_=ts_: e.reciprocal(out=rd[0:64, :], in_=acc[64:128, 0, ts_])),
                             r=[("acc", 0), "ao"], w=["rd0"])
                        P.op("dve", (lambda e, ts_=ts_: e.tensor_tensor(out=ao[0:64, ts_], in0=acc[0:64, 0, ts_], in1=rd[0:64, :], op=ALU.mult)),
                             r=[("acc", 0), "rd0"], w=["ao"])
                        P.op("dve", (lambda e, ts_=ts_: e.reciprocal(out=rd[64:128, :], in_=acc[0:64, 1, ts_])),
                             r=[("acc", 1), "ao"], w=["rd1"])
                        P.op("dve", (lambda e, ts_=ts_: e.tensor_tensor(out=ao[64:128, ts_], in0=acc[64:128, 1, ts_], in1=rd[64:128, :], op=ALU.mult)),
                             r=[("acc", 1), "rd1"], w=["ao"])
                    P.op("act", (lambda e, hp=hp: e.dma_start(out=catT[hp * 128:(hp + 1) * 128, :], in_=ao[:])),
                         r=["ao"], w=[("cat", hp)], dma=("at", "st"))
                P.barrier()
                P.flush()

        NI = 10
        PI_ = float(np.pi)
        NJC = SEQ // 8
        GK = "ssgen"

        def ssm_persist(st):
            def S(nm, shape, dt):
                return st.enter_context(nc.sbuf_tensor("s2" + nm, shape, dt))
            sv = S("sv", [128, 16], F32)
            swp = S("swp", [128, 128], F32)
            idf = S("idf", [128, 128], F32)
            idb = S("idb", [128, 128], BF16)
            wglu = S("wglu", [128, 4, 512], BF16)
            Bdec = S("Bdec", [128, 32, 128], BF16)
            Cdec = S("Cdec", [128, 32, 128], BF16)
            Toep = S("Toep", [128, 32, 128], BF16)
            PR8 = S("PR8", [128, NI, 32], F32)
            PI8 = S("PI8", [128, NI, 32], F32)
            SPI8 = S("SPI8", [128, NI, 32], F32)

            def ld(dst, src_, key, eng="sp"):
                P.op(eng, (lambda e: e.dma_start(out=dst, in_=src_)), w=[key], dma=("s2", key))
            ld(sv[:], ssm_v_d, "sv")
            ld(swp[:], swap_d, "swp")
            ld(idf[:], ident_d, "idf")
            ld(idb[:], ident_d, "idb", "pool")
            ld(wglu[:], w_glu.rearrange("(k p) f -> p k f", p=128), "wglu", "pool")
            return (sv, swp, idf, idb, wglu, Bdec, Cdec, Toep, PR8, PI8, SPI8)

        def ssm_gen(W, st2, psF):
            name = "s2"
            sv, swp, idf, idb, wglu, Bdec, Cdec, Toep, PR8, PI8, SPI8 = W
            sgnA, sgnB = sv[:, 8:9], sv[:, 9:10]

            def ld(dst, src_, key, eng="sp"):
                P.op(eng, (lambda e: e.dma_start(out=dst, in_=src_)), w=[key], dma=("s2", key))

            def dve(fn, r=(GK,), w=(GK,)):
                P.op("dve", fn, r=list(r), w=list(w))

            def actf(fn, r=(GK,), w=(GK,)):
                P.op("act", fn, r=list(r), w=list(w))
            if True:
                if True:
                    def S2(nm, shape, dt):
                        return st2.enter_context(nc.sbuf_tensor(name + nm, shape, dt))
                    prm = S2("prm", [128, 96], F32)
                    bab = S2("bab", [128, 2, 32, 16], F32)
                    cab = S2("cab", [128, 2, 32, 16], F32)
                    cmask = S2("cmask", [128, 128], F32)
                    dstk = S2("dstk", [128, 32], F32)
                    tmpv = [S2("tv%d" % i, [128, 32], F32) for i in range(12)]
                    POWr = S2("POWr", [128, 16, 32], F32)
                    POWi = S2("POWi", [128, 16, 32], F32)
                    Qr = S2("Qr", [128, 32, 8], F32)
                    Qi = S2("Qi", [128, 32, 8], F32)
                    Q2r = S2("Q2r", [128, 32, 8], F32)
                    Q2i = S2("Q2i", [128, 32, 8], F32)
                    BdT = S2("BdT", [128, 32, 8, 16], F32)
                    big = S2("big", [128, 32, 9, 16], F32)
                    VV = S2("VV", [128, 32, 9, 16], F32)
                    rt = S2("rt", [128, 128], F32)
                    ld(prm[:], ssm_a_d, "prm")
                    ld(bab[:], ssm_b_d.rearrange("p (a g c) -> p a g c", a=2, g=32), "bab")
                    ld(cab[:], ssm_c_d.rearrange("p (a g c) -> p a g c", a=2, g=32), "cab")
                    ld(cmask[:], cmask_d, "cmask")
                    ld(dstk[:], dstk_d, "dstk")
                    are, aim, ldt = prm[:, 0:32], prm[:, 32:64], prm[:, 64:96]
                    dt_, lr, li, mag, angs, angc, m_, t1, t2, t3, wr, wi = [t[:] for t in tmpv]
                    actf(lambda e: e.activation(out=dt_, in_=ldt, func=AF.Exp), r=("prm", GK))
                    dve(lambda e: e.tensor_tensor(out=lr, in0=dt_, in1=are, op=ALU.mult), r=("prm", GK))
                    dve(lambda e: e.tensor_tensor(out=li, in0=dt_, in1=aim, op=ALU.mult))
                    actf(lambda e: e.activation(out=mag, in_=lr, func=AF.Exp))
                    dve(lambda e: e.tensor_copy(out=angs, in_=li))
                    dve(lambda e: e.tensor_scalar(out=angc, in0=li, scalar1=PI_ / 2, scalar2=None, op0=ALU.add))
                    for it in range(4):
                        for ang in (angs, angc):
                            dve(lambda e, ang=ang: e.tensor_scalar(out=m_, in0=ang, scalar1=PI_, scalar2=2 * PI_, op0=ALU.is_gt, op1=ALU.mult))
                            dve(lambda e, ang=ang: e.tensor_tensor(out=ang, in0=ang, in1=m_, op=ALU.subtract))
                    actf(lambda e: e.activation(out=angs, in_=angs, func=AF.Sin))
                    actf(lambda e: e.activation(out=angc, in_=angc, func=AF.Sin))
                    K0 = 7

                    def pw(k):
                        return POWr[:, K0 + k, :], POWi[:, K0 + k, :]
                    dve(lambda e: e.memset(POWr[:, K0, :], 1.0))
                    dve(lambda e: e.memset(POWi[:, K0, :], 0.0))
                    dve(lambda e: e.tensor_tensor(out=pw(1)[0], in0=mag, in1=angc, op=ALU.mult))
                    dve(lambda e: e.tensor_tensor(out=pw(1)[1], in0=mag, in1=angs, op=ALU.mult))

                    def cmul(zr, zi, xr, xi, yr, yi):
                        dve(lambda e: e.tensor_tensor(out=t1, in0=xr, in1=yr, op=ALU.mult))
                        dve(lambda e: e.tensor_tensor(out=t2, in0=xi, in1=yi, op=ALU.mult))
                        dve(lambda e: e.tensor_tensor(out=zr, in0=t1, in1=t2, op=ALU.subtract))
                        dve(lambda e: e.tensor_tensor(out=t1, in0=xr, in1=yi, op=ALU.mult))
                        dve(lambda e: e.tensor_tensor(out=t2, in0=xi, in1=yr, op=ALU.mult))
                        dve(lambda e: e.tensor_tensor(out=zi, in0=t1, in1=t2, op=ALU.add))
                    for k in range(2, 9):
                        cmul(*pw(k), *pw(k - 1), *pw(1))
                    dve(lambda e: e.tensor_tensor(out=t1, in0=pw(1)[0], in1=pw(1)[0], op=ALU.mult))
                    dve(lambda e: e.tensor_tensor(out=t2, in0=pw(1)[1], in1=pw(1)[1], op=ALU.mult))
                    dve(lambda e: e.tensor_tensor(out=t3, in0=t1, in1=t2, op=ALU.add))
                    dve(lambda e: e.reciprocal(out=t3, in_=t3))
                    dve(lambda e: e.tensor_tensor(out=pw(-1)[0], in0=pw(1)[0], in1=t3, op=ALU.mult))
                    dve(lambda e: e.scalar_tensor_tensor(out=pw(-1)[1], in0=pw(1)[1], scalar=-1.0, in1=t3, op0=ALU.mult, op1=ALU.mult))
                    for k in range(-2, -8, -1):
                        cmul(*pw(k), *pw(k + 1), *pw(-1))
                    dve(lambda e: e.tensor_scalar(out=m_, in0=pw(1)[0], scalar1=-1.0, scalar2=None, op0=ALU.add))
                    dve(lambda e: e.tensor_tensor(out=t1, in0=are, in1=are, op=ALU.mult))
                    dve(lambda e: e.tensor_tensor(out=t2, in0=aim, in1=aim, op=ALU.mult))
                    dve(lambda e: e.tensor_tensor(out=t3, in0=t1, in1=t2, op=ALU.add))
                    dve(lambda e: e.reciprocal(out=t3, in_=t3))
                    dve(lambda e: e.tensor_tensor(out=wr, in0=m_, in1=are, op=ALU.mult))
                    dve(lambda e: e.tensor_tensor(out=t1, in0=pw(1)[1], in1=aim, op=ALU.mult))
                    dve(lambda e: e.tensor_tensor(out=wr, in0=wr, in1=t1, op=ALU.add))
                    dve(lambda e: e.tensor_tensor(out=wr, in0=wr, in1=t3, op=ALU.mult))
                    dve(lambda e: e.tensor_tensor(out=wi, in0=pw(1)[1], in1=are, op=ALU.mult))
                    dve(lambda e: e.tensor_tensor(out=t1, in0=m_, in1=aim, op=ALU.mult))
                    dve(lambda e: e.tensor_tensor(out=wi, in0=wi, in1=t1, op=ALU.subtract))
                    dve(lambda e: e.tensor_tensor(out=wi, in0=wi, in1=t3, op=ALU.mult))
                    for s_ in range(8):
                        cmul(Qr[:, :, s_], Qi[:, :, s_], *pw(7 - s_), wr, wi)
                        cmul(Q2r[:, :, s_], Q2i[:, :, s_], *pw(-s_), wr, wi)
                    SH = [128, 32, 8, 16]
                    BAb = bab[:, 0, :, :].unsqueeze(2).to_broadcast(SH)
                    BBb = bab[:, 1, :, :].unsqueeze(2).to_broadcast(SH)
                    def bq(dst, qr_, qi_):
                        qrb = qr_[:].unsqueeze(3).to_broadcast(SH)
                        qib = qi_[:].unsqueeze(3).to_broadcast(SH)
                        dve(lambda e: e.tensor_tensor(out=big[:, :, 0:8, :], in0=BBb, in1=qib, op=ALU.mult), r=("bab", GK))
                        dve(lambda e: e.tensor_scalar(out=big[:, :, 0:8, :], in0=big[:, :, 0:8, :], scalar1=sgnB, scalar2=None, op0=ALU.mult), r=("sv", GK))
                        dve(lambda e: e.tensor_tensor(out=dst[:], in0=BAb, in1=qrb, op=ALU.mult), r=("bab", GK))
                        dve(lambda e: e.tensor_tensor(out=dst[:], in0=dst[:], in1=big[:, :, 0:8, :], op=ALU.add))
                    bq(BdT, Qr, Qi)
                    for g in range(32):
                        P.op("pe", (lambda e, g=g: e.transpose(psF[:, 0:128], BdT[:, g, :, :].rearrange("p s c -> p (s c)"), idf[:])),
                             r=[GK, "idf"], w=["psF"])
                        P.op("act", (lambda e, g=g: e.activation(out=Bdec[:, g, :], in_=psF[:, 0:128], func=AF.Copy)),
                             r=["psF"], w=["Bdec"])
                    UTp = BdT
                    bq(UTp, Q2r, Q2i)
                    SH9 = [128, 32, 9, 16]
                    CAb = cab[:, 0, :, :].unsqueeze(2).to_broadcast(SH9)
                    CBb = cab[:, 1, :, :].unsqueeze(2).to_broadcast(SH9)
                    prb = POWr[:, K0:K0 + 9, :].rearrange("p k g -> p g k").unsqueeze(3).to_broadcast(SH9)
                    pib = POWi[:, K0:K0 + 9, :].rearrange("p k g -> p g k").unsqueeze(3).to_broadcast(SH9)
                    dve(lambda e: e.tensor_tensor(out=VV[:], in0=CAb, in1=prb, op=ALU.mult), r=("cab", GK))
                    dve(lambda e: e.tensor_scalar(out=VV[:], in0=VV[:], scalar1=sgnA, scalar2=None, op0=ALU.mult), r=("sv", GK))
                    dve(lambda e: e.tensor_tensor(out=big[:], in0=CBb, in1=pib, op=ALU.mult), r=("cab", GK))
                    dve(lambda e: e.tensor_tensor(out=VV[:], in0=VV[:], in1=big[:], op=ALU.subtract))
                    dve(lambda e: e.tensor_copy(out=Cdec[:].rearrange("p g (t c) -> p g t c", t=8), in_=VV[:, :, 1:9, :]), w=(GK, "Cdec"))
                    for g in range(32):
                        P.op("pe", (lambda e, g=g: e.matmul(psF[:, 128:256], lhsT=UTp[:, g, :, :].rearrange("p s c -> p (s c)"),
                                                            rhs=VV[:, g, 0:8, :].rearrange("p t c -> p (t c)"), start=True, stop=True)),
                             r=[GK, "Bdec"], w=["psF"])
                        P.op("dve", (lambda e, g=g: e.tensor_tensor(out=rt[:], in0=psF[:, 128:256], in1=cmask[:], op=ALU.mult)),
                             r=["psF", "cmask"], w=["rt"])
                        P.op("dve", (lambda e, g=g: e.scalar_tensor_tensor(out=Toep[:, g, :], in0=idf[:], scalar=dstk[:, g:g + 1], in1=rt[:],
                                                                           op0=ALU.mult, op1=ALU.add)),
                             r=["rt", "idf", "dstk"], w=["Toep"])
                    dve(lambda e: e.tensor_copy(out=PR8[:, 0, :], in_=pw(8)[0]))
                    dve(lambda e: e.tensor_copy(out=PI8[:, 0, :], in_=pw(8)[1]))
                    for i in range(1, NI):
                        dve(lambda e, i=i: e.tensor_tensor(out=t1, in0=PR8[:, i - 1, :], in1=PR8[:, i - 1, :], op=ALU.mult))
                        dve(lambda e, i=i: e.tensor_tensor(out=t2, in0=PI8[:, i - 1, :], in1=PI8[:, i - 1, :], op=ALU.mult))
                        dve(lambda e, i=i: e.tensor_tensor(out=PR8[:, i, :], in0=t1, in1=t2, op=ALU.subtract))
                        dve(lambda e, i=i: e.scalar_tensor_tensor(out=PI8[:, i, :], in0=PR8[:, i - 1, :], scalar=2.0, in1=PI8[:, i - 1, :],
                                                                  op0=ALU.mult, op1=ALU.mult))
                    dve(lambda e: e.tensor_scalar(out=SPI8[:], in0=PI8[:], scalar1=sgnA, scalar2=None, op0=ALU.mult), r=("sv", GK))

        def ssm_phase(W):
            name = "s2"
            sv, swp, idf, idb, wglu, Bdec, Cdec, Toep, PR8, PI8, SPI8 = W
            with ExitStack() as st:
                def S(nm, shape, dt):
                    return st.enter_context(nc.sbuf_tensor(name + nm, shape, dt))

                def PS(nm, shape, dt=F32):
                    return st.enter_context(nc.psum_tensor(name + nm, shape, dt))
                sel = S("sel", [128, 64, 128], BF16)
                selT = S("selT", [128, 64, 128], BF16)
                psA = [PS("a%d" % i, [128, T]) for i in range(4)]
                psY = [PS("y%d" % i, [128, T]) for i in range(2)]
                P.op("pool", (lambda e: e.dma_start(out=sel[:], in_=sel_d.rearrange("p (a b) -> p a b", a=64))), w=["sel"], dma=("s2", "sel"))
                P.op("pool", (lambda e: e.dma_start(out=selT[:], in_=selT_d.rearrange("p (a b) -> p a b", a=64))), w=["selT"], dma=("s2", "selT"))
                Rq2 = [S("Rq%d" % i, [128, 8, NI, 128], BF16) for i in range(2)]
                rtmp = [S("rtmp%d" % i, [128, 128], F32) for i in range(2)]
                uT = S("uT", [128, SEQ], BF16)
                U1 = [S("U1%d" % i, [128, NJC], BF16) for i in range(4)]
                Hs = [[S("Hs%d_%d" % (a_, i), [128, NJC], BF16) for i in range(2)] for a_ in range(4)]
                g1 = [S("g1%d" % i, [128, T], F32) for i in range(2)]
                Yg = S("Yg", [128, 8, T], BF16)
                yg = S("yg", [128, 4, HALF], BF16)
                so = [S("so%d" % i, [128, 4, T], BF16) for i in range(2)]
                sgm = [S("sgm%d" % i, [128, T], F32) for i in range(2)]
                allq = [("qkvu", ti) for ti in range(NT_ALL)]
                nev = 0
                nrt = 0
                ngr = 0

                def evac(pb, dst, keys_w):
                    if pb % 2 == 0:
                        P.op("act", (lambda e: e.activation(out=dst, in_=psA[pb][:], func=AF.Copy)), r=[("psA", pb)], w=keys_w)
                    else:
                        P.op("dve", (lambda e: e.tensor_copy(out=dst, in_=psA[pb][:])), r=[("psA", pb)], w=keys_w)
                def gen_R(q):
                    nonlocal nrt
                    Rq_ = Rq2[q % 2]
                    for gm in range(8):
                        g = 8 * q + gm
                        for i in range(NI):
                            rb = nrt % 2
                            nrt += 1
                            P.op("dve", (lambda e, rb=rb, i=i, g=g: e.tensor_scalar(
                                out=rtmp[rb][:], in0=swp[:], scalar1=SPI8[:, i, g:g + 1], scalar2=None, op0=ALU.mult)),
                                r=[GK, "swp"], w=[("rtmp", rb)])
                            P.op("dve", (lambda e, rb=rb, i=i, g=g, gm=gm, Rq_=Rq_: e.scalar_tensor_tensor(
                                out=Rq_[:, gm, i, :], in0=idf[:], scalar=PR8[:, i, g:g + 1], in1=rtmp[rb][:],
                                op0=ALU.mult, op1=ALU.add)), r=[GK, "idf", ("rtmp", rb)], w=[("R", q % 2, gm)])
                gen_R(0)
                for q in range(4):
                    Rq = Rq2[q % 2]
                    P.op("sp", (lambda e, q=q: e.dma_start(out=uT[:], in_=qkvuT[1536 + q * 128:1536 + (q + 1) * 128, :])),
                         r=allq, w=["uT"], dma=("s2", "u"))
                    for pr_ in range(2):
                        if pr_ == 1 and q + 1 < 4:
                            gen_R(q + 1)
                        gms = tuple(range(4 * pr_, 4 * pr_ + 4))
                        for ab, gm in enumerate(gms):
                            for hf in range(2):
                                pb = nev % 4
                                nev += 1
                                for s_ in range(8):
                                    P.op("pe", (lambda e, pb=pb, gm=gm, s_=s_, hf=hf: e.matmul(
                                        psA[pb][:], lhsT=sel[:, gm * 8 + s_, :],
                                        rhs=uT[:].rearrange("p (t s j) -> p t s j", s=8, j=64)[:, hf * 8:(hf + 1) * 8, s_, :],
                                        start=(s_ == 0), stop=(s_ == 7))), r=["uT", "sel"], w=[("psA", pb)])
                                evac(pb, U1[ab][:, hf * T:(hf + 1) * T], [("U1", ab, hf)])
                        cur = 0
                        for ab, gm in enumerate(gms):
                            g = 8 * q + gm
                            for hf in range(2):
                                pb = nev % 4
                                nev += 1
                                P.op("pe", (lambda e, pb=pb, g=g, ab=ab, hf=hf: e.matmul(
                                    psA[pb][:], lhsT=Bdec[:, g, :], rhs=U1[ab][:, hf * T:(hf + 1) * T], start=True, stop=True)),
                                    r=[("U1", ab, hf), "Bdec"], w=[("psA", pb)])
                                evac(pb, Hs[ab][cur][:, hf * T:(hf + 1) * T], [("Hs", ab, cur, hf)])
                        for i in range(NI):
                            dd = 1 << i
                            nxt = 1 - cur
                            for tt in range(2):
                                for ab, gm in enumerate(gms):
                                    c0 = tt * T
                                    lo = max(0, dd - c0)
                                    has = lo < T
                                    pb = nev % 4
                                    nev += 1
                                    P.op("pe", (lambda e, pb=pb, c0=c0, cur=cur, has=has, ab=ab: e.matmul(
                                        psA[pb][:], lhsT=idb[:], rhs=Hs[ab][cur][:, c0:c0 + T], start=True, stop=not has)),
                                        r=[("Hs", ab, cur, tt), "idb"], w=[("psA", pb)])
                                    if has:
                                        s_lo = c0 + lo - dd
                                        s_hi = c0 + T - dd
                                        rk = [("Hs", ab, cur, s_lo // T), ("Hs", ab, cur, (s_hi - 1) // T), ("R", q % 2, gm)]
                                        P.op("pe", (lambda e, pb=pb, lo=lo, s_lo=s_lo, s_hi=s_hi, cur=cur, gm=gm, i=i, ab=ab, Rq=Rq: e.matmul(
                                            psA[pb][:, lo:T], lhsT=Rq[:, gm, i, :], rhs=Hs[ab][cur][:, s_lo:s_hi], start=False, stop=True)),
                                            r=rk, w=[("psA", pb)])
                                    evac(pb, Hs[ab][nxt][:, c0:c0 + T], [("Hs", ab, nxt, tt)])
                            cur = nxt
                        for ab, gm in enumerate(gms):
                            g = 8 * q + gm
                            yb = ab % 2
                            P.op("pe", (lambda e, yb=yb, g=g, ab=ab: e.matmul(
                                psY[yb][:], lhsT=Toep[:, g, :], rhs=U1[ab][:, T:2 * T], start=True, stop=False)),
                                r=[("U1", ab, 1), "Toep"], w=[("psY", yb)])
                            P.op("pe", (lambda e, yb=yb, g=g, cur=cur, ab=ab: e.matmul(
                                psY[yb][:], lhsT=Cdec[:, g, :], rhs=Hs[ab][cur][:, T - 1:2 * T - 1], start=False, stop=True)),
                                r=[("Hs", ab, cur, 0), ("Hs", ab, cur, 1), "Cdec"], w=[("psY", yb)])
                            gk = ("g1", yb)
                            gg = g1[yb]
                            P.op("act", (lambda e, yb=yb, gg=gg: e.activation(out=gg[:], in_=psY[yb][:], func=AF.Square)), r=[("psY", yb)], w=[gk])
                            P.op("dve", (lambda e, gg=gg: e.tensor_scalar(out=gg[:], in0=gg[:], scalar1=0.044715, scalar2=1.0, op0=ALU.mult, op1=ALU.add)),
                                 r=[gk], w=[gk])
                            P.op("dve", (lambda e, yb=yb, gg=gg: e.tensor_tensor(out=gg[:], in0=gg[:], in1=psY[yb][:], op=ALU.mult)),
                                 r=[gk, ("psY", yb)], w=[gk])
                            P.op("act", (lambda e, gg=gg: e.activation(out=gg[:], in_=gg[:], func=AF.Sigmoid, scale=1.5957691216057308)), r=[gk], w=[gk])
                            P.op("dve", (lambda e, yb=yb, gg=gg, gm=gm: e.tensor_tensor(out=Yg[:, gm, :], in0=gg[:], in1=psY[yb][:], op=ALU.mult)),
                                 r=[gk, ("psY", yb)], w=[("Yg", gm)])
                    for t_ in range(8):
                        pb = nev % 4
                        nev += 1
                        for gm in range(8):
                            P.op("pe", (lambda e, pb=pb, gm=gm, t_=t_: e.matmul(
                                psA[pb][:], lhsT=selT[:, gm * 8 + t_, :], rhs=Yg[:, gm, :], start=(gm == 0), stop=(gm == 7))),
                                r=[("Yg", gm), "selT"], w=[("psA", pb)])
                        evac(pb, yg[:, q, t_:t_ + 8 * 511 + 1:8], [("yg", q)])
                for t8 in range(NT_OWN):
                    ts_ = slice(t8 * T, (t8 + 1) * T)
                    sob = so[t8 % 2]
                    for jo in range(4):
                        yb = jo % 2
                        sg_ = sgm[jo % 2]
                        for k in range(4):
                            P.op("pe", (lambda e, yb=yb, k=k, jo=jo, ts_=ts_: e.matmul(
                                psY[yb][:], lhsT=wglu[:, k, jo * 128:(jo + 1) * 128], rhs=yg[:, k, ts_], start=(k == 0), stop=(k == 3))),
                                r=[("yg", k), "wglu"], w=[("psY", yb)])
                        P.op("act", (lambda e, yb=yb, jo=jo, sg_=sg_: e.activation(out=sg_[:], in_=psY[yb][:], func=AF.Sigmoid, bias=sv[:, 4 + jo:5 + jo])),
                             r=[("psY", yb), "sv"], w=[("sgm", jo % 2)])
                        P.op("dve", (lambda e, jo=jo, ts_=ts_, sg_=sg_, sob=sob: e.tensor_tensor(out=sob[:, jo, :], in0=yg[:, jo, ts_], in1=sg_[:], op=ALU.mult)),
                             r=[("sgm", jo % 2), ("yg", jo)], w=[("so", t8 % 2)])
                    P.op("sp", (lambda e, t8=t8, sob=sob: e.dma_start(out=dview(catT, 4, 4, t8 * T, T), in_=sob[:])),
                         r=[("so", t8 % 2)], w=[("cat", 4 + t8 % 4)], dma=("s2", "st", t8 % 2))
                P.barrier()
                P.flush()


        def outproj_phase():
            name = "op"
            with ExitStack() as st:
                def S(nm, shape, dt):
                    return st.enter_context(nc.sbuf_tensor(name + nm, shape, dt))

                def PS(nm, shape):
                    return st.enter_context(nc.psum_tensor(name + nm, shape, F32))
                wmo = S("w", [128, 8, D], BF16)
                cin = [S("c%d" % i, [128, 8, T], BF16) for i in range(2)]
                xin = [S("x%d" % i, [128, 8, T], F32) for i in range(2)]
                ysb2 = [S("ysb%d" % i, [128, 8, T], F32) for i in range(2)]
                sq2 = [S("sq%d" % i, [128, 8, T], BF16) for i in range(2)]
                rstd2 = [S("rstd%d" % i, [128, T], F32) for i in range(2)]
                pY = [PS("pY%d" % i, [128, T]) for i in range(4)]
                pS2 = [PS("pS%d" % i, [128, T]) for i in range(2)]
                wv = w_mix_out.rearrange("(k p) f -> p k f", p=128)
                for b in range(2):
                    P.op("pool", (lambda e, b=b: e.dma_start(out=wmo[:, b * 4:(b + 1) * 4, :], in_=wv[:, b * 4:(b + 1) * 4, :])),
                         w=[("wmo", b)], dma=("op", "w", b))
                allcat = [("cat", i) for i in range(8)]
                for ti in range(NT_OWN):
                    slot = ti % 2
                    t0 = ti * T
                    ysb, sq, rstd, pS = ysb2[slot], sq2[slot], rstd2[slot], pS2[slot]
                    kY, kQ, kR, kP = ("opysb", slot), ("opsq", slot), ("oprstd", slot), ("oppS", slot)
                    def op_loads(tj):
                        sl_, tq = tj % 2, tj * T
                        P.op("sp", (lambda e: e.dma_start(out=cin[sl_][:], in_=dview(catT, 0, 8, tq, T))),
                             r=allcat, w=[("cin", sl_)], dma=("op", "c", sl_))
                        P.op("sp", (lambda e: e.dma_start(out=xin[sl_][:], in_=dview(x1T, 0, 8, tq, T))),
                             r=[("f1", "dst", tq)], w=[("xin", sl_)], dma=("op", "x", sl_))
                    if ti == 0:
                        op_loads(0)
                    if ti + 1 < NT_OWN:
                        op_loads(ti + 1)
                    for j in range(8):
                        pb = (ti * 8 + j) % 4
                        for k in range(8):
                            P.op("pe", (lambda e, j=j, k=k, pb=pb, slot=slot: e.matmul(
                                pY[pb][:], lhsT=wmo[:, k, j * 128:(j + 1) * 128], rhs=cin[slot][:, k, :],
                                start=(k == 0), stop=(k == 7))), r=[("cin", slot), ("wmo", k // 4)], w=[("opY", pb)])
                        P.op("act", (lambda e, j=j, pb=pb, ysb=ysb: e.activation(out=ysb[:, j, :], in_=pY[pb][:], func=AF.Copy)),
                             r=[("opY", pb)], w=[kY + (j,)])
                        P.op("dve", (lambda e, j=j, ysb=ysb, sq=sq: e.tensor_tensor(out=sq[:, j, :], in0=ysb[:, j, :], in1=ysb[:, j, :], op=ALU.mult)),
                             r=[kY + (j,)], w=[kQ + (j,)])
                    for c in range(8):
                        P.op("pe", (lambda e, c=c, sq=sq, pS=pS: e.matmul(pS[:], lhsT=ones[:], rhs=sq[:, c, :], start=(c == 0), stop=(c == 7))),
                             r=[kQ + (c,), "ones"], w=[kP])
                    P.op("act", (lambda e, rstd=rstd, pS=pS: e.activation(out=rstd[:], in_=pS[:], func=AF.Sqrt, bias=EPS, scale=1.0 / D)),
                         r=[kP], w=[kR])
                    P.op("dve", (lambda e, rstd=rstd: e.reciprocal(out=rstd[:], in_=rstd[:])), r=[kR], w=[kR])
                    x = xin[slot]
                    for c in range(8):
                        P.op("dve", (lambda e, c=c, ysb=ysb, rstd=rstd: e.scalar_tensor_tensor(
                            out=ysb[:, c, :], in0=ysb[:, c, :], scalar=gcol(gsb, G_MIXPOST, c), in1=rstd[:],
                            op0=ALU.mult, op1=ALU.mult)), r=[kY + (c,), kR, "gsb"], w=[kY + (c,)])
                        P.op("dve", (lambda e, c=c, x=x, ysb=ysb: e.tensor_tensor(
                            out=x[:, c, :], in0=x[:, c, :], in1=ysb[:, c, :], op=ALU.add)),
                            r=[kY + (c,), ("xin", slot)], w=[("xin", slot)])
                    P.op("sp", (lambda e, x=x, t0=t0: e.dma_start(out=dview(x2T, 0, 8, t0, T), in_=x[:])),
                         r=[("xin", slot)], w=[("x2T", t0)], dma=("op", "st", slot))
                P.barrier()
                P.flush()

        ntl = NT_ALL if debug is None else debug.get("nt1", NT_ALL)
        tiles1 = []
        for i in range(NT_ALL - ntl, NT_ALL):
            tiles1.append((i * T, (i - NT_OWN) * T if i >= NT_OWN else None, i * T))
        ph = (debug or {}).get("phases", "all")
        last = []
        if ph == "all" or "ffn1" in ph:
            last = ffn_phase("f1", xT, tiles1, w1_in, w1_out, G_F1PRE, G_F1POST, x1T, G_MIXPRE, h2T)
        sstack = ExitStack()
        ssmW = ssm_persist(sstack) if (ph == "all" or "ssm" in ph) else None
        if ph == "all" or "inproj" in ph:
            inproj_phase(ssmW)
        if ph == "all" or "attn" in ph:
            attn_phase()
        if ph == "all" or "ssm" in ph:
            ssm_phase(ssmW)
        sstack.close()
        if ph == "all" or "outproj" in ph:
            outproj_phase()
        if ph == "all" or "ffn2" in ph:
            tiles2 = [(i * T, i * T, None) for i in range(NT_OWN)]
            last = ffn_phase("f2", x2T, tiles2, w2_in, w2_out, G_F2PRE, G_F2POST, outT, None, None)
        P.barrier()
        P.flush()
    return nc


def make_inputs(inputs):
    x = np.asarray(inputs["x"], dtype=np.float32)
    g = np.zeros((128, 48), np.float32)
    for i, k in enumerate(["ffn1_pre_g", "ffn1_post_g", "mix_pre_g", "mix_post_g", "ffn2_pre_g", "ffn2_post_g"]):
        g[:, i * 8:(i + 1) * 8] = np.asarray(inputs[k], np.float32)[0].reshape(8, 128).T
    common = {
        "gains": g,
        "ffn1_w_in": np.ascontiguousarray(inputs["ffn1_w_in"][0], dtype=np.float32),
        "ffn1_w_out": np.ascontiguousarray(inputs["ffn1_w_out"][0], dtype=np.float32),
        "ffn2_w_in": np.ascontiguousarray(inputs["ffn2_w_in"][0], dtype=np.float32),
        "ffn2_w_out": np.ascontiguousarray(inputs["ffn2_w_out"][0], dtype=np.float32),
    }
    slopes = 2.0 ** (-8.0 * np.arange(1, 9) / 8.0)
    cc = np.arange(128)[:, None]
    ii = np.arange(128)[None, :]
    atab = np.zeros((128, 24, 2, 128), np.float32)
    for h in range(8):
        for br, d in enumerate((1, 4, 16)):
            for hh in range(2):
                steps = 128 + ii - (hh * 128 + cc)
                valid = (steps >= 0) & (steps <= 128)
                atab[:, h * 3 + br, hh, :] = np.where(valid, -slopes[h] * d * steps * 8.0, -240000.0)
    btab_first = np.full((128, 24, 128), -240000.0, np.float32)
    btab_second = np.ascontiguousarray(atab[:, :, 0, :])
    common["w_mix_in"] = np.ascontiguousarray(inputs["w_mix_in"][0], dtype=np.float32)
    common["ident"] = np.eye(128, dtype=np.float32)
    f32 = np.float32
    a_re = np.asarray(inputs["a_re"], f32)[0]
    a_im = np.asarray(inputs["a_im"], f32)[0]
    ldt = np.asarray(inputs["log_dt"], f32)[0]
    ssm_a = np.zeros((128, 96), f32)
    ssm_a[:, 0:32] = np.tile(a_re.T, (2, 1))
    ssm_a[:, 32:64] = np.tile(a_im.T, (2, 1))
    ssm_a[:, 64:96] = np.tile(ldt[None, :], (128, 1))
    b_re = np.asarray(inputs["b_re"], f32)[0].transpose(1, 0, 2)
    b_im = np.asarray(inputs["b_im"], f32)[0].transpose(1, 0, 2)
    ssm_b = np.stack([np.concatenate([b_re, b_im], 0), np.concatenate([b_im, b_re], 0)], axis=1)
    c_re = np.asarray(inputs["c_re"], f32)[0].transpose(2, 0, 1)
    c_im = np.asarray(inputs["c_im"], f32)[0].transpose(2, 0, 1)
    ssm_c = np.stack([np.concatenate([c_re, c_im], 0), np.concatenate([c_im, c_re], 0)], axis=1)
    selm = np.zeros((128, 8, 8, 128), f32)
    for gm in range(8):
        for s_ in range(8):
            for c_ in range(16):
                selm[16 * gm + c_, gm, s_, 16 * s_ + c_] = 1.0
    selTm = np.ascontiguousarray(selm.transpose(3, 1, 2, 0))
    ss_, cc_ = np.arange(128) // 16, np.arange(128) % 16
    cmask = (ss_[None, :] >= ss_[:, None]).astype(f32)
    dsk = np.asarray(inputs["d_skip"], f32)[0]
    dstk = np.zeros((128, 32), f32)
    for g_ in range(32):
        dstk[:, g_] = dsk[16 * g_ + cc_]
    common.update({"sel": selm.reshape(128, -1), "selT": selTm.reshape(128, -1), "cmask": cmask, "dstk": dstk})
    ssm_v = np.zeros((128, 16), f32)
    ssm_v[:, 0:4] = np.asarray(inputs["d_skip"], f32)[0].reshape(4, 128).T
    ssm_v[:, 4:8] = np.asarray(inputs["b_glu"], f32)[0].reshape(4, 128).T
    ssm_v[:64, 8] = 1.0
    ssm_v[64:, 8] = -1.0
    ssm_v[:, 9] = -ssm_v[:, 8]
    rmask = np.zeros((128, 8), f32)
    for gm in range(8):
        rmask[16 * gm:16 * gm + 16, gm] = 1.0
    swapm = np.zeros((128, 128), f32)
    swapm[np.arange(128), (np.arange(128) + 64) % 128] = 1.0
    common.update({"ssm_a": ssm_a, "ssm_b": np.ascontiguousarray(ssm_b.reshape(128, -1)),
                   "ssm_c": np.ascontiguousarray(ssm_c.reshape(128, -1)), "ssm_v": ssm_v, "rmask": rmask, "swapm": swapm,
                   "w_glu": np.ascontiguousarray(inputs["w_glu"][0], dtype=f32)})
    common["w_mix_out"] = np.ascontiguousarray(inputs["w_mix_out"][0], dtype=np.float32)
    common["atab"] = atab.reshape(128, -1)
    in_maps = []
    for c in range(NCORES):
        b, hf = c // 2, c % 2
        xt = np.zeros((D, SEQ), np.float32)
        if hf == 0:
            xt[:, HALF:] = x[b, :HALF].T
        else:
            xt[:, :] = x[b].T
        m = dict(common)
        m["xT"] = xt
        m["btab"] = (btab_first if hf == 0 else btab_second).reshape(128, -1)
        in_maps.append(m)
    return in_maps


def kernel(**inputs):
    nc = build()
    in_maps = make_inputs(inputs)
    res = run_bass_kernel_spmd(nc, in_maps, core_ids=list(range(NCORES)))
    out = np.zeros((4, SEQ, D), np.float32)
    for c in range(NCORES):
        b, hf = c // 2, c % 2
        out[b, hf * HALF:(hf + 1) * HALF] = res.results[c]["outT"].T
    return out
```

```python
import numpy as np
import ml_dtypes
from contextlib import ExitStack
import concourse.bass as bass
import concourse.mybir as mybir
from concourse.bass_utils import run_bass_kernel_spmd

F32 = mybir.dt.float32
BF16 = mybir.dt.bfloat16
AF = mybir.ActivationFunctionType
ALU = mybir.AluOpType

D = 1024
DFF = 2816
NF = DFF // 128
SEQ = 8192
HALF = 4096
T = 512
NT_ALL = SEQ // T
NT_OWN = HALF // T
EPS = 1e-6
NCORES = 8


class Op:
    __slots__ = ("eng", "fn", "is_dma", "waits", "signaled", "idx", "count", "sem", "val", "is_nop")


class Prog:
    ENGS = ("pe", "act", "dve", "pool", "sp")

    def __init__(self, nc, stack):
        self.nc = nc
        self.stack = stack
        self.streams = {e: [] for e in self.ENGS}
        self.nops = {e: 0 for e in self.ENGS}
        self.nsig = {e: 0 for e in self.ENGS}
        self.esem = {e: stack.enter_context(nc.semaphore("sem_" + e)) for e in ("pe", "act", "dve", "pool")}
        self.dsem = {}
        self.dval = {}
        self.writers = {}
        self.readers = {}
        self.waited = {e: {} for e in self.ENGS}
        self.last = {}
        self.last_dma = {}
        self.defer = None
        self.deferred = []

    def _dma_sem(self, key):
        if key not in self.dsem:
            self.dsem[key] = self.stack.enter_context(self.nc.semaphore("dsem_%d" % len(self.dsem)))
            self.dval[key] = 0
        return self.dsem[key]

    def _dep(self, op, dep):
        if dep is op:
            return
        if dep.is_dma:
            tk = ("d", id(dep.sem))
            if self.waited[op.eng].get(tk, 0) >= dep.val:
                return
            self.waited[op.eng][tk] = dep.val
            op.waits.append(dep)
        else:
            if dep.eng == "pe" and op.eng == "pe" and not op.is_dma:
                return
            tk = ("e", dep.eng)
            if self.waited[op.eng].get(tk, -1) >= dep.idx:
                return
            self.waited[op.eng][tk] = dep.idx
            if dep.count is None:
                dep.signaled = True
            op.waits.append(dep)

    def op(self, eng, fn, r=(), w=(), dma=None):
        if self.defer is not None:
            self.defer.append((eng, fn, list(r), list(w), dma))
            return None
        o = Op()
        o.eng = eng
        o.fn = fn
        o.is_dma = dma is not None
        o.waits = []
        o.signaled = False
        o.idx = self.nops[eng]
        self.nops[eng] += 1
        o.count = None
        o.is_nop = False
        if o.is_dma:
            o.sem = self._dma_sem(dma)
            self.dval[dma] += 16
            o.val = self.dval[dma]
        for k in r:
            for d in self.writers.get(k, {}).values():
                self._dep(o, d)
        for k in w:
            for d in self.writers.get(k, {}).values():
                self._dep(o, d)
            for d in self.readers.get(k, {}).values():
                self._dep(o, d)
        tag = ("d", id(o.sem)) if o.is_dma else eng
        for k in r:
            self.readers.setdefault(k, {})[tag] = o
        for k in w:
            self.writers[k] = {tag: o}
            self.readers[k] = {}
        self.streams[eng].append(o)
        if o.is_dma:
            self.last_dma[id(o.sem)] = o
        else:
            self.last[eng] = o
        return o

    def replay(self, k):
        for _ in range(k):
            if not self.deferred:
                return
            eng, fn, r, w, dma = self.deferred.pop(0)
            self.op(eng, fn, r=r, w=w, dma=dma)

    def barrier(self):
        deps = [d for d in self.last.values() if not d.is_nop and d.eng != "sp"] + list(self.last_dma.values())
        for x in self.ENGS:
            o = Op()
            o.eng = x
            o.fn = lambda e: e.nop()
            o.is_dma = False
            o.is_nop = True
            o.waits = []
            o.signaled = False
            o.idx = self.nops[x]
            self.nops[x] += 1
            o.count = None
            for d in deps:
                self._dep(o, d)
            self.streams[x].append(o)

    def simulate(self):
        pos = {e: 0 for e in self.ENGS}
        done = set()
        progress = True
        while progress:
            progress = False
            for e in self.ENGS:
                st = self.streams[e]
                while pos[e] < len(st):
                    o = st[pos[e]]
                    if all((id(d) in done) or (d.count is not None and not d.is_dma and d not in self._cur) or
                           (d.is_dma and d not in self._cur) for d in o.waits):
                        done.add(id(o))
                        pos[e] += 1
                        progress = True
                    else:
                        break
        for e in self.ENGS:
            if pos[e] < len(self.streams[e]):
                o = self.streams[e][pos[e]]
                raise RuntimeError("deadlock: engine %s stuck at op %d/%d waiting on %s" % (
                    e, pos[e], len(self.streams[e]), [(d.eng, d.idx, d.is_dma) for d in o.waits if id(d) not in done]))

    def flush(self):
        nc = self.nc
        self._cur = set()
        for e in self.ENGS:
            self._cur.update(self.streams[e])
        self.simulate()
        for e in ("pe", "act", "dve", "pool"):
            c = self.nsig[e]
            pend = []
            comp = [o for o in self.streams[e] if not o.is_dma and not o.is_nop]
            if comp:
                comp[-1].signaled = True
            for o in self.streams[e]:
                if o.is_dma:
                    continue
                pend.append(o)
                if o.signaled:
                    c += 1
                    for p in pend:
                        p.count = c
                    pend = []
            self.nsig[e] = c
        streams = self.streams
        esem = self.esem

        def emit(eng_name, e):
            for o in streams[eng_name]:
                for d in o.waits:
                    if d.is_dma:
                        e.wait_ge(d.sem, d.val)
                    else:
                        assert d.count is not None
                        e.wait_ge(esem[d.eng], d.count)
                ins = o.fn(e)
                if o.is_nop:
                    continue
                if o.is_dma:
                    ins.then_inc(o.sem, 16)
                elif o.signaled:
                    ins.then_inc(esem[eng_name], 1)

        with nc.Block() as block:
            @block.tensor
            def _(e):
                emit("pe", e)

            @block.scalar
            def _(e):
                emit("act", e)

            @block.vector
            def _(e):
                emit("dve", e)

            @block.gpsimd
            def _(e):
                emit("pool", e)

            @block.sync
            def _(e):
                emit("sp", e)
        self.streams = {e: [] for e in self.ENGS}

    def final_wait(self, eng, ops):
        o = self.op(eng, lambda e: e.nop(), r=(), w=())
        o.is_nop = True
        for d in ops:
            self._dep(o, d)
        return o


def dview(t, c0, nchunks, t0, ntok):
    return t[c0 * 128:(c0 + nchunks) * 128, t0:t0 + ntok].rearrange("(c p) t -> p c t", p=128)


def build(debug=None):
    nc = bass.Bass("TRN2", target_bir_lowering=False)
    dt_ = nc.dram_tensor
    xT = dt_("xT", [D, SEQ], F32, kind="ExternalInput").ap()
    gains = dt_("gains", [128, 48], F32, kind="ExternalInput").ap()
    w1_in = dt_("ffn1_w_in", [D, 2 * DFF], F32, kind="ExternalInput").ap()
    w1_out = dt_("ffn1_w_out", [DFF, D], F32, kind="ExternalInput").ap()
    w2_in = dt_("ffn2_w_in", [D, 2 * DFF], F32, kind="ExternalInput").ap()
    w2_out = dt_("ffn2_w_out", [DFF, D], F32, kind="ExternalInput").ap()
    outT = dt_("outT", [D, HALF], F32, kind="ExternalOutput").ap()
    dbg_kind = "ExternalOutput" if debug else "Internal"
    w_mix_in = dt_("w_mix_in", [D, 2048], F32, kind="ExternalInput").ap()
    ident_d = dt_("ident", [128, 128], F32, kind="ExternalInput").ap()
    atab_d = dt_("atab", [128, 24 * 256], F32, kind="ExternalInput").ap()
    btab_d = dt_("btab", [128, 24 * 128], F32, kind="ExternalInput").ap()
    x1T = dt_("x1T", [D, HALF], F32, kind=dbg_kind).ap()
    qkvuT = dt_("qkvuT", [2048, SEQ], BF16, kind=dbg_kind).ap()
    catT = dt_("catT", [D, HALF], BF16, kind=dbg_kind).ap()
    x2T = dt_("x2T", [D, HALF], F32, kind=dbg_kind).ap()
    ssm_a_d = dt_("ssm_a", [128, 96], F32, kind="ExternalInput").ap()
    ssm_b_d = dt_("ssm_b", [128, 2 * 32 * 16], F32, kind="ExternalInput").ap()
    ssm_c_d = dt_("ssm_c", [128, 2 * 32 * 16], F32, kind="ExternalInput").ap()
    ssm_v_d = dt_("ssm_v", [128, 16], F32, kind="ExternalInput").ap()
    rmask_d = dt_("rmask", [128, 8], F32, kind="ExternalInput").ap()
    swap_d = dt_("swapm", [128, 128], F32, kind="ExternalInput").ap()
    w_glu = dt_("w_glu", [512, 512], F32, kind="ExternalInput").ap()
    sel_d = dt_("sel", [128, 64 * 128], F32, kind="ExternalInput").ap()
    selT_d = dt_("selT", [128, 64 * 128], F32, kind="ExternalInput").ap()
    cmask_d = dt_("cmask", [128, 128], F32, kind="ExternalInput").ap()
    dstk_d = dt_("dstk", [128, 32], F32, kind="ExternalInput").ap()
    w_mix_out = dt_("w_mix_out", [D, D], F32, kind="ExternalInput").ap()
    h2T = dt_("h2T", [D, SEQ], BF16, kind=("ExternalOutput" if debug else "Internal")).ap()

    with ExitStack() as gstack:
        P = Prog(nc, gstack)
        A = nc.alloc_sbuf_tensor
        ones = A("ones", [128, 128], BF16)
        gsb = A("gsb", [128, 48], F32)
        ghalf = A("ghalf", [128, 48], F32)
        P.op("pool", lambda e: e.memset(ones[:], 1.0), w=["ones"])
        P.op("sp", lambda e: e.dma_start(out=gsb[:], in_=gains), w=["gsb"], dma="c0")
        P.op("dve", lambda e: e.tensor_scalar(out=ghalf[:], in0=gsb[:], scalar1=0.5, scalar2=None, op0=ALU.mult),
             r=["gsb"], w=["ghalf"])
        G_F1PRE, G_F1POST, G_MIXPRE, G_MIXPOST, G_F2PRE, G_F2POST = range(6)

        def gcol(tile_, gi, c):
            return tile_[:, gi * 8 + c:gi * 8 + c + 1]

        def ffn_phase(name, src, tiles, w_in, w_out, g_pre, g_post, store_x, next_g, store_h):
            with ExitStack() as st:
                def S(nm, shape, dt):
                    return st.enter_context(nc.sbuf_tensor(name + nm, shape, dt))

                def PS(nm, shape):
                    return st.enter_context(nc.psum_tensor(name + nm, shape, F32))
                win = S("win", [128, 8, 2 * DFF], BF16)
                wout = S("wout", [128, NF, D], BF16)
                XA = S("xa", [128, 8, T], F32)
                hT = S("hT", [128, 8, T], BF16)
                act = S("act", [128, NF, T], BF16)
                sg = [S("sg%d" % i, [128, T], BF16) for i in range(2)]
                ysb = S("ysb", [128, 8, T], F32)
                sqj = [S("sq%d" % i, [128, T], BF16) for i in range(5)]
                rsA = S("rsA", [128, T], F32)
                rsB = S("rsB", [128, T], F32)
                h2c = [S("h2c%d" % i, [128, 1, T], BF16) for i in range(2)]
                mhalf = S("mhalf", [128, 1], F32)
                P.op("pool", lambda e: e.memset(mhalf[:], -0.5), w=["mhalf"])
                pG = [PS("pG%d" % i, [128, T]) for i in range(2)]
                pU = [PS("pU%d" % i, [128, T]) for i in range(2)]
                pY = [PS("pY%d" % i, [128, T]) for i in range(2)]
                pS0 = PS("pS0", [128, T])
                pS1 = PS("pS1", [128, T])

                fblocks = (2, 6, 7, 7)
                fstart = [sum(fblocks[:b]) for b in range(len(fblocks))]
                blk_of = [b for b, nb_ in enumerate(fblocks) for _ in range(nb_)]
                win_v = w_in.rearrange("(k p) f -> p k f", p=128)
                for b in range(len(fblocks)):
                    for half in range(2):
                        c0 = half * DFF + fstart[b] * 128
                        cw = fblocks[b] * 128
                        P.op("pool", (lambda e, c0=c0, cw=cw: e.dma_start(out=win[:, :, c0:c0 + cw], in_=win_v[:, :, c0:c0 + cw])),
                             w=[(name, "win", half, b)], dma=(name, "win", half, b))
                wout_v = w_out.rearrange("(f p) d -> p f d", p=128)
                for b in range(2):
                    P.op("pool", (lambda e, b=b: e.dma_start(out=wout[:, b * 11:(b + 1) * 11, :], in_=wout_v[:, b * 11:(b + 1) * 11, :])),
                         w=[(name, "wout", b)], dma=(name, "wout", b))
                nsq = [0]

                def stat_sq(src_ap, srckeys, ring="A", idx=None):
                    if ring == "A":
                        k = nsq[0] % 3
                        nsq[0] += 1
                        buf, key = sqj[k], ("sqj", k)
                    else:
                        buf, key = sqj[3 + idx % 2], ("sqj", 3 + idx % 2)
                    P.op("dve", (lambda e: e.tensor_tensor(out=buf[:], in0=src_ap, in1=src_ap, op=ALU.mult)),
                         r=srckeys, w=[key])
                    return (buf, key)

                def stat_mm(pS, pskey, bk, c):
                    buf, key = bk
                    P.op("pe", (lambda e: e.matmul(pS[:], lhsT=ones[:], rhs=buf[:], start=(c == 0), stop=(c == 7))),
                         r=[key, "ones"], w=[pskey])

                def rstd_step(pS, pskey, rs, rskey):
                    P.op("act", lambda e: e.activation(out=rs[:], in_=pS[:], func=AF.Sqrt, bias=EPS, scale=1.0 / D), r=[pskey], w=[rskey])
                    P.op("dve", lambda e: e.reciprocal(out=rs[:], in_=rs[:]), r=[rskey], w=[rskey])

                def stat_steps(src_fn, keys_fn, pS, pskey, lag):
                    st_ = []
                    ks = {}
                    for c in range(8 + lag):
                        def f_(c=c):
                            if c - lag >= 0:
                                stat_mm(pS, pskey, ks[c - lag], c - lag)
                            if c < 8:
                                ks[c] = stat_sq(src_fn(c), keys_fn(c))
                        st_.append(f_)
                    return st_

                def load_x(i):
                    s0 = tiles[i][0]
                    P.op("sp", (lambda e: e.dma_start(out=XA[:], in_=dview(src, 0, 8, s0, T))),
                         r=[("x2T", s0)] if name == "f2" else [], w=["xa"], dma=(name, "x"))

                def steps_N(i):
                    st_ = stat_steps(lambda c: XA[:, c, :], lambda c: ["xa"], pS0, "pS0", 2)
                    st_.append(lambda: rstd_step(pS0, "pS0", rsA, "rsA"))
                    for c in range(8):
                        st_.append(lambda c=c: P.op("dve", (lambda e: e.scalar_tensor_tensor(
                            out=hT[:, c, :], in0=XA[:, c, :], scalar=gcol(gsb, g_pre, c), in1=rsA[:],
                            op0=ALU.mult, op1=ALU.mult)), r=["xa", "rsA", "gsb"], w=[("hT", c)]))
                    return st_

                pend = {}

                def steps_R(i):
                    s0, d0, h0 = tiles[i]
                    st_ = []
                    nop_ = lambda: None
                    st_.append(lambda: pend.__setitem__(7, stat_sq(ysb[:, 7, :], [("ysb", 7)], "B", 7)))
                    st_.append(nop_)
                    st_.append(lambda: (stat_mm(pS1, "pS1", pend[6], 6), stat_mm(pS1, "pS1", pend[7], 7)))
                    st_.append(lambda: rstd_step(pS1, "pS1", rsB, "rsB"))
                    for c in range(8):
                        st_.append(lambda c=c: P.op("dve", (lambda e: e.scalar_tensor_tensor(
                            out=ysb[:, c, :], in0=ysb[:, c, :], scalar=gcol(ghalf, g_post, c), in1=rsB[:],
                            op0=ALU.mult, op1=ALU.mult)), r=[("ysb", c), "rsB", "ghalf"], w=[("ysb", c)]))
                    allk = [("ysb", c) for c in range(8)]
                    st_.append(lambda: P.op("pool", (lambda e: e.dma_start(out=ysb[:], in_=dview(src, 0, 8, s0, T), accum_op=ALU.add)),
                                            r=allk, w=allk, dma=(name, "xacc")))
                    if d0 is not None:
                        st_.append(lambda: P.op("pool", (lambda e: e.dma_start(out=dview(store_x, 0, 8, d0, T), in_=ysb[:])),
                                                r=allk, w=[(name, "dst", d0)], dma=(name, "st")))
                    if next_g is not None and h0 is not None:
                        st_ += [nop_] * 12
                        st_ += stat_steps(lambda c: ysb[:, c, :], lambda c: [("ysb", c)], pS0, "pS0", 2)
                        st_.append(lambda: rstd_step(pS0, "pS0", rsB, "rsB"))
                        for c in range(8):
                            def f_(c=c):
                                sl_ = c % 2
                                P.op("dve", (lambda e: e.scalar_tensor_tensor(
                                    out=h2c[sl_][:, 0, :], in0=ysb[:, c, :], scalar=gcol(gsb, next_g, c), in1=rsB[:],
                                    op0=ALU.mult, op1=ALU.mult)), r=[("ysb", c), "rsB", "gsb"], w=[("h2c", sl_)])
                                P.op("sp", (lambda e: e.dma_start(out=dview(store_h, c, 1, h0, T), in_=h2c[sl_][:])),
                                     r=[("h2c", sl_)], w=[("h2T", h0)], dma=(name, "sth", sl_))
                            st_.append(f_)
                    return st_

                def run_some(lst, k):
                    for _ in range(k):
                        if lst:
                            lst.pop(0)()

                n = len(tiles)
                load_x(0)
                run_some(steps_N(0), 99)
                for i in range(n):
                    if i + 1 < n:
                        load_x(i + 1)
                    side = steps_R(i - 1) if i >= 1 else []
                    per = 2 if side else 0
                    for f in range(NF):
                        pb = f % 2
                        blk = blk_of[f]
                        wk = [(name, "win", 0, blk), (name, "win", 1, blk)]
                        for c in range(8):
                            P.op("pe", (lambda e, c=c, f=f, pb=pb: e.matmul(
                                pG[pb][:], lhsT=win[:, c, f * 128:(f + 1) * 128], rhs=hT[:, c, :],
                                start=(c == 0), stop=(c == 7))), r=[("hT", c)] + wk, w=[("pG", pb)])
                        for c in range(8):
                            P.op("pe", (lambda e, c=c, f=f, pb=pb: e.matmul(
                                pU[pb][:], lhsT=win[:, c, DFF + f * 128:DFF + (f + 1) * 128], rhs=hT[:, c, :],
                                start=(c == 0), stop=(c == 7))), r=[("hT", c)] + wk, w=[("pU", pb)])
                        P.op("act", (lambda e, pb=pb: e.activation(out=sg[pb][:], in_=pG[pb][:], func=AF.Silu)),
                             r=[("pG", pb)], w=[("sg", pb)])
                        P.op("dve", (lambda e, pb=pb, f=f: e.tensor_tensor(
                            out=act[:, f, :], in0=sg[pb][:], in1=pU[pb][:], op=ALU.mult)),
                            r=[("sg", pb), ("pU", pb)], w=[("act", f)])
                        if f >= 1:
                            run_some(side, per)
                    run_some(side, 99)
                    side = steps_N(i + 1) if i + 1 < n else []
                    per = -(-len(side) // 7) if side else 0
                    for j in range(8):
                        pb = j % 2
                        for f in range(NF):
                            P.op("pe", (lambda e, j=j, f=f, pb=pb: e.matmul(
                                pY[pb][:], lhsT=wout[:, f, j * 128:(j + 1) * 128], rhs=act[:, f, :],
                                start=(f == 0), stop=(f == NF - 1))),
                                r=[("act", f), (name, "wout", f // 11)], w=[("pY", pb)])
                        P.op("act", (lambda e, j=j, pb=pb: e.activation(out=ysb[:, j, :], in_=pY[pb][:], func=AF.Copy)),
                             r=[("pY", pb)], w=[("ysb", j)])
                        if j >= 2:
                            stat_mm(pS1, "pS1", pend[j - 2], j - 2)
                        if j >= 1:
                            pend[j - 1] = stat_sq(ysb[:, j - 1, :], [("ysb", j - 1)], "B", j - 1)
                        if j >= 1:
                            run_some(side, per)
                    run_some(side, 99)
                run_some(steps_R(n - 1), 99)
                P.barrier()
                P.flush()

        def inproj_phase(ssmW=None):
            name = "ip"
            with ExitStack() as st:
                def S(nm, shape, dt):
                    return st.enter_context(nc.sbuf_tensor(name + nm, shape, dt))

                def PS(nm, shape):
                    return st.enter_context(nc.psum_tensor(name + nm, shape, F32))
                wmi = S("w", [128, 8, 2048], BF16)
                hin = [S("h%d" % i, [128, 8, T], BF16) for i in range(2)]
                stg = [S("stg%d" % i, [128, 16, T], BF16) for i in range(2)]
                pp = [PS("p%d" % i, [128, T]) for i in range(4)]
                psF = [PS("psF%d" % i, [128, T]) for i in range(3)]
                if ssmW is not None:
                    P.defer = []
                    ssm_gen(ssmW, st, psF)
                    P.deferred, P.defer = P.defer, None
                    per_rep = -(-len(P.deferred) // 150)
                wv = w_mix_in.rearrange("(k p) f -> p k f", p=128)
                for b in (3, 1, 2, 0):
                    P.op("pool", (lambda e, b=b: e.dma_start(out=wmi[:, :, b * 512:(b + 1) * 512], in_=wv[:, :, b * 512:(b + 1) * 512])),
                         w=[("wmi", b)], dma=("ip", "w", b))
                n = 0
                for ti in range(NT_ALL):
                    slot = ti % 2
                    P.op("sp", (lambda e, slot=slot, ti=ti: e.dma_start(out=hin[slot][:], in_=dview(h2T, 0, 8, ti * T, T))),
                         r=[("h2T", ti * T)], w=[("hin", slot)], dma=("ip", "h", slot))
                    c_lo = 0 if ti >= NT_OWN else (4 if ti >= 4 else 12)
                    for cc in range(c_lo, 16):
                        pb = n % 4
                        for k in range(8):
                            P.op("pe", (lambda e, k=k, cc=cc, pb=pb, slot=slot: e.matmul(
                                pp[pb][:], lhsT=wmi[:, k, cc * 128:(cc + 1) * 128], rhs=hin[slot][:, k, :],
                                start=(k == 0), stop=(k == 7))), r=[("hin", slot), ("wmi", cc // 4)], w=[("ipp", pb)])
                        if cc >= 12:
                            o_ap = stg[slot][:, cc, :].rearrange("p (s j) -> p s j", s=8)
                            i_ap = pp[pb][:].rearrange("p (j s) -> p s j", s=8)
                        else:
                            o_ap = stg[slot][:, cc, :]
                            i_ap = pp[pb][:]
                        if n % 2 == 0:
                            P.op("act", (lambda e, o_ap=o_ap, i_ap=i_ap: e.activation(out=o_ap, in_=i_ap, func=AF.Copy)),
                                 r=[("ipp", pb)], w=[("stg", slot, cc)])
                        else:
                            P.op("dve", (lambda e, o_ap=o_ap, i_ap=i_ap: e.tensor_copy(out=o_ap, in_=i_ap)),
                                 r=[("ipp", pb)], w=[("stg", slot, cc)])
                        n += 1
                        if ssmW is not None:
                            P.replay(per_rep)
                    P.op("act", (lambda e, slot=slot, ti=ti, c_lo=c_lo: e.dma_start(
                        out=dview(qkvuT, c_lo, 16 - c_lo, ti * T, T), in_=stg[slot][:, c_lo:16, :])),
                        r=[("stg", slot, cc) for cc in range(c_lo, 16)], w=[("qkvu", ti)], dma=("ip", "st", slot))
                P.replay(1 << 30)
                P.barrier()
                P.flush()

        def attn_phase():
            name = "at"
            KW = SEQ - 2048
            with ExitStack() as st:
                def S(nm, shape, dt):
                    return st.enter_context(nc.sbuf_tensor(name + nm, shape, dt))
                ident = S("ident", [128, 128], BF16)
                atab = S("atab", [128, 24, 2, 128], F32)
                btab = S("btab", [128, 24, 128], F32)
                qT = S("qT", [128, HALF], BF16)
                kT = S("kT", [128, KW], BF16)
                vT = S("vT", [128, KW], BF16)
                vtoks = [S("vtok%d" % i, [128, 48, 2, 128], BF16) for i in range(2)]
                acc = S("acc", [128, 2, HALF], F32)
                sb = [S("sb%d" % i, [128, 4, 128], F32) for i in range(3)]
                pT = [S("pT%d" % i, [128, 4, 128], BF16) for i in range(3)]
                rd = S("rd", [128, T], F32)
                ao = S("ao", [128, HALF], BF16)
                psS = [st.enter_context(nc.psum_tensor(name + "s%d" % i, [128, 4, 128], F32)) for i in range(3)]
                psO = [st.enter_context(nc.psum_tensor(name + "o%d" % i, [128, 512], F32)) for i in range(2)]
                psT = [st.enter_context(nc.psum_tensor(name + "t%d" % i, [128, 8, 128], BF16)) for i in range(2)]

                P.op("pool", lambda e: e.dma_start(out=ident[:], in_=ident_d), w=["ident"], dma=("at", "c"))
                P.op("sp", lambda e: e.dma_start(out=atab[:], in_=atab_d.rearrange("p (a b c) -> p a b c", a=24, b=2)), w=["atab"], dma=("at", "c1"))
                P.op("sp", lambda e: e.dma_start(out=btab[:], in_=btab_d.rearrange("p (a c) -> p a c", a=24)), w=["btab"], dma=("at", "c2"))
                for vt_ in vtoks:
                    P.op("pool", (lambda e, vt_=vt_: e.memset(vt_[:, :, 0, 64:128], 1.0)), w=["vones"])
                    P.op("pool", (lambda e, vt_=vt_: e.memset(vt_[:, :, 1, 0:64], 1.0)), w=["vones"])
                allq = [("qkvu", ti) for ti in range(NT_ALL)]
                nq = 0
                for hp in range(4):
                    P.op("sp", (lambda e, hp=hp: e.dma_start(out=qT[:], in_=qkvuT[hp * 128:(hp + 1) * 128, HALF:SEQ])),
                         r=allq, w=["qT"], dma=("at", "q"))
                    P.op("sp", (lambda e, hp=hp: e.dma_start(out=kT[:], in_=qkvuT[512 + hp * 128:512 + (hp + 1) * 128, 2048:SEQ])),
                         r=allq, w=["kT"], dma=("at", "k"))
                    def load_vT(hq):
                        P.op("sp", (lambda e: e.dma_start(out=vT[:], in_=qkvuT[1024 + hq * 128:1024 + (hq + 1) * 128, 2048:SEQ])),
                             r=allq, w=["vT"], dma=("at", "v"))
                    if hp == 0:
                        load_vT(0)
                    def vbuild_steps(br_, hp=hp):
                        d_ = (1, 4, 16)[br_]
                        nblk_ = 48 // d_
                        vb_ = (hp * 3 + br_) % 2
                        vt_ = vtoks[vb_]
                        steps_ = []
                        for g4 in range(12):
                            def f_(g4=g4):
                                tb = g4 % 2
                                for j in range(4):
                                    blk = g4 * 4 + j
                                    r_, n_ = blk // nblk_, blk % nblk_
                                    s0 = r_ + d_ * 128 * n_
                                    P.op("pe", (lambda e, j=j, s0=s0: e.transpose(
                                        psT[tb][:, j, :], vT[:, s0:s0 + 127 * d_ + 1:d_], ident[:])),
                                        r=["vT", "ident"], w=[("psT", tb)])
                                P.op("act", (lambda e: e.activation(
                                    out=vt_[:, g4 * 4:g4 * 4 + 4, 0, 0:64], in_=psT[tb][:, 0:4, 0:64], func=AF.Copy)),
                                    r=[("psT", tb)], w=[("vtok", vb_, g4, 0)])
                                P.op("act", (lambda e: e.activation(
                                    out=vt_[:, g4 * 4:g4 * 4 + 4, 1, 64:128], in_=psT[tb][:, 0:4, 64:128], func=AF.Copy)),
                                    r=[("psT", tb)], w=[("vtok", vb_, g4, 1)])
                            steps_.append(f_)
                        return steps_
                    if hp == 0:
                        for f_ in vbuild_steps(0):
                            f_()
                    for br, d in enumerate((1, 4, 16)):
                        nblk = 48 // d
                        n0 = 16 // d
                        vbi = (hp * 3 + br) % 2
                        vtok = vtoks[vbi]
                        if br < 2:
                            vnext = vbuild_steps(br + 1)
                        elif hp + 1 < 4:
                            load_vT(hp + 1)
                            vnext = vbuild_steps(0, hp + 1)
                        else:
                            vnext = []
                        pairs = [(h2, r_, n_) for h2 in range(2) for r_ in range(d) for n_ in range(n0, nblk, 2)]
                        LAG = 2

                        def stage_a(idx, h2, r_, n_, d=d, br=br, hp=hp, nblk=nblk, n0=n0):
                            rows = slice(h2 * 64, h2 * 64 + 64)
                            tix = (hp * 2 + h2) * 3 + br
                            sbi = idx % 3
                            for b in range(2):
                                kb = r_ + d * 128 * (n_ + b - 1)
                                qb = r_ + d * 128 * (n_ + b) - 2048
                                for hh in range(2):
                                    k0 = kb + hh * 128 * d
                                    P.op("pe", (lambda e, hh=hh, k0=k0, qb=qb, b=b: e.matmul(
                                        psS[sbi][:, 2 * b + hh, :], lhsT=kT[rows, k0:k0 + 127 * d + 1:d], rhs=qT[rows, qb:qb + 127 * d + 1:d],
                                        start=True, stop=True)), r=["kT", "qT"], w=[("psS", sbi)])
                            if n_ == n0:
                                P.op("dve", (lambda e: e.tensor_tensor(
                                    out=sb[sbi][:, 0, :], in0=psS[sbi][:, 0, :], in1=btab[:, tix, :], op=ALU.add)),
                                    r=[("psS", sbi), "btab"], w=[("sb", sbi)])
                                P.op("dve", (lambda e: e.tensor_tensor(
                                    out=sb[sbi][:, 1, :], in0=psS[sbi][:, 1, :], in1=atab[:, tix, 1, :], op=ALU.add)),
                                    r=[("psS", sbi), "atab"], w=[("sb", sbi)])
                                P.op("dve", (lambda e: e.tensor_tensor(
                                    out=sb[sbi][:, 2:4, :], in0=psS[sbi][:, 2:4, :], in1=atab[:, tix, :, :], op=ALU.add)),
                                    r=[("psS", sbi), "atab"], w=[("sb", sbi)])
                            else:
                                tb2 = atab[:, tix, :, :].rearrange("p a b -> p (a b)").unsqueeze(1).to_broadcast([128, 2, 256])
                                P.op("dve", (lambda e: e.tensor_tensor(
                                    out=sb[sbi][:].rearrange("p (x a) b -> p x (a b)", x=2),
                                    in0=psS[sbi][:].rearrange("p (x a) b -> p x (a b)", x=2), in1=tb2, op=ALU.add)),
                                    r=[("psS", sbi), "atab"], w=[("sb", sbi)])
                            P.op("act", (lambda e: e.activation(out=pT[sbi][:], in_=sb[sbi][:], func=AF.Exp, scale=0.125)),
                                 r=[("sb", sbi)], w=[("pT", sbi)])

                        def stage_b(idx, h2, r_, n_, d=d, br=br, hp=hp, nblk=nblk, n0=n0, vtok=vtok, vbi=vbi):
                            sbi = idx % 3
                            ob = idx % 2
                            for b in range(2):
                                blk = r_ * nblk + n_ + b
                                for hh in range(2):
                                    vb = blk - 1 + hh
                                    P.op("pe", (lambda e, hh=hh, vb=vb, b=b: e.matmul(
                                        psO[ob][:, b * 128:(b + 1) * 128], lhsT=vtok[:, vb, h2, :], rhs=pT[sbi][:, 2 * b + hh, :],
                                        start=(hh == 0), stop=(hh == 1))),
                                        r=[("pT", sbi), ("vtok", vbi, vb // 4, h2), "vones"], w=[("psO", ob)])
                            qb = r_ + d * 128 * n_ - 2048
                            asl = acc[:, h2, qb:qb + 255 * d + 1:d]
                            if br == 0:
                                P.op("act", (lambda e: e.activation(out=asl, in_=psO[ob][:, 0:256], func=AF.Copy)),
                                     r=[("psO", ob)], w=[("acc", h2)])
                            else:
                                P.op("dve", (lambda e: e.tensor_tensor(out=asl, in0=asl, in1=psO[ob][:, 0:256], op=ALU.add)),
                                     r=[("psO", ob), ("acc", h2)], w=[("acc", h2)])
                        for idx in range(len(pairs) + LAG):
                            if idx < len(pairs):
                                stage_a(idx, *pairs[idx])
                            if idx - LAG >= 0:
                                stage_b(idx - LAG, *pairs[idx - LAG])
                            if idx % 2 == 1 and vnext:
                                vnext.pop(0)()
                        while vnext:
                            vnext.pop(0)()
                    for tt in range(NT_OWN):
                        ts_ = slice(tt * T, (tt + 1) * T)
                        P.op("dve", (lambda e, ts_=ts_: e.reciprocal(out=rd[0:64, :], in_=acc[64:128, 0, ts_])),
                             r=[("acc", 0), "ao"], w=["rd0"])
                        P.op("dve", (lambda e, ts_=ts_: e.tensor_tensor(out=ao[0:64, ts_], in0=acc[0:64, 0, ts_], in1=rd[0:64, :], op=ALU.mult)),
                             r=[("acc", 0), "rd0"], w=["ao"])
                        P.op("dve", (lambda e, ts_=ts_: e.reciprocal(out=rd[64:128, :], in_=acc[0:64, 1, ts_])),
                             r=[("acc", 1), "ao"], w=["rd1"])
                        P.op("dve", (lambda e, ts_=ts_: e.tensor_tensor(out=ao[64:128, ts_], in0=acc[64:128, 1, ts_], in1=rd[64:128, :], op=ALU.mult)),
                             r=[("acc", 1), "rd1"], w=["ao"])
                    P.op("act", (lambda e, hp=hp: e.dma_start(out=catT[hp * 128:(hp + 1) * 128, :], in_=ao[:])),
                         r=["ao"], w=[("cat", hp)], dma=("at", "st"))
                P.barrier()
                P.flush()

        NI = 10
        PI_ = float(np.pi)
        NJC = SEQ // 8
        GK = "ssgen"

        def ssm_persist(st):
            def S(nm, shape, dt):
                return st.enter_context(nc.sbuf_tensor("s2" + nm, shape, dt))
            sv = S("sv", [128, 16], F32)
            swp = S("swp", [128, 128], F32)
            idf = S("idf", [128, 128], F32)
            idb = S("idb", [128, 128], BF16)
            wglu = S("wglu", [128, 4, 512], BF16)
            Bdec = S("Bdec", [128, 32, 128], BF16)
            Cdec = S("Cdec", [128, 32, 128], BF16)
            Toep = S("Toep", [128, 32, 128], BF16)
            PR8 = S("PR8", [128, NI, 32], F32)
            PI8 = S("PI8", [128, NI, 32], F32)
            SPI8 = S("SPI8", [128, NI, 32], F32)

            def ld(dst, src_, key, eng="sp"):
                P.op(eng, (lambda e: e.dma_start(out=dst, in_=src_)), w=[key], dma=("s2", key))
            ld(sv[:], ssm_v_d, "sv")
            ld(swp[:], swap_d, "swp")
            ld(idf[:], ident_d, "idf")
            ld(idb[:], ident_d, "idb", "pool")
            ld(wglu[:], w_glu.rearrange("(k p) f -> p k f", p=128), "wglu", "pool")
            return (sv, swp, idf, idb, wglu, Bdec, Cdec, Toep, PR8, PI8, SPI8)

        def ssm_gen(W, st2, psF):
            name = "s2"
            sv, swp, idf, idb, wglu, Bdec, Cdec, Toep, PR8, PI8, SPI8 = W
            sgnA, sgnB = sv[:, 8:9], sv[:, 9:10]

            def ld(dst, src_, key, eng="sp"):
                P.op(eng, (lambda e: e.dma_start(out=dst, in_=src_)), w=[key], dma=("s2", key))

            def dve(fn, r=(GK,), w=(GK,)):
                P.op("dve", fn, r=list(r), w=list(w))

            def actf(fn, r=(GK,), w=(GK,)):
                P.op("act", fn, r=list(r), w=list(w))
            if True:
                if True:
                    def S2(nm, shape, dt):
                        return st2.enter_context(nc.sbuf_tensor(name + nm, shape, dt))
                    prm = S2("prm", [128, 96], F32)
                    bab = S2("bab", [128, 2, 32, 16], F32)
                    cab = S2("cab", [128, 2, 32, 16], F32)
                    cmask = S2("cmask", [128, 128], F32)
                    dstk = S2("dstk", [128, 32], F32)
                    tmpv = [S2("tv%d" % i, [128, 32], F32) for i in range(12)]
                    POWr = S2("POWr", [128, 16, 32], F32)
                    POWi = S2("POWi", [128, 16, 32], F32)
                    Qr = S2("Qr", [128, 32, 8], F32)
                    Qi = S2("Qi", [128, 32, 8], F32)
                    Q2r = S2("Q2r", [128, 32, 8], F32)
                    Q2i = S2("Q2i", [128, 32, 8], F32)
                    BdT = S2("BdT", [128, 32, 8, 16], F32)
                    big = S2("big", [128, 32, 9, 16], F32)
                    VV = S2("VV", [128, 32, 9, 16], F32)
                    rt = S2("rt", [128, 128], F32)
                    ld(prm[:], ssm_a_d, "prm")
                    ld(bab[:], ssm_b_d.rearrange("p (a g c) -> p a g c", a=2, g=32), "bab")
                    ld(cab[:], ssm_c_d.rearrange("p (a g c) -> p a g c", a=2, g=32), "cab")
                    ld(cmask[:], cmask_d, "cmask")
                    ld(dstk[:], dstk_d, "dstk")
                    are, aim, ldt = prm[:, 0:32], prm[:, 32:64], prm[:, 64:96]
                    dt_, lr, li, mag, angs, angc, m_, t1, t2, t3, wr, wi = [t[:] for t in tmpv]
                    actf(lambda e: e.activation(out=dt_, in_=ldt, func=AF.Exp), r=("prm", GK))
                    dve(lambda e: e.tensor_tensor(out=lr, in0=dt_, in1=are, op=ALU.mult), r=("prm", GK))
                    dve(lambda e: e.tensor_tensor(out=li, in0=dt_, in1=aim, op=ALU.mult))
                    actf(lambda e: e.activation(out=mag, in_=lr, func=AF.Exp))
                    dve(lambda e: e.tensor_copy(out=angs, in_=li))
                    dve(lambda e: e.tensor_scalar(out=angc, in0=li, scalar1=PI_ / 2, scalar2=None, op0=ALU.add))
                    for it in range(4):
                        for ang in (angs, angc):
                            dve(lambda e, ang=ang: e.tensor_scalar(out=m_, in0=ang, scalar1=PI_, scalar2=2 * PI_, op0=ALU.is_gt, op1=ALU.mult))
                            dve(lambda e, ang=ang: e.tensor_tensor(out=ang, in0=ang, in1=m_, op=ALU.subtract))
                    actf(lambda e: e.activation(out=angs, in_=angs, func=AF.Sin))
                    actf(lambda e: e.activation(out=angc, in_=angc, func=AF.Sin))
                    K0 = 7

                    def pw(k):
                        return POWr[:, K0 + k, :], POWi[:, K0 + k, :]
                    dve(lambda e: e.memset(POWr[:, K0, :], 1.0))
                    dve(lambda e: e.memset(POWi[:, K0, :], 0.0))
                    dve(lambda e: e.tensor_tensor(out=pw(1)[0], in0=mag, in1=angc, op=ALU.mult))
                    dve(lambda e: e.tensor_tensor(out=pw(1)[1], in0=mag, in1=angs, op=ALU.mult))

                    def cmul(zr, zi, xr, xi, yr, yi):
                        dve(lambda e: e.tensor_tensor(out=t1, in0=xr, in1=yr, op=ALU.mult))
                        dve(lambda e: e.tensor_tensor(out=t2, in0=xi, in1=yi, op=ALU.mult))
                        dve(lambda e: e.tensor_tensor(out=zr, in0=t1, in1=t2, op=ALU.subtract))
                        dve(lambda e: e.tensor_tensor(out=t1, in0=xr, in1=yi, op=ALU.mult))
                        dve(lambda e: e.tensor_tensor(out=t2, in0=xi, in1=yr, op=ALU.mult))
                        dve(lambda e: e.tensor_tensor(out=zi, in0=t1, in1=t2, op=ALU.add))
                    for k in range(2, 9):
                        cmul(*pw(k), *pw(k - 1), *pw(1))
                    dve(lambda e: e.tensor_tensor(out=t1, in0=pw(1)[0], in1=pw(1)[0], op=ALU.mult))
                    dve(lambda e: e.tensor_tensor(out=t2, in0=pw(1)[1], in1=pw(1)[1], op=ALU.mult))
                    dve(lambda e: e.tensor_tensor(out=t3, in0=t1, in1=t2, op=ALU.add))
                    dve(lambda e: e.reciprocal(out=t3, in_=t3))
                    dve(lambda e: e.tensor_tensor(out=pw(-1)[0], in0=pw(1)[0], in1=t3, op=ALU.mult))
                    dve(lambda e: e.scalar_tensor_tensor(out=pw(-1)[1], in0=pw(1)[1], scalar=-1.0, in1=t3, op0=ALU.mult, op1=ALU.mult))
                    for k in range(-2, -8, -1):
                        cmul(*pw(k), *pw(k + 1), *pw(-1))
                    dve(lambda e: e.tensor_scalar(out=m_, in0=pw(1)[0], scalar1=-1.0, scalar2=None, op0=ALU.add))
                    dve(lambda e: e.tensor_tensor(out=t1, in0=are, in1=are, op=ALU.mult))
                    dve(lambda e: e.tensor_tensor(out=t2, in0=aim, in1=aim, op=ALU.mult))
                    dve(lambda e: e.tensor_tensor(out=t3, in0=t1, in1=t2, op=ALU.add))
                    dve(lambda e: e.reciprocal(out=t3, in_=t3))
                    dve(lambda e: e.tensor_tensor(out=wr, in0=m_, in1=are, op=ALU.mult))
                    dve(lambda e: e.tensor_tensor(out=t1, in0=pw(1)[1], in1=aim, op=ALU.mult))
                    dve(lambda e: e.tensor_tensor(out=wr, in0=wr, in1=t1, op=ALU.add))
                    dve(lambda e: e.tensor_tensor(out=wr, in0=wr, in1=t3, op=ALU.mult))
                    dve(lambda e: e.tensor_tensor(out=wi, in0=pw(1)[1], in1=are, op=ALU.mult))
                    dve(lambda e: e.tensor_tensor(out=t1, in0=m_, in1=aim, op=ALU.mult))
                    dve(lambda e: e.tensor_tensor(out=wi, in0=wi, in1=t1, op=ALU.subtract))
                    dve(lambda e: e.tensor_tensor(out=wi, in0=wi, in1=t3, op=ALU.mult))
                    for s_ in range(8):
                        cmul(Qr[:, :, s_], Qi[:, :, s_], *pw(7 - s_), wr, wi)
                        cmul(Q2r[:, :, s_], Q2i[:, :, s_], *pw(-s_), wr, wi)
                    SH = [128, 32, 8, 16]
                    BAb = bab[:, 0, :, :].unsqueeze(2).to_broadcast(SH)
                    BBb = bab[:, 1, :, :].unsqueeze(2).to_broadcast(SH)
                    def bq(dst, qr_, qi_):
                        qrb = qr_[:].unsqueeze(3).to_broadcast(SH)
                        qib = qi_[:].unsqueeze(3).to_broadcast(SH)
                        dve(lambda e: e.tensor_tensor(out=big[:, :, 0:8, :], in0=BBb, in1=qib, op=ALU.mult), r=("bab", GK))
                        dve(lambda e: e.tensor_scalar(out=big[:, :, 0:8, :], in0=big[:, :, 0:8, :], scalar1=sgnB, scalar2=None, op0=ALU.mult), r=("sv", GK))
                        dve(lambda e: e.tensor_tensor(out=dst[:], in0=BAb, in1=qrb, op=ALU.mult), r=("bab", GK))
                        dve(lambda e: e.tensor_tensor(out=dst[:], in0=dst[:], in1=big[:, :, 0:8, :], op=ALU.add))
                    bq(BdT, Qr, Qi)
                    for g in range(32):
                        pf = psF[g % 3]
                        P.op("pe", (lambda e, g=g, pf=pf: e.transpose(pf[:, 0:128], BdT[:, g, :, :].rearrange("p s c -> p (s c)"), idf[:])),
                             r=[GK, "idf"], w=[("psF", g % 3)])
                        P.op("act", (lambda e, g=g, pf=pf: e.activation(out=Bdec[:, g, :], in_=pf[:, 0:128], func=AF.Copy)),
                             r=[("psF", g % 3)], w=["Bdec"])
                    UTp = BdT
                    bq(UTp, Q2r, Q2i)
                    SH9 = [128, 32, 9, 16]
                    CAb = cab[:, 0, :, :].unsqueeze(2).to_broadcast(SH9)
                    CBb = cab[:, 1, :, :].unsqueeze(2).to_broadcast(SH9)
                    prb = POWr[:, K0:K0 + 9, :].rearrange("p k g -> p g k").unsqueeze(3).to_broadcast(SH9)
                    pib = POWi[:, K0:K0 + 9, :].rearrange("p k g -> p g k").unsqueeze(3).to_broadcast(SH9)
                    dve(lambda e: e.tensor_tensor(out=VV[:], in0=CAb, in1=prb, op=ALU.mult), r=("cab", GK))
                    dve(lambda e: e.tensor_scalar(out=VV[:], in0=VV[:], scalar1=sgnA, scalar2=None, op0=ALU.mult), r=("sv", GK))
                    dve(lambda e: e.tensor_tensor(out=big[:], in0=CBb, in1=pib, op=ALU.mult), r=("cab", GK))
                    dve(lambda e: e.tensor_tensor(out=VV[:], in0=VV[:], in1=big[:], op=ALU.subtract))
                    dve(lambda e: e.tensor_copy(out=Cdec[:].rearrange("p g (t c) -> p g t c", t=8), in_=VV[:, :, 1:9, :]), w=(GK, "Cdec"))
                    for g in range(32):
                        pf = psF[g % 3]
                        P.op("pe", (lambda e, g=g, pf=pf: e.matmul(pf[:, 128:256], lhsT=UTp[:, g, :, :].rearrange("p s c -> p (s c)"),
                                                                   rhs=VV[:, g, 0:8, :].rearrange("p t c -> p (t c)"), start=True, stop=True)),
                             r=[GK, "Bdec"], w=[("psF", g % 3)])
                        P.op("dve", (lambda e, g=g, pf=pf: e.tensor_tensor(out=rt[:], in0=pf[:, 128:256], in1=cmask[:], op=ALU.mult)),
                             r=[("psF", g % 3), "cmask"], w=["rt"])
                        P.op("dve", (lambda e, g=g: e.scalar_tensor_tensor(out=Toep[:, g, :], in0=idf[:], scalar=dstk[:, g:g + 1], in1=rt[:],
                                                                           op0=ALU.mult, op1=ALU.add)),
                             r=["rt", "idf", "dstk"], w=["Toep"])
                    dve(lambda e: e.tensor_copy(out=PR8[:, 0, :], in_=pw(8)[0]))
                    dve(lambda e: e.tensor_copy(out=PI8[:, 0, :], in_=pw(8)[1]))
                    for i in range(1, NI):
                        dve(lambda e, i=i: e.tensor_tensor(out=t1, in0=PR8[:, i - 1, :], in1=PR8[:, i - 1, :], op=ALU.mult))
                        dve(lambda e, i=i: e.tensor_tensor(out=t2, in0=PI8[:, i - 1, :], in1=PI8[:, i - 1, :], op=ALU.mult))
                        dve(lambda e, i=i: e.tensor_tensor(out=PR8[:, i, :], in0=t1, in1=t2, op=ALU.subtract))
                        dve(lambda e, i=i: e.scalar_tensor_tensor(out=PI8[:, i, :], in0=PR8[:, i - 1, :], scalar=2.0, in1=PI8[:, i - 1, :],
                                                                  op0=ALU.mult, op1=ALU.mult))
                    dve(lambda e: e.tensor_scalar(out=SPI8[:], in0=PI8[:], scalar1=sgnA, scalar2=None, op0=ALU.mult), r=("sv", GK))

        def ssm_phase(W):
            name = "s2"
            sv, swp, idf, idb, wglu, Bdec, Cdec, Toep, PR8, PI8, SPI8 = W
            with ExitStack() as st:
                def S(nm, shape, dt):
                    return st.enter_context(nc.sbuf_tensor(name + nm, shape, dt))

                def PS(nm, shape, dt=F32):
                    return st.enter_context(nc.psum_tensor(name + nm, shape, dt))
                sel = S("sel", [128, 64, 128], BF16)
                selT = S("selT", [128, 64, 128], BF16)
                psA = [PS("a%d" % i, [128, T]) for i in range(4)]
                psY = [PS("y%d" % i, [128, T]) for i in range(2)]
                P.op("pool", (lambda e: e.dma_start(out=sel[:], in_=sel_d.rearrange("p (a b) -> p a b", a=64))), w=["sel"], dma=("s2", "sel"))
                P.op("pool", (lambda e: e.dma_start(out=selT[:], in_=selT_d.rearrange("p (a b) -> p a b", a=64))), w=["selT"], dma=("s2", "selT"))
                Rq2 = [S("Rq%d" % i, [128, 8, NI, 128], BF16) for i in range(2)]
                rtmp = [S("rtmp%d" % i, [128, 128], F32) for i in range(2)]
                uT = S("uT", [128, SEQ], BF16)
                U1 = [S("U1%d" % i, [128, NJC], BF16) for i in range(4)]
                Hs = [[S("Hs%d_%d" % (a_, i), [128, NJC], BF16) for i in range(2)] for a_ in range(4)]
                g1 = [S("g1%d" % i, [128, T], F32) for i in range(2)]
                Yg = S("Yg", [128, 8, T], BF16)
                yg = S("yg", [128, 4, HALF], BF16)
                so = [S("so%d" % i, [128, 4, T], BF16) for i in range(2)]
                sgm = [S("sgm%d" % i, [128, T], F32) for i in range(2)]
                allq = [("qkvu", ti) for ti in range(NT_ALL)]
                nev = 0
                nrt = 0
                ngr = 0

                def evac(pb, dst, keys_w):
                    if pb % 2 == 0:
                        P.op("act", (lambda e: e.activation(out=dst, in_=psA[pb][:], func=AF.Copy)), r=[("psA", pb)], w=keys_w)
                    else:
                        P.op("dve", (lambda e: e.tensor_copy(out=dst, in_=psA[pb][:])), r=[("psA", pb)], w=keys_w)
                def gen_R(q):
                    nonlocal nrt
                    Rq_ = Rq2[q % 2]
                    for gm in range(8):
                        g = 8 * q + gm
                        for i in range(NI):
                            rb = nrt % 2
                            nrt += 1
                            P.op("dve", (lambda e, rb=rb, i=i, g=g: e.tensor_scalar(
                                out=rtmp[rb][:], in0=swp[:], scalar1=SPI8[:, i, g:g + 1], scalar2=None, op0=ALU.mult)),
                                r=[GK, "swp"], w=[("rtmp", rb)])
                            P.op("dve", (lambda e, rb=rb, i=i, g=g, gm=gm, Rq_=Rq_: e.scalar_tensor_tensor(
                                out=Rq_[:, gm, i, :], in0=idf[:], scalar=PR8[:, i, g:g + 1], in1=rtmp[rb][:],
                                op0=ALU.mult, op1=ALU.add)), r=[GK, "idf", ("rtmp", rb)], w=[("R", q % 2, gm)])
                gen_R(0)
                for q in range(4):
                    Rq = Rq2[q % 2]
                    P.op("sp", (lambda e, q=q: e.dma_start(out=uT[:], in_=qkvuT[1536 + q * 128:1536 + (q + 1) * 128, :])),
                         r=allq, w=["uT"], dma=("s2", "u"))
                    for pr_ in range(2):
                        if pr_ == 1 and q + 1 < 4:
                            gen_R(q + 1)
                        gms = tuple(range(4 * pr_, 4 * pr_ + 4))
                        for ab, gm in enumerate(gms):
                            for hf in range(2):
                                pb = nev % 4
                                nev += 1
                                for s_ in range(8):
                                    P.op("pe", (lambda e, pb=pb, gm=gm, s_=s_, hf=hf: e.matmul(
                                        psA[pb][:], lhsT=sel[:, gm * 8 + s_, :],
                                        rhs=uT[:].rearrange("p (t s j) -> p t s j", s=8, j=64)[:, hf * 8:(hf + 1) * 8, s_, :],
                                        start=(s_ == 0), stop=(s_ == 7))), r=["uT", "sel"], w=[("psA", pb)])
                                evac(pb, U1[ab][:, hf * T:(hf + 1) * T], [("U1", ab, hf)])
                        cur = 0
                        for ab, gm in enumerate(gms):
                            g = 8 * q + gm
                            for hf in range(2):
                                pb = nev % 4
                                nev += 1
                                P.op("pe", (lambda e, pb=pb, g=g, ab=ab, hf=hf: e.matmul(
                                    psA[pb][:], lhsT=Bdec[:, g, :], rhs=U1[ab][:, hf * T:(hf + 1) * T], start=True, stop=True)),
                                    r=[("U1", ab, hf), "Bdec"], w=[("psA", pb)])
                                evac(pb, Hs[ab][cur][:, hf * T:(hf + 1) * T], [("Hs", ab, cur, hf)])
                        for i in range(NI):
                            dd = 1 << i
                            nxt = 1 - cur
                            for tt in range(2):
                                for ab, gm in enumerate(gms):
                                    c0 = tt * T
                                    lo = max(0, dd - c0)
                                    has = lo < T
                                    pb = nev % 4
                                    nev += 1
                                    P.op("pe", (lambda e, pb=pb, c0=c0, cur=cur, has=has, ab=ab: e.matmul(
                                        psA[pb][:], lhsT=idb[:], rhs=Hs[ab][cur][:, c0:c0 + T], start=True, stop=not has)),
                                        r=[("Hs", ab, cur, tt), "idb"], w=[("psA", pb)])
                                    if has:
                                        s_lo = c0 + lo - dd
                                        s_hi = c0 + T - dd
                                        rk = [("Hs", ab, cur, s_lo // T), ("Hs", ab, cur, (s_hi - 1) // T), ("R", q % 2, gm)]
                                        P.op("pe", (lambda e, pb=pb, lo=lo, s_lo=s_lo, s_hi=s_hi, cur=cur, gm=gm, i=i, ab=ab, Rq=Rq: e.matmul(
                                            psA[pb][:, lo:T], lhsT=Rq[:, gm, i, :], rhs=Hs[ab][cur][:, s_lo:s_hi], start=False, stop=True)),
                                            r=rk, w=[("psA", pb)])
                                    evac(pb, Hs[ab][nxt][:, c0:c0 + T], [("Hs", ab, nxt, tt)])
                            cur = nxt
                        for ab, gm in enumerate(gms):
                            g = 8 * q + gm
                            yb = ab % 2
                            P.op("pe", (lambda e, yb=yb, g=g, ab=ab: e.matmul(
                                psY[yb][:], lhsT=Toep[:, g, :], rhs=U1[ab][:, T:2 * T], start=True, stop=False)),
                                r=[("U1", ab, 1), "Toep"], w=[("psY", yb)])
                            P.op("pe", (lambda e, yb=yb, g=g, cur=cur, ab=ab: e.matmul(
                                psY[yb][:], lhsT=Cdec[:, g, :], rhs=Hs[ab][cur][:, T - 1:2 * T - 1], start=False, stop=True)),
                                r=[("Hs", ab, cur, 0), ("Hs", ab, cur, 1), "Cdec"], w=[("psY", yb)])
                            gk = ("g1", yb)
                            gg = g1[yb]
                            P.op("act", (lambda e, yb=yb, gg=gg: e.activation(out=gg[:], in_=psY[yb][:], func=AF.Square)), r=[("psY", yb)], w=[gk])
                            P.op("dve", (lambda e, gg=gg: e.tensor_scalar(out=gg[:], in0=gg[:], scalar1=0.044715, scalar2=1.0, op0=ALU.mult, op1=ALU.add)),
                                 r=[gk], w=[gk])
                            P.op("dve", (lambda e, yb=yb, gg=gg: e.tensor_tensor(out=gg[:], in0=gg[:], in1=psY[yb][:], op=ALU.mult)),
                                 r=[gk, ("psY", yb)], w=[gk])
                            P.op("act", (lambda e, gg=gg: e.activation(out=gg[:], in_=gg[:], func=AF.Sigmoid, scale=1.5957691216057308)), r=[gk], w=[gk])
                            P.op("dve", (lambda e, yb=yb, gg=gg, gm=gm: e.tensor_tensor(out=Yg[:, gm, :], in0=gg[:], in1=psY[yb][:], op=ALU.mult)),
                                 r=[gk, ("psY", yb)], w=[("Yg", gm)])
                    for t_ in range(8):
                        pb = nev % 4
                        nev += 1
                        for gm in range(8):
                            P.op("pe", (lambda e, pb=pb, gm=gm, t_=t_: e.matmul(
                                psA[pb][:], lhsT=selT[:, gm * 8 + t_, :], rhs=Yg[:, gm, :], start=(gm == 0), stop=(gm == 7))),
                                r=[("Yg", gm), "selT"], w=[("psA", pb)])
                        evac(pb, yg[:, q, t_:t_ + 8 * 511 + 1:8], [("yg", q)])
                for t8 in range(NT_OWN):
                    ts_ = slice(t8 * T, (t8 + 1) * T)
                    sob = so[t8 % 2]
                    for jo in range(4):
                        yb = jo % 2
                        sg_ = sgm[jo % 2]
                        for k in range(4):
                            P.op("pe", (lambda e, yb=yb, k=k, jo=jo, ts_=ts_: e.matmul(
                                psY[yb][:], lhsT=wglu[:, k, jo * 128:(jo + 1) * 128], rhs=yg[:, k, ts_], start=(k == 0), stop=(k == 3))),
                                r=[("yg", k), "wglu"], w=[("psY", yb)])
                        P.op("act", (lambda e, yb=yb, jo=jo, sg_=sg_: e.activation(out=sg_[:], in_=psY[yb][:], func=AF.Sigmoid, bias=sv[:, 4 + jo:5 + jo])),
                             r=[("psY", yb), "sv"], w=[("sgm", jo % 2)])
                        P.op("dve", (lambda e, jo=jo, ts_=ts_, sg_=sg_, sob=sob: e.tensor_tensor(out=sob[:, jo, :], in0=yg[:, jo, ts_], in1=sg_[:], op=ALU.mult)),
                             r=[("sgm", jo % 2), ("yg", jo)], w=[("so", t8 % 2)])
                    P.op("sp", (lambda e, t8=t8, sob=sob: e.dma_start(out=dview(catT, 4, 4, t8 * T, T), in_=sob[:])),
                         r=[("so", t8 % 2)], w=[("cat", 4 + t8 % 4)], dma=("s2", "st", t8 % 2))
                P.barrier()
                P.flush()


        def outproj_phase():
            name = "op"
            with ExitStack() as st:
                def S(nm, shape, dt):
                    return st.enter_context(nc.sbuf_tensor(name + nm, shape, dt))

                def PS(nm, shape):
                    return st.enter_context(nc.psum_tensor(name + nm, shape, F32))
                wmo = S("w", [128, 8, D], BF16)
                cin = [S("c%d" % i, [128, 8, T], BF16) for i in range(2)]
                xin = [S("x%d" % i, [128, 8, T], F32) for i in range(2)]
                ysb2 = [S("ysb%d" % i, [128, 8, T], F32) for i in range(2)]
                sq2 = [S("sq%d" % i, [128, 8, T], BF16) for i in range(2)]
                rstd2 = [S("rstd%d" % i, [128, T], F32) for i in range(2)]
                pY = [PS("pY%d" % i, [128, T]) for i in range(4)]
                pS2 = [PS("pS%d" % i, [128, T]) for i in range(2)]
                wv = w_mix_out.rearrange("(k p) f -> p k f", p=128)
                for b in range(2):
                    P.op("pool", (lambda e, b=b: e.dma_start(out=wmo[:, b * 4:(b + 1) * 4, :], in_=wv[:, b * 4:(b + 1) * 4, :])),
                         w=[("wmo", b)], dma=("op", "w", b))
                allcat = [("cat", i) for i in range(8)]
                for ti in range(NT_OWN):
                    slot = ti % 2
                    t0 = ti * T
                    ysb, sq, rstd, pS = ysb2[slot], sq2[slot], rstd2[slot], pS2[slot]
                    kY, kQ, kR, kP = ("opysb", slot), ("opsq", slot), ("oprstd", slot), ("oppS", slot)
                    def op_loads(tj):
                        sl_, tq = tj % 2, tj * T
                        P.op("sp", (lambda e: e.dma_start(out=cin[sl_][:], in_=dview(catT, 0, 8, tq, T))),
                             r=allcat, w=[("cin", sl_)], dma=("op", "c", sl_))
                        P.op("sp", (lambda e: e.dma_start(out=xin[sl_][:], in_=dview(x1T, 0, 8, tq, T))),
                             r=[("f1", "dst", tq)], w=[("xin", sl_)], dma=("op", "x", sl_))
                    if ti == 0:
                        op_loads(0)
                    if ti + 1 < NT_OWN:
                        op_loads(ti + 1)
                    for j in range(8):
                        pb = (ti * 8 + j) % 4
                        for k in range(8):
                            P.op("pe", (lambda e, j=j, k=k, pb=pb, slot=slot: e.matmul(
                                pY[pb][:], lhsT=wmo[:, k, j * 128:(j + 1) * 128], rhs=cin[slot][:, k, :],
                                start=(k == 0), stop=(k == 7))), r=[("cin", slot), ("wmo", k // 4)], w=[("opY", pb)])
                        P.op("act", (lambda e, j=j, pb=pb, ysb=ysb: e.activation(out=ysb[:, j, :], in_=pY[pb][:], func=AF.Copy)),
                             r=[("opY", pb)], w=[kY + (j,)])
                        P.op("dve", (lambda e, j=j, ysb=ysb, sq=sq: e.tensor_tensor(out=sq[:, j, :], in0=ysb[:, j, :], in1=ysb[:, j, :], op=ALU.mult)),
                             r=[kY + (j,)], w=[kQ + (j,)])
                    for c in range(8):
                        P.op("pe", (lambda e, c=c, sq=sq, pS=pS: e.matmul(pS[:], lhsT=ones[:], rhs=sq[:, c, :], start=(c == 0), stop=(c == 7))),
                             r=[kQ + (c,), "ones"], w=[kP])
                    P.op("act", (lambda e, rstd=rstd, pS=pS: e.activation(out=rstd[:], in_=pS[:], func=AF.Sqrt, bias=EPS, scale=1.0 / D)),
                         r=[kP], w=[kR])
                    P.op("dve", (lambda e, rstd=rstd: e.reciprocal(out=rstd[:], in_=rstd[:])), r=[kR], w=[kR])
                    x = xin[slot]
                    for c in range(8):
                        P.op("dve", (lambda e, c=c, ysb=ysb, rstd=rstd: e.scalar_tensor_tensor(
                            out=ysb[:, c, :], in0=ysb[:, c, :], scalar=gcol(gsb, G_MIXPOST, c), in1=rstd[:],
                            op0=ALU.mult, op1=ALU.mult)), r=[kY + (c,), kR, "gsb"], w=[kY + (c,)])
                        P.op("dve", (lambda e, c=c, x=x, ysb=ysb: e.tensor_tensor(
                            out=x[:, c, :], in0=x[:, c, :], in1=ysb[:, c, :], op=ALU.add)),
                            r=[kY + (c,), ("xin", slot)], w=[("xin", slot)])
                    P.op("sp", (lambda e, x=x, t0=t0: e.dma_start(out=dview(x2T, 0, 8, t0, T), in_=x[:])),
                         r=[("xin", slot)], w=[("x2T", t0)], dma=("op", "st", slot))
                P.barrier()
                P.flush()

        ntl = NT_ALL if debug is None else debug.get("nt1", NT_ALL)
        tiles1 = []
        for i in range(NT_ALL - ntl, NT_ALL):
            tiles1.append((i * T, (i - NT_OWN) * T if i >= NT_OWN else None, i * T))
        ph = (debug or {}).get("phases", "all")
        last = []
        if ph == "all" or "ffn1" in ph:
            last = ffn_phase("f1", xT, tiles1, w1_in, w1_out, G_F1PRE, G_F1POST, x1T, G_MIXPRE, h2T)
        sstack = ExitStack()
        ssmW = ssm_persist(sstack) if (ph == "all" or "ssm" in ph) else None
        if ph == "all" or "inproj" in ph:
            inproj_phase(ssmW)
        if ph == "all" or "attn" in ph:
            attn_phase()
        if ph == "all" or "ssm" in ph:
            ssm_phase(ssmW)
        sstack.close()
        if ph == "all" or "outproj" in ph:
            outproj_phase()
        if ph == "all" or "ffn2" in ph:
            tiles2 = [(i * T, i * T, None) for i in range(NT_OWN)]
            last = ffn_phase("f2", x2T, tiles2, w2_in, w2_out, G_F2PRE, G_F2POST, outT, None, None)
        P.barrier()
        P.flush()
    return nc


def make_inputs(inputs):
    x = np.asarray(inputs["x"], dtype=np.float32)
    g = np.zeros((128, 48), np.float32)
    for i, k in enumerate(["ffn1_pre_g", "ffn1_post_g", "mix_pre_g", "mix_post_g", "ffn2_pre_g", "ffn2_post_g"]):
        g[:, i * 8:(i + 1) * 8] = np.asarray(inputs[k], np.float32)[0].reshape(8, 128).T
    common = {
        "gains": g,
        "ffn1_w_in": np.ascontiguousarray(inputs["ffn1_w_in"][0], dtype=np.float32),
        "ffn1_w_out": np.ascontiguousarray(inputs["ffn1_w_out"][0], dtype=np.float32),
        "ffn2_w_in": np.ascontiguousarray(inputs["ffn2_w_in"][0], dtype=np.float32),
        "ffn2_w_out": np.ascontiguousarray(inputs["ffn2_w_out"][0], dtype=np.float32),
    }
    slopes = 2.0 ** (-8.0 * np.arange(1, 9) / 8.0)
    cc = np.arange(128)[:, None]
    ii = np.arange(128)[None, :]
    atab = np.zeros((128, 24, 2, 128), np.float32)
    for h in range(8):
        for br, d in enumerate((1, 4, 16)):
            for hh in range(2):
                steps = 128 + ii - (hh * 128 + cc)
                valid = (steps >= 0) & (steps <= 128)
                atab[:, h * 3 + br, hh, :] = np.where(valid, -slopes[h] * d * steps * 8.0, -240000.0)
    btab_first = np.full((128, 24, 128), -240000.0, np.float32)
    btab_second = np.ascontiguousarray(atab[:, :, 0, :])
    common["w_mix_in"] = np.ascontiguousarray(inputs["w_mix_in"][0], dtype=np.float32)
    common["ident"] = np.eye(128, dtype=np.float32)
    f32 = np.float32
    a_re = np.asarray(inputs["a_re"], f32)[0]
    a_im = np.asarray(inputs["a_im"], f32)[0]
    ldt = np.asarray(inputs["log_dt"], f32)[0]
    ssm_a = np.zeros((128, 96), f32)
    ssm_a[:, 0:32] = np.tile(a_re.T, (2, 1))
    ssm_a[:, 32:64] = np.tile(a_im.T, (2, 1))
    ssm_a[:, 64:96] = np.tile(ldt[None, :], (128, 1))
    b_re = np.asarray(inputs["b_re"], f32)[0].transpose(1, 0, 2)
    b_im = np.asarray(inputs["b_im"], f32)[0].transpose(1, 0, 2)
    ssm_b = np.stack([np.concatenate([b_re, b_im], 0), np.concatenate([b_im, b_re], 0)], axis=1)
    c_re = np.asarray(inputs["c_re"], f32)[0].transpose(2, 0, 1)
    c_im = np.asarray(inputs["c_im"], f32)[0].transpose(2, 0, 1)
    ssm_c = np.stack([np.concatenate([c_re, c_im], 0), np.concatenate([c_im, c_re], 0)], axis=1)
    selm = np.zeros((128, 8, 8, 128), f32)
    for gm in range(8):
        for s_ in range(8):
            for c_ in range(16):
                selm[16 * gm + c_, gm, s_, 16 * s_ + c_] = 1.0
    selTm = np.ascontiguousarray(selm.transpose(3, 1, 2, 0))
    ss_, cc_ = np.arange(128) // 16, np.arange(128) % 16
    cmask = (ss_[None, :] >= ss_[:, None]).astype(f32)
    dsk = np.asarray(inputs["d_skip"], f32)[0]
    dstk = np.zeros((128, 32), f32)
    for g_ in range(32):
        dstk[:, g_] = dsk[16 * g_ + cc_]
    common.update({"sel": selm.reshape(128, -1), "selT": selTm.reshape(128, -1), "cmask": cmask, "dstk": dstk})
    ssm_v = np.zeros((128, 16), f32)
    ssm_v[:, 0:4] = np.asarray(inputs["d_skip"], f32)[0].reshape(4, 128).T
    ssm_v[:, 4:8] = np.asarray(inputs["b_glu"], f32)[0].reshape(4, 128).T
    ssm_v[:64, 8] = 1.0
    ssm_v[64:, 8] = -1.0
    ssm_v[:, 9] = -ssm_v[:, 8]
    rmask = np.zeros((128, 8), f32)
    for gm in range(8):
        rmask[16 * gm:16 * gm + 16, gm] = 1.0
    swapm = np.zeros((128, 128), f32)
    swapm[np.arange(128), (np.arange(128) + 64) % 128] = 1.0
    common.update({"ssm_a": ssm_a, "ssm_b": np.ascontiguousarray(ssm_b.reshape(128, -1)),
                   "ssm_c": np.ascontiguousarray(ssm_c.reshape(128, -1)), "ssm_v": ssm_v, "rmask": rmask, "swapm": swapm,
                   "w_glu": np.ascontiguousarray(inputs["w_glu"][0], dtype=f32)})
    common["w_mix_out"] = np.ascontiguousarray(inputs["w_mix_out"][0], dtype=np.float32)
    common["atab"] = atab.reshape(128, -1)
    in_maps = []
    for c in range(NCORES):
        b, hf = c // 2, c % 2
        xt = np.zeros((D, SEQ), np.float32)
        if hf == 0:
            xt[:, HALF:] = x[b, :HALF].T
        else:
            xt[:, :] = x[b].T
        m = dict(common)
        m["xT"] = xt
        m["btab"] = (btab_first if hf == 0 else btab_second).reshape(128, -1)
        in_maps.append(m)
    return in_maps


def kernel(**inputs):
    nc = build()
    in_maps = make_inputs(inputs)
    res = run_bass_kernel_spmd(nc, in_maps, core_ids=list(range(NCORES)))
    out = np.zeros((4, SEQ, D), np.float32)
    for c in range(NCORES):
        b, hf = c // 2, c % 2
        out[b, hf * HALF:(hf + 1) * HALF] = res.results[c]["outT"].T
    return out
```
